# Optimizing a Trainium2 kernel written in Bass

```python
import jax, jax.numpy as jnp
from jax import lax
import numpy as np

D_MODEL = 2048
BATCH = 2
SEQ = 8192
DEPTH = 2
DEC_BATCH = 8
DEC_SEQ = 32
PAST_LEN = 4096

CHUNK = 64
Q_BLOCK = 128
FOX_HEAD_DIM = 128
FOX_WIDTH = D_MODEL // 2
FOX_HEADS = FOX_WIDTH // FOX_HEAD_DIM
GLA_HEADS = 4
GLA_KEY_WIDTH = D_MODEL // 2
GLA_VAL_WIDTH = D_MODEL // 2
GLA_DK = GLA_KEY_WIDTH // GLA_HEADS
GLA_DV = GLA_VAL_WIDTH // GLA_HEADS
GLA_GATE_RANK = 16
GLA_GATE_TAU = 16.0
NORM_EPS = 1e-6
IN_SIZES = (FOX_WIDTH, FOX_WIDTH, FOX_WIDTH, FOX_HEADS, FOX_WIDTH,
            GLA_KEY_WIDTH, GLA_KEY_WIDTH, GLA_VAL_WIDTH, GLA_GATE_RANK, GLA_VAL_WIDTH,
            D_MODEL, D_MODEL)
IN_WIDTH = 4 * FOX_WIDTH + FOX_HEADS + 2 * GLA_KEY_WIDTH + 2 * GLA_VAL_WIDTH + GLA_GATE_RANK + 2 * D_MODEL

kernel_name = "fox_gla_gated_hybrid_stream_step"


def rmsnorm(x, g):
    xf = x.astype(jnp.float32)
    y = xf * lax.rsqrt(jnp.mean(xf * xf, axis=-1, keepdims=True) + NORM_EPS)
    return (y * g.astype(jnp.float32)).astype(x.dtype)


def fox_attention(q, k, v, cq, ck):
    B, Lq, H, dh = q.shape
    Lk = k.shape[1]
    qb = Q_BLOCK if Lq % Q_BLOCK == 0 else Lq
    nb = Lq // qb
    kpos = jnp.arange(Lk)
    qpos = jnp.arange(Lq) + (Lk - Lq)
    scale = dh ** -0.5
    ck_t = jnp.transpose(ck, (0, 2, 1))[:, :, None, :]

    def block(args):
        qi, cqi, pi = args
        s = jnp.einsum('bqhd,bkhd->bhqk', qi, k, preferred_element_type=jnp.float32) * scale
        s = s + jnp.transpose(cqi, (0, 2, 1))[..., None] - ck_t
        s = jnp.where(kpos[None, :] <= pi[:, None], s, -jnp.inf)
        p = jax.nn.softmax(s, axis=-1)
        return jnp.einsum('bhqk,bkhd->bqhd', p.astype(v.dtype), v)

    qs = q.reshape(B, nb, qb, H, dh).transpose(1, 0, 2, 3, 4)
    cs = cq.reshape(B, nb, qb, H).transpose(1, 0, 2, 3)
    ps = qpos.reshape(nb, qb)
    out = lax.map(block, (qs, cs, ps))
    return out.transpose(1, 0, 2, 3, 4).reshape(B, Lq, H, dh)


def gla_chunked(q, k, v, log_a, s0):
    B, L, H, dk = q.shape
    dv = v.shape[-1]
    c = CHUNK if L % CHUNK == 0 else L
    n = L // c

    def to_chunks(t):
        return t.astype(jnp.float32).reshape(B, n, c, H, t.shape[-1]).transpose(1, 0, 2, 3, 4)

    tril = jnp.tril(jnp.ones((c, c), dtype=bool))

    def step(S, xs):
        qc, kc, vc, gc = xs
        b = jnp.cumsum(gc, axis=1)
        diff = b[:, :, None] - b[:, None, :]
        decay = jnp.exp(jnp.where(tril[None, :, :, None, None], diff, -jnp.inf))
        a = jnp.einsum('btshk,bshk->bhts', qc[:, :, None] * decay, kc)
        o = jnp.einsum('bhts,bshv->bthv', a, vc) + jnp.einsum('bthk,bhkv->bthv', qc * jnp.exp(b), S)
        b_last = b[:, -1]
        S = jnp.exp(b_last)[..., None] * S + jnp.einsum('bshk,bshv->bhkv', kc * jnp.exp(b_last[:, None] - b), vc)
        return S, o

    S, o = lax.scan(step, s0.astype(jnp.float32), (to_chunks(q), to_chunks(k), to_chunks(v), to_chunks(log_a)))
    return S, o.transpose(1, 0, 2, 3, 4).reshape(B, L, H, dv)


def mixer_layer(x, past_k, past_v, past_logf, s0, w_in, w_a2, b_a, b_f, gla_gain, w_oa, w_ob, w_out, pre_g, post_g):
    B, L, _ = x.shape
    h = rmsnorm(x, pre_g)
    proj = h @ w_in
    idx = [int(i) for i in np.cumsum(IN_SIZES)[:-1]]
    fq, fk, fv, ff, fg, gq, gk, gv, glr, gg, ma, mb = jnp.split(proj, idx, axis=-1)

    fq = fq.reshape(B, L, FOX_HEADS, FOX_HEAD_DIM)
    fk = fk.reshape(B, L, FOX_HEADS, FOX_HEAD_DIM)
    fv = fv.reshape(B, L, FOX_HEADS, FOX_HEAD_DIM)
    logf = jax.nn.log_sigmoid((ff + b_f).astype(jnp.float32))
    k_all = jnp.concatenate([past_k.astype(fk.dtype), fk], axis=1)
    v_all = jnp.concatenate([past_v.astype(fv.dtype), fv], axis=1)
    c_all = jnp.cumsum(jnp.concatenate([past_logf.astype(jnp.float32), logf], axis=1), axis=1)
    o_a = fox_attention(fq, k_all, v_all, c_all[:, -L:], c_all)
    o_a = o_a.reshape(B, L, FOX_WIDTH) * jax.nn.silu(fg)

    gq = gq.reshape(B, L, GLA_HEADS, GLA_DK) * (GLA_DK ** -0.5)
    gk = gk.reshape(B, L, GLA_HEADS, GLA_DK)
    gv = gv.reshape(B, L, GLA_HEADS, GLA_DV)
    log_a = jax.nn.log_sigmoid((glr @ w_a2 + b_a).astype(jnp.float32)) / GLA_GATE_TAU
    log_a = log_a.reshape(B, L, GLA_HEADS, GLA_DK)
    s_new, o_b = gla_chunked(gq, gk, gv, log_a, s0)
    o_b = o_b * lax.rsqrt(jnp.mean(o_b * o_b, axis=-1, keepdims=True) + NORM_EPS) * gla_gain.astype(jnp.float32)
    o_b = o_b.reshape(B, L, GLA_VAL_WIDTH).astype(x.dtype) * jax.nn.silu(gg)

    merged = jax.nn.sigmoid(ma) * (o_a @ w_oa) + jax.nn.sigmoid(mb) * (o_b @ w_ob)
    y = x + rmsnorm(merged @ w_out, post_g)
    return y, fk, fv, logf, s_new


def setup_inputs(seed: int = 0) -> dict:
    key = jax.random.key(seed)
    ks = jax.random.split(key, 20)
    f32 = jnp.float32
    n = jax.random.normal
    return {
        "x_prompt": n(ks[0], (BATCH, SEQ, D_MODEL), f32),
        "x_sample": n(ks[1], (DEC_BATCH, DEC_SEQ, D_MODEL), f32),
        "cache_k": n(ks[2], (DEPTH, DEC_BATCH, PAST_LEN, FOX_HEADS, FOX_HEAD_DIM), f32),
        "cache_v": n(ks[3], (DEPTH, DEC_BATCH, PAST_LEN, FOX_HEADS, FOX_HEAD_DIM), f32),
        "cache_logf": jax.nn.log_sigmoid(2.0 + 0.5 * n(ks[4], (DEPTH, DEC_BATCH, PAST_LEN, FOX_HEADS), f32)),
        "state_gla": n(ks[5], (DEPTH, DEC_BATCH, GLA_HEADS, GLA_DK, GLA_DV), f32),
        "w_in": n(ks[6], (DEPTH, D_MODEL, IN_WIDTH), f32) * D_MODEL ** -0.5,
        "w_a2": n(ks[7], (DEPTH, GLA_GATE_RANK, GLA_KEY_WIDTH), f32) * GLA_GATE_RANK ** -0.5,
        "b_a": 0.1 * n(ks[8], (DEPTH, GLA_KEY_WIDTH), f32),
        "b_f": 2.0 + 0.1 * n(ks[9], (DEPTH, FOX_HEADS), f32),
        "gla_gain": 1.0 + 0.02 * n(ks[10], (DEPTH, GLA_DV), f32),
        "w_oa": n(ks[11], (DEPTH, FOX_WIDTH, D_MODEL), f32) * FOX_WIDTH ** -0.5,
        "w_ob": n(ks[12], (DEPTH, GLA_VAL_WIDTH, D_MODEL), f32) * GLA_VAL_WIDTH ** -0.5,
        "w_out": n(ks[13], (DEPTH, D_MODEL, D_MODEL), f32) * D_MODEL ** -0.5,
        "pre_norm": 1.0 + 0.02 * n(ks[14], (DEPTH, D_MODEL), f32),
        "post_norm": 1.0 + 0.02 * n(ks[15], (DEPTH, D_MODEL), f32),
    }


def reference(x_prompt, x_sample, cache_k, cache_v, cache_logf, state_gla, w_in, w_a2, b_a, b_f, gla_gain, w_oa, w_ob, w_out, pre_norm, post_norm):
    yp, ys = x_prompt, x_sample
    bp = x_prompt.shape[0]
    kp, vp, fp, sp = [], [], [], []
    ksm, vsm, fsm, ssm = [], [], [], []
    for l in range(DEPTH):
        params = (w_in[l], w_a2[l], b_a[l], b_f[l], gla_gain[l], w_oa[l], w_ob[l], w_out[l], pre_norm[l], post_norm[l])
        empty_kv = jnp.zeros((bp, 0, FOX_HEADS, FOX_HEAD_DIM), x_prompt.dtype)
        empty_f = jnp.zeros((bp, 0, FOX_HEADS), jnp.float32)
        s0 = jnp.zeros((bp, GLA_HEADS, GLA_DK, GLA_DV), jnp.float32)
        yp, k1, v1, f1, s1 = mixer_layer(yp, empty_kv, empty_kv, empty_f, s0, *params)
        ys, k2, v2, f2, s2 = mixer_layer(ys, cache_k[l], cache_v[l], cache_logf[l], state_gla[l], *params)
        kp.append(k1); vp.append(v1); fp.append(f1); sp.append(s1)
        ksm.append(k2); vsm.append(v2); fsm.append(f2); ssm.append(s2)
    return (yp, ys, jnp.stack(kp), jnp.stack(vp), jnp.stack(fp), jnp.stack(sp), jnp.stack(ksm), jnp.stack(vsm), jnp.stack(fsm), jnp.stack(ssm))
```

```python
import math
import numpy as np
from contextlib import ExitStack
import concourse.bass as bass
import concourse.mybir as mybir
from concourse.bass_utils import run_bass_kernel_spmd

F32, BF16, I32 = mybir.dt.float32, mybir.dt.bfloat16, mybir.dt.int32
AF = mybir.ActivationFunctionType
ALU = mybir.AluOpType

D = 2048
KC = 16
NT = 68
NBLK = 17
TL = 17
NTOK = NT * 128
DEPTH = 2
SCALE_F = 128 ** -0.5
GROUPS = [[0, 1, 2, 3], [4, 5, 6, 7]]
STOP_AFTER = None
TGROUPS = [[0, 1, 2, 3], [4, 5, 6, 7], [8, 9, 10, 11], [12, 13, 14], [15, 16]]
SB_BASE = 16512
SB_END = 229344


class Buf:
    __slots__ = ("name", "w", "r", "dsem", "dcnt")

    def __init__(self, name):
        self.name = name
        self.w = None
        self.r = []
        self.dsem = None
        self.dcnt = 0


class Op:
    __slots__ = ("eng", "fn", "deps", "dma", "token", "mark", "pos")

    def __init__(self, eng, fn, deps, dma):
        self.eng = eng
        self.fn = fn
        self.deps = deps
        self.dma = dma
        self.token = None
        self.mark = False
        self.pos = 0


class Prog:
    ENG = ("pe", "act", "dve", "pool", "sp")

    def __init__(self, nc, stack):
        self.nc = nc
        self.stack = stack
        self.ops = []
        self.h = {"pe": nc.tensor, "act": nc.scalar, "dve": nc.vector, "pool": nc.gpsimd, "sp": nc.sync}
        self.esem = {e: stack.enter_context(nc.semaphore("es_" + e)) for e in self.ENG}
        self.ccsem = stack.enter_context(nc.semaphore("ccsem"))
        self.cccnt = 0
        self.last = {e: None for e in self.ENG}
        self.pending = []
        self.nbuf = 0
        self.nsem = 6

    def buf(self, name=None):
        self.nbuf += 1
        return Buf("%s_%d" % (name or "b", self.nbuf))

    def bufs(self, n, name="b"):
        return [self.buf(name) for _ in range(n)]

    def _dsem(self, b):
        if b.dsem is None:
            b.dsem = self.stack.enter_context(self.nc.semaphore("ds_" + b.name))
            self.nsem += 1
        return b.dsem

    def op(self, eng, fn, reads=(), writes=(), dma=None, cc=False, extra_deps=(), nobarrier=False):
        idx = len(self.ops)
        deps = set(extra_deps)
        for b in reads:
            if b.w is not None:
                deps.add(b.w)
            b.r.append(idx)
        for b in writes:
            if b.w is not None:
                pw = self.ops[b.w]
                if dma is not None and pw.dma is dma and not b.r:
                    deps.update(pw.deps)
                else:
                    deps.add(b.w)
            deps.update(b.r)
            b.r = []
            b.w = idx
        deps.discard(idx)
        latest = {}
        keep = []
        for di in deps:
            d = self.ops[di]
            if d.dma is None and d.fn is not None:
                if d.eng not in latest or latest[d.eng] < di:
                    latest[d.eng] = di
            else:
                keep.append(di)
        deps = keep + list(latest.values())
        o = Op(eng, fn, sorted(deps), dma)
        if dma is not None:
            sem = self._dsem(dma)
            dma.dcnt += 16
            o.token = (sem, dma.dcnt)
            if not nobarrier:
                self.pending.append(idx)
        elif cc:
            self.cccnt += 1
            o.token = (self.ccsem, self.cccnt)
            o.dma = "cc"
        self.ops.append(o)
        if fn is not None and not cc:
            self.last[eng] = idx
        return idx

    def barrier(self):
        deps = [v for v in self.last.values() if v is not None] + list(self.pending)
        self.pending = []
        for e in self.ENG:
            self.op(e, None, extra_deps=deps)

    def _needs_wait(self, o, d):
        if d.dma is not None:
            return True
        if d.eng != o.eng:
            return True
        if o.dma is not None:
            return True
        if o.eng == "pe":
            return False
        return True

    def emit(self):
        ops = self.ops
        pos = {e: 0 for e in self.ENG}
        for o in ops:
            if o.fn is not None:
                pos[o.eng] += 1
            o.pos = pos[o.eng]
        for o in ops:
            for di in o.deps:
                d = ops[di]
                if d.dma is None and d.fn is not None and self._needs_wait(o, d):
                    d.mark = True
        cnt = {e: 0 for e in self.ENG}
        for o in ops:
            if o.dma is None and o.mark:
                cnt[o.eng] += 1
                o.token = (self.esem[o.eng], cnt[o.eng])
        seen = {e: {} for e in self.ENG}
        nwait = 0
        for o in ops:
            E = self.h[o.eng]
            waits = {}
            for di in o.deps:
                d = ops[di]
                if d.fn is None or not self._needs_wait(o, d):
                    continue
                sem, val = d.token
                k = id(sem)
                if k not in waits or waits[k][1] < val:
                    waits[k] = (sem, val)
            for k, (sem, val) in waits.items():
                if seen[o.eng].get(k, 0) < val:
                    E.wait_ge(sem, val)
                    seen[o.eng][k] = val
                    nwait += 1
            if o.fn is None:
                continue
            ins = o.fn(E)
            if o.dma is not None:
                ins.then_inc(o.token[0], 1 if o.dma == "cc" else 16)
            elif o.mark:
                ins.then_inc(o.token[0], 1)
        return dict(n_ops=len(ops), n_wait=nwait, marked=cnt, nsem=self.nsem)


class Arena:
    def __init__(self, nc):
        self.nc = nc
        self.off = SB_BASE
        self.n = 0
        self.peak = 0

    def alloc(self, name, shape, dtype):
        nbytes = int(np.prod(shape[1:])) * (4 if dtype in (F32, I32) else 2)
        nbytes = (nbytes + 31) // 32 * 32
        assert self.off + nbytes <= SB_END, (name, self.off, nbytes)
        self.n += 1
        t = self.nc.alloc_sbuf_tensor_at("%s_%d" % (name, self.n), list(shape), dtype, offset=self.off)
        self.off += nbytes
        self.peak = max(self.peak, self.off)
        return t

    def mark(self):
        return self.off

    def reset(self, m):
        self.off = m


class RR:
    def __init__(self, items):
        self.items = list(items)
        self.i = 0

    def next(self):
        x = self.items[self.i % len(self.items)]
        self.i += 1
        return x


def MM(out, lhsT, rhs, start=True, stop=True, skip=False):
    return lambda e: e.matmul(out, lhsT=lhsT, rhs=rhs, start=start, stop=stop, skip_group_check=skip)


def TR(out, in_, ident):
    return lambda e: e.transpose(out=out, in_=in_, identity=ident)


def ACT(out, in_, func, bias, scale):
    return lambda e: e.activation(out=out, in_=in_, func=func, bias=bias, scale=scale)


def MUL(out, in_, c):
    return lambda e: e.mul(out, in_, c)


def CP(out, in_):
    return lambda e: e.copy(out=out, in_=in_)


def TC(out, in_):
    return lambda e: e.tensor_copy(out=out, in_=in_)


def TT(out, in0, in1, op):
    return lambda e: e.tensor_tensor(out=out, in0=in0, in1=in1, op=op)


def TS(out, in0, s1, s2, op0, op1=None):
    if op1 is None:
        return lambda e: e.tensor_scalar(out=out, in0=in0, scalar1=s1, scalar2=None, op0=op0)
    return lambda e: e.tensor_scalar(out=out, in0=in0, scalar1=s1, scalar2=s2, op0=op0, op1=op1)


def STT(out, in0, scalar, in1, op0, op1, accum=None):
    if accum is None:
        return lambda e: e.scalar_tensor_tensor(out=out, in0=in0, scalar=scalar, in1=in1, op0=op0, op1=op1)
    return lambda e: e.scalar_tensor_tensor(out=out, in0=in0, scalar=scalar, in1=in1, op0=op0, op1=op1, accum_out=accum)


def SCAN(out, d0, d1, init):
    return lambda e: e.tensor_tensor_scan(out=out, data0=d0, data1=d1, initial=init, op0=ALU.mult, op1=ALU.add)


def RCP(out, in_):
    return lambda e: e.reciprocal(out=out, in_=in_)


def MS(ap, v):
    return lambda e: e.memset(ap, v)


def DMA(out, in_):
    return lambda e: e.dma_start(out=out, in_=in_)


def flat(ap3):
    n = len(ap3.shape)
    if n == 3:
        return ap3.rearrange("p a b -> p (a b)")
    if n == 4:
        return ap3.rearrange("p a b c -> p (a b c)")
    return ap3


def split(ap2, a):
    return ap2.rearrange("p (a b) -> p a b", a=a)


def build_program():
    nc = bass.Bass("TRN2", target_bir_lowering=False)

    def din(name, shape, dt=F32):
        return nc.dram_tensor(name, list(shape), dt, kind="ExternalInput").ap()

    def dout(name, shape, dt=F32):
        return nc.dram_tensor(name, list(shape), dt, kind="ExternalOutput").ap()

    x_d = din("x_loc", [TL * 128, D])
    wffm_d = din("wffm", [DEPTH, 128, KC, 256])
    wftm_d = din("wftm", [DEPTH, 128, KC, 770])
    wgfm_d = din("wgfm", [DEPTH, 128, KC, 528])
    wgtm_d = din("wgtm", [DEPTH, 128, KC, 512])
    wm_d = din("wm", [DEPTH, 32, 128, KC, 128])
    woa_d = din("woa", [DEPTH, 16, 128, 8, 128])
    wob_d = din("wob", [DEPTH, 16, 128, 8, 128])
    wout_d = din("wout", [DEPTH, 8, 128, KC, 256])
    wa2_d = din("wa2", [DEPTH, 16, 256])
    ba_d = din("ba", [DEPTH, 128, 2])
    bfb_d = din("bfb", [DEPTH, 128, 2])
    gain_d = din("gain", [DEPTH, 128, 256])
    pre_d = din("pre", [DEPTH, 128, D])
    post_d = din("post", [DEPTH, 128, D])
    ck_d = din("ck", [DEPTH, 4, 4096, 256])
    cv_d = din("cv", [DEPTH, 4, 4096, 256])
    clf_d = din("clf", [DEPTH, 4, 128, 32, 2])
    sg_d = din("sg", [DEPTH, 4, 128, 2, 256])
    idx_d = din("idx", [128, TL * 4], I32)

    y_d = dout("y_loc", [TL * 128, D])
    ko_d = dout("k_out", [DEPTH, NTOK, 256])
    vo_d = dout("v_out", [DEPTH, NTOK, 256])
    lfo_d = dout("lf_out", [DEPTH, 128, NT, 2])
    gp_d = dout("gla_p", [DEPTH, 128, 2, 256])
    gs_d = dout("gla_s", [DEPTH, 4, 128, 2, 256])

    hsrc = nc.dram_tensor("hsrc", [TL * 128, D], BF16).ap()
    hag = nc.dram_tensor("hag", [9 * 1024, D], BF16).ap()
    osrcA = nc.dram_tensor("osrcA", [NTOK, 256], BF16).ap()
    osrcB = nc.dram_tensor("osrcB", [NTOK, 256], BF16).ap()
    oagA = nc.dram_tensor("oagA", [5 * 8192, 256], BF16).ap()
    oagB_d = nc.dram_tensor("oagB", [5 * 8192, 256], BF16).ap()
    yscr = nc.dram_tensor("yscr", [TL * 128, D], F32).ap()
    wm_b = nc.dram_tensor("wm_b", [DEPTH, 32, 128, KC * 128], BF16).ap()
    wffm_b = nc.dram_tensor("wffm_b", [DEPTH, 128, KC * 256], BF16).ap()
    wftm_b = nc.dram_tensor("wftm_b", [DEPTH, 128, KC * 770], BF16).ap()
    wgfm_b = nc.dram_tensor("wgfm_b", [DEPTH, 128, KC * 528], BF16).ap()
    wgtm_b = nc.dram_tensor("wgtm_b", [DEPTH, 128, KC * 512], BF16).ap()
    woa_b = nc.dram_tensor("woa_b", [DEPTH, 16, 128, 8 * 128], BF16).ap()
    wob_b = nc.dram_tensor("wob_b", [DEPTH, 16, 128, 8 * 128], BF16).ap()
    wout_b = nc.dram_tensor("wout_b", [DEPTH, 8, 128, KC * 256], BF16).ap()

    with ExitStack() as st:
        P = Prog(nc, st)
        A = Arena(nc)
        ps = [nc.alloc_psum_tensor("psb%d" % i, [128, 512], F32) for i in range(8)]
        psB = [P.buf("ps%d" % i) for i in range(8)]

        ones_f = A.alloc("ones_f", [128, 128], F32)
        ident_f = A.alloc("ident_f", [128, 128], F32)
        tri_f = A.alloc("tri_f", [128, 128], F32)
        tri_b = A.alloc("tri_b", [128, 128], BF16)
        resetm = A.alloc("resetm", [128, 512], F32)
        cst = A.alloc("cst", [128, 8], F32)
        idx_t = A.alloc("idx_t", [128, TL * 4], I32)
        Bc = P.buf("consts")
        ID = ident_f[:]

        P.op("pool", MS(ones_f[:], 1.0), writes=[Bc])
        P.op("pool", MS(ident_f[:], 0.0), writes=[Bc])
        P.op("pool", lambda e: e.affine_select(out=ident_f[:], in_=ones_f[:], pattern=[[-1, 128]],
                                               compare_op=ALU.is_equal, fill=0.0, base=0, channel_multiplier=1),
             writes=[Bc])
        P.op("pool", MS(tri_f[:], 0.0), writes=[Bc])
        P.op("pool", lambda e: e.affine_select(out=tri_f[:], in_=ones_f[:], pattern=[[1, 128]],
                                               compare_op=ALU.is_ge, fill=0.0, base=0, channel_multiplier=-1),
             writes=[Bc])
        P.op("pool", TC(tri_b[:], tri_f[:]), writes=[Bc])
        P.op("pool", MS(resetm[:], 1.0), writes=[Bc])
        for q in range(4):
            P.op("pool", MS(resetm[:, q * 128:q * 128 + 1], 0.0), writes=[Bc])
        P.op("pool", MS(cst[:, 0:1], 1e-6), writes=[Bc])
        P.op("pool", MS(cst[:, 1:2], 1.0), writes=[Bc])
        P.op("pool", MS(cst[:, 2:3], -math.log(16.0)), writes=[Bc])
        P.op("pool", MS(cst[:, 3:4], 0.0), writes=[Bc])
        P.op("pool", MS(cst[:, 4:5], math.log(0.5)), writes=[Bc])
        P.op("sp", DMA(idx_t[:], idx_d), writes=[Bc], dma=Bc)
        EPS, ONE, NL16, ZERO, LNH = cst[:, 0:1], cst[:, 1:2], cst[:, 2:3], cst[:, 3:4], cst[:, 4:5]

        hsrcB = P.bufs(TL, "hsrc")
        hagB = P.bufs(9, "hag")
        oagAB = P.bufs(5, "oagA")
        oagBB = P.bufs(5, "oagB")
        base_mark = A.mark()
        psrr = RR(range(8))

        def rstd_of(ssq, n, B, lnbias=None):
            P.op("act", ACT(ssq[:, 1:2], ssq[:, 0:1], AF.Ln, EPS, 1.0 / n), reads=[B, Bc], writes=[B])
            P.op("act", ACT(ssq[:, 2:3], ssq[:, 1:2], AF.Exp, ZERO if lnbias is None else lnbias, -0.5), reads=[B, Bc], writes=[B])

        def emit_h(y_t, yB, pre_t, preB, t, tmp):
            junk, ssq, hf, hTst, hTstB, tB = tmp
            P.op("dve", STT(junk[:], y_t[:], 1.0, y_t[:], ALU.mult, ALU.mult, accum=ssq[:, 0:1]), reads=[yB], writes=[tB])
            rstd_of(ssq, D, tB)
            P.op("dve", STT(hf[:], y_t[:], ssq[:, 2:3], pre_t[:], ALU.mult, ALU.mult), reads=[yB, tB, preB], writes=[tB])
            for q in range(4):
                bk = psrr.next()
                for j in range(4):
                    c = q * 4 + j
                    P.op("pe", TR(ps[bk][:, j * 128:(j + 1) * 128], hf[:, c * 128:(c + 1) * 128], ID),
                         reads=[tB, Bc], writes=[psB[bk]])
                P.op("act", CP(hTst[:, q * 4:(q + 1) * 4, :], split(ps[bk][:, :], 4)), reads=[psB[bk]], writes=[hTstB])
            return P.op("sp", DMA(hsrc[t * 128:(t + 1) * 128, :], flat(hTst[:, :, :])), reads=[hTstB], writes=[hsrcB[t]], dma=hTstB)

        def ag_chunk(src, dst, dstB, q, rows_total, rows_chunk, store_ops):
            r0 = q * rows_chunk
            n = min(rows_chunk, rows_total - r0)
            P.op("pool", (lambda i_, o_: (lambda e: e.collective_compute(
                "AllGather", ALU.bypass, replica_groups=GROUPS, ins=[i_], outs=[o_])))(
                    src[r0:r0 + n, :], dst[q * 4 * rows_chunk:q * 4 * rows_chunk + 4 * n, :]),
                writes=[dstB[q]], cc=True, extra_deps=store_ops)

        def cumsum_tiles(bkrr, lf_ap, n, carry_ap, out_ap, tmp, tB, lfB, outB, carry_out_ap=None):
            bk = bkrr.next()
            sb_, incl = tmp
            P.op("pe", MM(ps[bk][:, 0:2 * n], tri_f[:], flat(lf_ap)), reads=[lfB, Bc], writes=[psB[bk]])
            P.op("pe", MM(ps[bk][:, 2 * n:4 * n], ones_f[:], flat(lf_ap), start=False, stop=True, skip=True),
                 reads=[lfB, Bc], writes=[psB[bk]])
            P.op("act", CP(sb_[:, 0:4 * n], ps[bk][:, 0:4 * n]), reads=[psB[bk]], writes=[tB])
            tot = sb_[:, 2 * n:4 * n].rearrange("p (a b) -> p a b", b=2)
            loc = sb_[:, 0:2 * n].rearrange("p (a b) -> p a b", b=2)
            inc3 = incl[:, 0:2 * n].rearrange("p (a b) -> p a b", b=2)
            for hh in range(2):
                P.op("dve", SCAN(inc3[:, :, hh], ones_f[:, 0:n], tot[:, :, hh], carry_ap[:, hh:hh + 1]),
                     reads=[tB, Bc, outB], writes=[tB])
            P.op("dve", TT(loc, loc, tot, ALU.subtract), reads=[tB], writes=[tB])
            P.op("dve", TT(out_ap, loc, inc3, ALU.add), reads=[tB], writes=[outB])
            if carry_out_ap is not None:
                P.op("dve", TC(carry_out_ap, inc3[:, n - 1, :]), reads=[tB], writes=[outB])

        def load_hT_block(i, hTb_t, hTbB):
            q, tl = i // 2, i % 2
            nq = 256 if q < 8 else 128
            for tt in range(4):
                row = q * 1024 + tt * nq + tl * 128
                P.op("sp", DMA(hTb_t[:, :, tt * 128:(tt + 1) * 128], split(hag[row:row + 128, :], KC)),
                     reads=[hagB[q]], writes=[hTbB], dma=hTbB)

        hwB = {}
        hw_ops = {}
        hw_jobs = {}
        for l_ in range(DEPTH):
            for ps_, mats in (("F", ((wffm_b, wffm_d, 256, 8), (wftm_b, wftm_d, 770, 2))),
                              ("G", ((wgfm_b, wgfm_d, 528, 3), (wgtm_b, wgtm_d, 512, 4)))):
                hwB[(l_, ps_)] = P.buf("hwcast%d%s" % (l_, ps_))
                hw_ops[(l_, ps_)] = []
                jl = []
                for (dst_, src_, cols, kk) in mats:
                    for k0 in range(0, KC, kk):
                        k1 = min(KC, k0 + kk)
                        jl.append((dst_[l_, :, k0 * cols:k1 * cols], flat(src_[l_, :, k0:k1, :])))
                hw_jobs[(l_, ps_)] = jl

        def emit_hw_casts(key, n=None):
            jl = hw_jobs[key]
            for _ in range(len(jl) if n is None else min(n, len(jl))):
                o_, i_ = jl.pop(0)
                hw_ops[key].append(P.op("pool", DMA(o_, i_), dma=hwB[key], nobarrier=True))

        emit_hw_casts((0, "F"))

        def load_w_bf(dst_t, src_ap, B, key):
            emit_hw_casts(key)
            P.op("sp", DMA(flat(dst_t[:, :, :]), src_ap), writes=[B], dma=B, extra_deps=hw_ops[key])

        def silu2_from_psum(out_ap, ps_ap, et_ap, etB_, psBuf, outB):
            P.op("act", ACT(et_ap, ps_ap, AF.Tanh, ZERO, 0.5), reads=[psBuf, Bc], writes=[etB_])
            P.op("dve", STT(out_ap, et_ap, 1.0, ps_ap, ALU.add, ALU.mult), reads=[psBuf, etB_], writes=[outB])

        wcastB = [P.buf("wcast%d" % l_) for l_ in range(DEPTH)]
        wcast_ops = [[] for _ in range(DEPTH)]

        def cast_jobs(l):
            jobs = []
            for c in range(32):
                jobs.append((wm_b[l, c], flat(wm_d[l, c])))
            for c in range(16):
                jobs.append((woa_b[l, c], flat(woa_d[l, c])))
                jobs.append((wob_b[l, c], flat(wob_d[l, c])))
            for blk in range(8):
                for hf_ in range(2):
                    jobs.append((wout_b[l, blk, :, hf_ * 2048:(hf_ + 1) * 2048], flat(wout_d[l, blk, :, hf_ * 8:(hf_ + 1) * 8, :])))
            return jobs

        def emit_casts(l, jobs, n):
            for _ in range(min(n, len(jobs))):
                o_, i_ = jobs.pop(0)
                wcast_ops[l].append(P.op("pool", DMA(o_, i_), dma=wcastB[l], nobarrier=True))

        def prologue():
            A.reset(base_mark)
            pre_t = A.alloc("pre_t", [128, D], F32)
            preB = P.buf("pre")
            P.op("sp", DMA(pre_t[:], pre_d[0]), writes=[preB], dma=preB)
            xts = [A.alloc("xt", [128, D], F32) for _ in range(2)]
            xBs = P.bufs(2, "xt")
            tmps = []
            for _ in range(2):
                tmps.append((A.alloc("junk", [128, D], BF16), A.alloc("ssq", [128, 4], F32), A.alloc("hf", [128, D], F32),
                             A.alloc("hTst", [128, KC, 128], BF16), P.buf("hTst"), P.buf("htmp")))
            sts = []
            P.op("sp", DMA(xts[0][:], x_d[0:128, :]), writes=[xBs[0]], dma=xBs[0])
            for t in range(TL):
                s = t % 2
                if t + 1 < TL:
                    P.op("sp", DMA(xts[1 - s][:], x_d[(t + 1) * 128:(t + 2) * 128, :]), writes=[xBs[1 - s]], dma=xBs[1 - s])
                sts.append(emit_h(xts[s], xBs[s], pre_t, preB, t, tmps[s]))
                if t % 2 == 1 or t == TL - 1:
                    ag_chunk(hsrc, hag, hagB, t // 2, TL * 128, 256, sts)
                    sts = []
            P.barrier()

        def pass_f(l):
            A.reset(base_mark)
            WFfm = A.alloc("WFfm", [128, KC, 256], BF16)
            WFtm = A.alloc("WFtm", [128, KC, 770], BF16)
            WB_ = P.buf("WF")
            load_w_bf(WFfm, wffm_b[l], WB_, (l, "F"))
            load_w_bf(WFtm, wftm_b[l], WB_, (l, "F"))
            bfb = A.alloc("bfb", [128, 2], F32)
            P.op("sp", DMA(bfb[:], bfb_d[l]), writes=[WB_], dma=WB_)
            kT = A.alloc("kT", [128, 2, NTOK], BF16)
            Vg = A.alloc("Vg", [128, NT, 2, 129], BF16)
            kTB = P.bufs(NT, "kT")
            VB = P.bufs(NT, "V")
            P.op("pool", MS(flat(Vg[:, :, :, :]), 2.0), writes=VB)
            hTb = [A.alloc("hTb", [128, KC, 512], BF16) for _ in range(2)]
            hTbB = P.bufs(2, "hTb")
            qT = [A.alloc("qT", [128, 2, 512], BF16) for _ in range(2)]
            qTB = P.bufs(2, "qT")
            kvf = [A.alloc("kvf", [128, 512], F32) for _ in range(2)]
            kvfB = P.bufs(2, "kvf")
            et = [A.alloc("et", [128, 256], F32) for _ in range(2)]
            etB = P.bufs(2, "et")
            fgs = [A.alloc("fgs", [128, 4, 256], BF16) for _ in range(2)]
            fgsB = P.bufs(2, "fgs")
            lfpre = [A.alloc("lfpre", [128, 4, 2], F32) for _ in range(2)]
            lfpB = P.bufs(2, "lfp")
            LF = A.alloc("LF", [128, NT, 2], F32)
            LFB = P.buf("LF")
            C = A.alloc("C", [128, NT, 2], F32)
            CB = P.buf("C")
            carry = A.alloc("carry", [128, 2], F32)
            cref = A.alloc("cref", [128, NBLK, 2], F32)
            cs_tmp = (A.alloc("cs_sb", [128, 128], F32), A.alloc("cs_incl", [128, 64], F32))
            csB = P.buf("cstmp")
            btab = [A.alloc("btab", [128, 2, NT], F32) for _ in range(2)]
            btB = P.bufs(2, "btab")
            PT = [A.alloc("PT", [128, 512], BF16) for _ in range(4)]
            PTB = P.bufs(4, "PT")
            ptrr = RR(range(4))
            ost = [A.alloc("ost", [128, 256], BF16) for _ in range(8)]
            ostB = P.bufs(8, "ost")
            rec = [A.alloc("rec", [128, 4], F32) for _ in range(8)]
            ckf = [A.alloc("ckf", [128, 256], F32) for _ in range(3)]
            ckfB = P.bufs(3, "ckf")
            ckT = [A.alloc("ckT", [128, 2, 128], BF16) for _ in range(3)]
            ckTB = P.bufs(3, "ckT")
            cV = [A.alloc("cV", [128, 2, 129], BF16) for _ in range(3)]
            cVB = P.bufs(3, "cV")
            for s3 in range(3):
                P.op("pool", MS(flat(cV[s3][:, :, :]), 2.0), writes=[cVB[s3]])
            clf = A.alloc("clf", [128, 32, 2], F32)
            clfB = P.buf("clf")
            cC = A.alloc("cC", [128, 32, 2], F32)
            cCB = P.buf("cC")
            ccar = A.alloc("ccar", [128, 2], F32)
            Cn = A.alloc("Cn", [128, 1, 2], F32)
            btS = A.alloc("btS", [128, 2, 33], F32)
            btSB = P.buf("btS")
            PTs = [A.alloc("PTs", [128, 64], BF16) for _ in range(3)]
            PTsB = P.bufs(3, "PTs")

            P.op("pool", MS(carry[:], 0.0), writes=[CB])
            pj = RR([0, 1, 2, 3])
            pjs = RR([0])
            stb = RR([1, 2, 3])
            OB = [(4, 5), (6, 7)]

            def sample_attention(s):
                for k4 in range(4):
                    j = 64 + k4
                    P.op("sp", DMA(clf[:], clf_d[l, k4]), writes=[clfB], dma=clfB)
                    P.op("pool", MS(ccar[:], 0.0), writes=[cCB])
                    cumsum_tiles(pjs, clf[:, :, :], 32, ccar, cC[:, :, :], cs_tmp, csB, clfB, cCB, carry_out_ap=ccar[:, :])
                    cumsum_tiles(pjs, LF[:, j:j + 1, :], 1, ccar, Cn[:, :, :], cs_tmp, csB, LFB, cCB)
                    for h in range(2):
                        P.op("dve", TS(btS[:, h, 0:32], cC[:, :, h], -1.0, ccar[:, h:h + 1], ALU.mult, ALU.add),
                             reads=[cCB], writes=[btSB])
                        P.op("dve", TS(btS[:, h, 32:33], Cn[:, :, h], -1.0, ccar[:, h:h + 1], ALU.mult, ALU.add),
                             reads=[cCB], writes=[btSB])
                    bO = OB[k4 % 2][0]
                    qsl = slice(k4 * 128, k4 * 128 + 32)
                    for jt in range(33):
                        r3 = (k4 * 33 + jt) % 3
                        sk = stb.next()
                        if jt < 32:
                            P.op("sp", DMA(ckf[r3][:], ck_d[l, k4, jt * 128:(jt + 1) * 128, :]), writes=[ckfB[r3]], dma=ckfB[r3])
                            P.op("pool", DMA(cV[r3][:, :, 0:128], split(cv_d[l, k4, jt * 128:(jt + 1) * 128, :], 2)),
                                 writes=[cVB[r3]], dma=cVB[r3])
                            bk2 = pjs.next()
                            for h in range(2):
                                P.op("pe", TR(ps[bk2][:, h * 128:(h + 1) * 128], ckf[r3][:, h * 128:(h + 1) * 128], ID),
                                     reads=[ckfB[r3], Bc], writes=[psB[bk2]])
                            P.op("dve", TC(ckT[r3][:, :, :], split(ps[bk2][:, 0:256], 2)), reads=[psB[bk2]], writes=[ckTB[r3]])
                            for h in range(2):
                                P.op("pe", MM(ps[sk][:, h * 32:(h + 1) * 32], ckT[r3][:, h, :], qT[s][:, h, qsl],
                                              start=(h == 0), stop=True, skip=True),
                                     reads=[ckTB[r3], qTB[s]], writes=[psB[sk]])
                            for h in range(2):
                                P.op("act", ACT(PTs[r3][:, h * 32:(h + 1) * 32], ps[sk][:, h * 32:(h + 1) * 32], AF.Exp,
                                                btS[:, h, jt:jt + 1], SCALE_F), reads=[psB[sk], btSB], writes=[PTsB[r3]])
                            for h in range(2):
                                P.op("pe", MM(ps[bO][0:32, h * 129:(h + 1) * 129], PTs[r3][:, h * 32:(h + 1) * 32], cV[r3][:, h, :],
                                              start=(jt == 0 and h == 0), stop=False, skip=True),
                                     reads=[PTsB[r3], cVB[r3]], writes=[psB[bO]])
                        else:
                            for h in range(2):
                                P.op("pe", MM(ps[sk][0:32, h * 32:(h + 1) * 32], kT[:, h, j * 128:j * 128 + 32], qT[s][:, h, qsl],
                                              start=(h == 0), stop=True, skip=True),
                                     reads=[kTB[j], qTB[s]], writes=[psB[sk]])
                            for h in range(2):
                                P.op("act", ACT(PTs[r3][0:32, h * 32:(h + 1) * 32], ps[sk][0:32, h * 32:(h + 1) * 32], AF.Exp,
                                                btS[0:32, h, 32:33], SCALE_F), reads=[psB[sk], btSB], writes=[PTsB[r3]])
                                P.op("dve", TT(PTs[r3][0:32, h * 32:(h + 1) * 32], PTs[r3][0:32, h * 32:(h + 1) * 32],
                                                tri_b[0:32, 0:32], ALU.mult), reads=[PTsB[r3], Bc], writes=[PTsB[r3]])
                            for h in range(2):
                                P.op("pe", MM(ps[bO][0:32, h * 129:(h + 1) * 129], PTs[r3][0:32, h * 32:(h + 1) * 32],
                                              Vg[0:32, j, h, :], start=False, stop=True, skip=True),
                                     reads=[PTsB[r3], VB[j]], writes=[psB[bO]])
                    oi = j % 8
                    P.op("dve", MS(ost[oi][:], 0.0), writes=[ostB[oi]])
                    for h in range(2):
                        P.op("dve", RCP(rec[oi][0:32, h:h + 1], ps[bO][0:32, h * 129 + 128:h * 129 + 129]),
                             reads=[psB[bO]], writes=[ostB[oi]])
                        P.op("dve", STT(ost[oi][0:32, h * 128:(h + 1) * 128], ps[bO][0:32, h * 129:h * 129 + 128],
                                        rec[oi][0:32, h:h + 1], fgs[s][0:32, k4, h * 128:(h + 1) * 128], ALU.mult, ALU.mult),
                             reads=[psB[bO], fgsB[s], ostB[oi]], writes=[ostB[oi]])
                    oa_st.append(P.op("sp", DMA(osrcA[j * 128:(j + 1) * 128, :], ost[oi][:]), reads=[ostB[oi]], dma=ostB[oi]))

            def block(i):
                s = i % 2
                if i + 1 < NBLK:
                    load_hT_block(i + 1, hTb[1 - s], hTbB[1 - s])
                hb, hbB = hTb[s], hTbB[s]
                for h in range(2):
                    bk = pj.next()
                    for k in range(KC):
                        P.op("pe", MM(ps[bk][:, :], WFfm[:, k, h * 128:(h + 1) * 128], hb[:, k, :], start=(k == 0), stop=(k == KC - 1)),
                             reads=[WB_, hbB], writes=[psB[bk]])
                    P.op("dve", TC(qT[s][:, h, :], ps[bk][:, :]), reads=[psB[bk]], writes=[qTB[s]])
                for tt in range(4):
                    j = 4 * i + tt
                    ks = j % 2
                    tsl = slice(tt * 128, (tt + 1) * 128)
                    bk = pj.next()
                    for k in range(KC):
                        P.op("pe", MM(ps[bk][:, :], hb[:, k, tsl], WFtm[:, k, 0:512], start=(k == 0), stop=(k == KC - 1)),
                             reads=[WB_, hbB], writes=[psB[bk]])
                    P.op("dve", TC(kvf[ks][:], ps[bk][:, :]), reads=[psB[bk]], writes=[kvfB[ks]])
                    P.op("sp", DMA(ko_d[l, j * 128:(j + 1) * 128, :], kvf[ks][:, 0:256]), reads=[kvfB[ks]], dma=kvfB[ks])
                    P.op("sp", DMA(vo_d[l, j * 128:(j + 1) * 128, :], kvf[ks][:, 256:512]), reads=[kvfB[ks]], dma=kvfB[ks])
                    P.op("dve", TC(Vg[:, j, :, 0:128], split(kvf[ks][:, 256:512], 2)), reads=[kvfB[ks]], writes=[VB[j]])
                    bk = pj.next()
                    for k in range(KC):
                        P.op("pe", MM(ps[bk][:, 0:258], hb[:, k, tsl], WFtm[:, k, 512:770], start=(k == 0), stop=(k == KC - 1)),
                             reads=[WB_, hbB], writes=[psB[bk]])
                    bk2 = pj.next()
                    for h in range(2):
                        P.op("pe", TR(ps[bk2][:, h * 128:(h + 1) * 128], kvf[ks][:, h * 128:(h + 1) * 128], ID),
                             reads=[kvfB[ks], Bc], writes=[psB[bk2]])
                    P.op("dve", TC(kT[:, :, j * 128:(j + 1) * 128], split(ps[bk2][:, 0:256], 2)), reads=[psB[bk2]], writes=[kTB[j]])
                    silu2_from_psum(fgs[s][:, tt, :], ps[bk][:, 0:256], et[ks][:], etB[ks], psB[bk], fgsB[s])
                    P.op("dve", TT(lfpre[s][:, tt, :], ps[bk][:, 256:258], bfb[:], ALU.add), reads=[psB[bk], WB_], writes=[lfpB[s]])
                lfp2 = flat(lfpre[s][:, :, :])
                P.op("act", ACT(lfp2, lfp2, AF.Exp, ZERO, -1.0), reads=[lfpB[s], Bc], writes=[lfpB[s]])
                P.op("act", ACT(lfp2, lfp2, AF.Ln, ONE, 1.0), reads=[lfpB[s], Bc], writes=[lfpB[s]])
                P.op("dve", TS(flat(LF[:, 4 * i:4 * i + 4, :]), lfp2, -1.0, None, ALU.mult), reads=[lfpB[s]], writes=[LFB])
                if i == 16:
                    sample_attention(s)
                    return
                cumsum_tiles(pj, LF[:, 4 * i:4 * i + 4, :], 4, carry, C[:, 4 * i:4 * i + 4, :], cs_tmp, csB, LFB, CB,
                             carry_out_ap=carry[:, :])
                P.op("dve", TC(cref[:, i, :], cs_tmp[1][:, 0:8].rearrange("p (a b) -> p a b", b=2)[:, 1, :]), reads=[csB], writes=[CB])
                nkt = 4 * i + 4
                for h in range(2):
                    P.op("dve", TS(btab[s][:, h, 0:nkt], C[:, 0:nkt, h], -1.0, cref[:, i, h:h + 1], ALU.mult, ALU.add),
                         reads=[CB], writes=[btB[s]])
                for h in range(2):
                    bA, bB = OB[h]

                    def score(kt):
                        jj = kt - 4 * i
                        c0 = 0 if jj < 0 else 128 * jj
                        sk = stb.next()
                        P.op("pe", MM(ps[sk][:, c0:512], kT[:, h, kt * 128:(kt + 1) * 128], qT[s][:, h, c0:512]),
                             reads=[kTB[kt], qTB[s]], writes=[psB[sk]])
                        return sk

                    skq = [score(0)]
                    if nkt > 1:
                        skq.append(score(1))
                    for kt in range(nkt):
                        if kt + 2 < nkt:
                            skq.append(score(kt + 2))
                        sk = skq.pop(0)
                        jj = kt - 4 * i
                        c0 = 0 if jj < 0 else 128 * jj
                        pi = ptrr.next()
                        P.op("act", ACT(PT[pi][:, c0:512], ps[sk][:, c0:512], AF.Exp, btab[s][:, h, kt:kt + 1], SCALE_F),
                             reads=[psB[sk], btB[s]], writes=[PTB[pi]])
                        if jj >= 0:
                            P.op("dve", TT(PT[pi][:, c0:c0 + 128], PT[pi][:, c0:c0 + 128], tri_b[:], ALU.mult),
                                 reads=[PTB[pi], Bc], writes=[PTB[pi]])
                        for sub in range(max(jj, 0), 4):
                            bo = bA if sub < 2 else bB
                            oc = (sub % 2) * 129
                            P.op("pe", MM(ps[bo][:, oc:oc + 129], PT[pi][:, sub * 128:(sub + 1) * 128], Vg[:, kt, h, :],
                                          start=(kt == 0 and sub % 2 == 0), stop=(kt == 4 * i + sub), skip=True),
                                 reads=[PTB[pi], VB[kt]], writes=[psB[bo]])
                    for sub in range(4):
                        j = 4 * i + sub
                        bo = bA if sub < 2 else bB
                        oc = (sub % 2) * 129
                        oi = j % 8
                        P.op("dve", RCP(rec[oi][:, h:h + 1], ps[bo][:, oc + 128:oc + 129]), reads=[psB[bo]], writes=[ostB[oi]])
                        P.op("dve", STT(ost[oi][:, h * 128:(h + 1) * 128], ps[bo][:, oc:oc + 128], rec[oi][:, h:h + 1],
                                        fgs[s][:, sub, h * 128:(h + 1) * 128], ALU.mult, ALU.mult),
                             reads=[psB[bo], fgsB[s], ostB[oi]], writes=[ostB[oi]])
                for sub in range(4):
                    j = 4 * i + sub
                    oi = j % 8
                    oa_st.append(P.op("sp", DMA(osrcA[j * 128:(j + 1) * 128, :], ost[oi][:]), reads=[ostB[oi]], dma=ostB[oi]))

            oa_st = []
            jobs = cast_jobs(l)
            load_hT_block(0, hTb[0], hTbB[0])
            for i in range(NBLK):
                block(i)
                emit_hw_casts((l, "G"), 2)
                emit_casts(l, jobs, 5)
                if i >= 4 and i % 4 == 0:
                    q = i // 4 - 1
                    ag_chunk(osrcA, oagA, oagAB, q, NTOK, 2048, oa_st[0:16])
                    del oa_st[0:16]
            ag_chunk(osrcA, oagA, oagAB, 4, NTOK, 2048, list(oa_st))
            emit_casts(l, jobs, len(jobs))
            P.op("sp", DMA(lfo_d[l], LF[:, :, :]), reads=[LFB], dma=LFB)
            P.barrier()

        def pass_g(l):
            A.reset(base_mark)
            WGfm = A.alloc("WGfm", [128, KC, 528], BF16)
            WGtm = A.alloc("WGtm", [128, KC, 512], BF16)
            WB_ = P.buf("WG")
            load_w_bf(WGfm, wgfm_b[l], WB_, (l, "G"))
            load_w_bf(WGtm, wgtm_b[l], WB_, (l, "G"))
            wa2 = A.alloc("wa2", [16, 256], BF16)
            wa2B = P.buf("wa2")
            P.op("pool", DMA(wa2[:], wa2_d[l]), writes=[wa2B], dma=wa2B)
            nba = A.alloc("nba", [128, 2], F32)
            gain = A.alloc("gain", [128, 256], F32)
            P.op("sp", DMA(nba[:], ba_d[l]), writes=[WB_], dma=WB_)
            P.op("sp", DMA(gain[:], gain_d[l]), writes=[WB_], dma=WB_)
            P.op("pool", TS(nba[:], nba[:], -1.0, None, ALU.mult), reads=[WB_], writes=[WB_])
            hTb = [A.alloc("hTb", [128, KC, 512], BF16) for _ in range(2)]
            hTbB = P.bufs(2, "hTb")
            glrT = A.alloc("glrT", [16, 512], BF16)
            glB = P.buf("glrT")
            et2 = A.alloc("et2", [128, 512], F32)
            et2B = P.buf("et2")
            sp_ = A.alloc("sp", [128, 2, 512], F32)
            spB = P.buf("sp")
            bpos = A.alloc("bpos", [128, 2, 512], F32)
            bpB = P.buf("bpos")
            ebT = A.alloc("ebT", [128, 2, 512], BF16)
            enbT = A.alloc("enbT", [128, 2, 512], BF16)
            ebB = P.buf("eb")
            ebl = [A.alloc("ebl", [128, 2, 4], F32) for _ in range(2)]
            eblB = P.bufs(2, "ebl")
            qtl = [A.alloc("qtl", [128, 2, 512], BF16) for _ in range(2)]
            qtB = P.bufs(2, "qtl")
            ktf = [A.alloc("ktf", [128, 2, 512], F32) for _ in range(2)]
            ktfB = P.bufs(2, "ktf")
            ktl = [A.alloc("ktl", [128, 2, 512], BF16) for _ in range(2)]
            ktlB = P.bufs(2, "ktl")
            gvb = [A.alloc("gvb", [128, 256], BF16) for _ in range(8)]
            gvB = P.bufs(8, "gvb")
            ggs = [A.alloc("ggs", [128, 256], BF16) for _ in range(8)]
            ggB = P.bufs(8, "ggs")
            etg = [A.alloc("etg", [128, 256], F32) for _ in range(2)]
            etgB = P.bufs(2, "etg")
            kttm = [A.alloc("kttm", [128, 256], BF16) for _ in range(2)]
            kttmB = P.bufs(2, "kttm")
            AT = [A.alloc("AT", [128, 128], BF16) for _ in range(2)]
            ATB = P.bufs(2, "AT")
            osb = [A.alloc("osb", [128, 256], F32) for _ in range(2)]
            osbB = P.bufs(2, "osb")
            junk = A.alloc("junkg", [128, 256], BF16)
            ssq = [A.alloc("ssqg", [128, 4], F32) for _ in range(2)]
            t1 = [A.alloc("t1g", [128, 256], F32) for _ in range(2)]
            obst = [A.alloc("obst", [128, 256], BF16) for _ in range(4)]
            obB = P.bufs(4, "obst")
            S = A.alloc("S", [128, 2, 256], F32)
            SB_ = P.buf("S")
            Sb = A.alloc("Sb", [128, 2, 256], BF16)
            SbB = P.buf("Sb")
            tmpS = A.alloc("tmpS", [128, 2, 256], F32)
            tSB = P.buf("tmpS")
            P.op("pool", MS(flat(S[:, :, :]), 0.0), writes=[SB_])
            P.op("pool", MS(flat(Sb[:, :, :]), 0.0), writes=[SbB])
            ostores = []

            def proj(i):
                s = i % 2
                hb, hbB = hTb[s], hTbB[s]
                bk = psrr.next()
                for k in range(KC):
                    P.op("pe", MM(ps[bk][0:16, :], WGfm[:, k, 512:528], hb[:, k, :], start=(k == 0), stop=(k == KC - 1)),
                         reads=[WB_, hbB], writes=[psB[bk]])
                P.op("act", CP(glrT[:, :], ps[bk][0:16, :]), reads=[psB[bk]], writes=[glB])
                for c in range(2):
                    bk = psrr.next()
                    P.op("pe", MM(ps[bk][:, :], wa2[:, c * 128:(c + 1) * 128], glrT[:, :]), reads=[wa2B, glB], writes=[psB[bk]])
                    P.op("act", ACT(et2[:], ps[bk][:, :], AF.Exp, nba[:, c:c + 1], -1.0), reads=[psB[bk], WB_], writes=[et2B])
                    P.op("act", ACT(sp_[:, c, :], et2[:], AF.Ln, ONE, 1.0), reads=[et2B, Bc], writes=[spB])
                    P.op("dve", SCAN(bpos[:, c, :], resetm[:], sp_[:, c, :], 0.0), reads=[spB, Bc], writes=[bpB])
                bp2 = flat(bpos[:, :, :])
                P.op("act", ACT(flat(ebT[:, :, :]), bp2, AF.Exp, NL16, -1.0 / 16.0), reads=[bpB, Bc], writes=[ebB])
                P.op("act", ACT(flat(enbT[:, :, :]), bp2, AF.Exp, ZERO, 1.0 / 16.0), reads=[bpB, Bc], writes=[ebB])
                lastc = 31 if i == 16 else 127
                for c in range(2):
                    P.op("act", ACT(ebl[s][:, c, :], split(bpos[:, c, :], 4)[:, :, lastc], AF.Exp, ZERO, -1.0 / 16.0),
                         reads=[bpB, Bc], writes=[eblB[s]])
                for c in range(2):
                    bk = psrr.next()
                    for k in range(KC):
                        P.op("pe", MM(ps[bk][:, :], WGfm[:, k, c * 128:(c + 1) * 128], hb[:, k, :], start=(k == 0), stop=(k == KC - 1)),
                             reads=[WB_, hbB], writes=[psB[bk]])
                    P.op("dve", TT(qtl[s][:, c, :], ps[bk][:, :], ebT[:, c, :], ALU.mult), reads=[psB[bk], ebB], writes=[qtB[s]])
                for c in range(2):
                    bk = psrr.next()
                    for k in range(KC):
                        P.op("pe", MM(ps[bk][:, :], WGfm[:, k, 256 + c * 128:256 + (c + 1) * 128], hb[:, k, :],
                                      start=(k == 0), stop=(k == KC - 1)), reads=[WB_, hbB], writes=[psB[bk]])
                    P.op("dve", TT(ktf[s][:, c, :], ps[bk][:, :], enbT[:, c, :], ALU.mult), reads=[psB[bk], ebB], writes=[ktfB[s]])
                P.op("act", CP(flat(ktl[s][:, :, :]), flat(ktf[s][:, :, :])), reads=[ktfB[s]], writes=[ktlB[s]])
                for tt in range(4):
                    j = 4 * i + tt
                    tsl = slice(tt * 128, (tt + 1) * 128)
                    g8, g2 = j % 8, j % 2
                    bk = psrr.next()
                    for k in range(KC):
                        P.op("pe", MM(ps[bk][:, :], hb[:, k, tsl], WGtm[:, k, :], start=(k == 0), stop=(k == KC - 1)),
                             reads=[WB_, hbB], writes=[psB[bk]])
                    P.op("act", CP(gvb[g8][:], ps[bk][:, 0:256]), reads=[psB[bk]], writes=[gvB[g8]])
                    silu2_from_psum(ggs[g8][:], ps[bk][:, 256:512], etg[g2][:], etgB[g2], psB[bk], ggB[g8])

            def recur(i):
                s = i % 2
                for tt in range(4):
                    j = 4 * i + tt
                    tsl = slice(tt * 128, (tt + 1) * 128)
                    g8, g4, g2 = j % 8, j % 4, j % 2
                    if i == 16:
                        P.op("sp", DMA(S[:, :, :], sg_d[l, tt]), writes=[SB_], dma=SB_)
                        P.op("act", CP(flat(Sb[:, :, :]), flat(S[:, :, :])), reads=[SB_], writes=[SbB])
                    bk = psrr.next()
                    for c in range(2):
                        P.op("pe", TR(ps[bk][:, c * 128:(c + 1) * 128], ktf[s][:, c, tsl], ID), reads=[ktfB[s], Bc], writes=[psB[bk]])
                    P.op("act", CP(kttm[g2][:], ps[bk][:, 0:256]), reads=[psB[bk]], writes=[kttmB[g2]])
                    bk = psrr.next()
                    for c in range(2):
                        P.op("pe", MM(ps[bk][:, 0:128], ktl[s][:, c, tsl], qtl[s][:, c, tsl], start=(c == 0), stop=(c == 1)),
                             reads=[ktlB[s], qtB[s]], writes=[psB[bk]])
                    P.op("dve", TT(AT[g2][:], ps[bk][:, 0:128], tri_f[:], ALU.mult), reads=[psB[bk], Bc], writes=[ATB[g2]])
                    bk = psrr.next()
                    P.op("pe", MM(ps[bk][:, 0:256], AT[g2][:], gvb[g8][:], start=True, stop=False),
                         reads=[ATB[g2], gvB[g8]], writes=[psB[bk]])
                    for c in range(2):
                        P.op("pe", MM(ps[bk][:, 0:256], qtl[s][:, c, tsl], Sb[:, c, :], start=False, stop=(c == 1)),
                             reads=[qtB[s], SbB], writes=[psB[bk]])
                    P.op("act", CP(osb[g2][:], ps[bk][:, 0:256]), reads=[psB[bk]], writes=[osbB[g2]])
                    bk = psrr.next()
                    for c in range(2):
                        P.op("pe", MM(ps[bk][:, c * 256:(c + 1) * 256], kttm[g2][:, c * 128:(c + 1) * 128], gvb[g8][:],
                                      start=(c == 0), stop=True, skip=True), reads=[kttmB[g2], gvB[g8]], writes=[psB[bk]])
                    P.op("dve", TT(flat(tmpS[:, :, :]), ps[bk][:, :], flat(S[:, :, :]), ALU.add), reads=[psB[bk], SB_], writes=[tSB])
                    for c in range(2):
                        P.op("dve", TS(S[:, c, :], tmpS[:, c, :], ebl[s][:, c, tt:tt + 1], None, ALU.mult), reads=[tSB, eblB[s]], writes=[SB_])
                    P.op("act", CP(flat(Sb[:, :, :]), flat(S[:, :, :])), reads=[SB_], writes=[SbB])
                    if i == 16:
                        P.op("sp", DMA(gs_d[l, tt], S[:, :, :]), reads=[SB_], dma=SB_)
                    elif j == 63:
                        P.op("sp", DMA(gp_d[l], S[:, :, :]), reads=[SB_], dma=SB_)
                    P.op("dve", STT(junk[:], osb[g2][:], 1.0, osb[g2][:], ALU.mult, ALU.mult, accum=ssq[g2][:, 0:1]),
                         reads=[osbB[g2]], writes=[osbB[g2]])
                    rstd_of(ssq[g2], 256, osbB[g2], lnbias=LNH)
                    P.op("dve", STT(t1[g2][:], osb[g2][:], ssq[g2][:, 2:3], gain[:], ALU.mult, ALU.mult),
                         reads=[osbB[g2], WB_], writes=[osbB[g2]])
                    P.op("pool", TT(obst[g4][:], t1[g2][:], ggs[g8][:], ALU.mult), reads=[osbB[g2], ggB[g8]], writes=[obB[g4]])
                    ostores.append(P.op("sp", DMA(osrcB[j * 128:(j + 1) * 128, :], obst[g4][:]), reads=[obB[g4]], dma=obB[g4]))

            load_hT_block(0, hTb[0], hTbB[0])
            load_hT_block(1, hTb[1], hTbB[1])
            proj(0)
            for i in range(NBLK):
                if i + 1 < NBLK:
                    proj(i + 1)
                    if i + 2 < NBLK:
                        load_hT_block(i + 2, hTb[i % 2], hTbB[i % 2])
                recur(i)
                if l + 1 < DEPTH:
                    emit_hw_casts((l + 1, "F"), 1)
                    emit_hw_casts((l + 1, "G"), 1)
                if i >= 4 and i % 4 == 0:
                    q = i // 4 - 1
                    ag_chunk(osrcB, oagB_d, oagBB, q, NTOK, 2048, ostores[0:16])
                    del ostores[0:16]
            ag_chunk(osrcB, oagB_d, oagBB, 4, NTOK, 2048, list(ostores))
            P.barrier()

        def token_phase(l):
            last = (l == DEPTH - 1)
            A.reset(base_mark)
            hTg = A.alloc("hTg", [128, KC, 512], BF16)
            hTgB = P.buf("hTg")
            oT = A.alloc("oT", [128, KC, 512], BF16)
            oTB = P.buf("oT")
            mT = A.alloc("mT", [128, KC, 512], BF16)
            mTB = P.buf("mT")
            z = A.alloc("z", [128, 4, D], F32)
            zB = P.bufs(4, "z")
            ogf = A.alloc("ogf", [128, D], F32)
            ogB = P.buf("ogf")
            wr = [dict(ma=A.alloc("wma", [128, KC, 128], BF16), mb=A.alloc("wmb", [128, KC, 128], BF16),
                       oa=A.alloc("woa", [128, 8, 128], BF16), ob=A.alloc("wob", [128, 8, 128], BF16)) for _ in range(2)]
            wrB = P.bufs(2, "wr")
            wo = [A.alloc("wo", [128, KC, 256], BF16) for _ in range(2)]
            woB = P.bufs(2, "wo")
            xt = A.alloc("xt", [128, D], F32)
            xB = P.buf("xt")
            yt = A.alloc("yt", [128, D], F32)
            yB = P.buf("yt")
            t1 = A.alloc("t1", [128, D], F32)
            tB1 = P.buf("t1")
            ssqz = A.alloc("ssqz", [128, 4], F32)
            post_t = A.alloc("post_t", [128, D], F32)
            pre_t = A.alloc("pre_t", [128, D], F32)
            ppB = P.buf("pp")
            P.op("sp", DMA(post_t[:], post_d[l]), writes=[ppB], dma=ppB)
            if not last:
                P.op("sp", DMA(pre_t[:], pre_d[l + 1]), writes=[ppB], dma=ppB)
            htmp = (A.alloc("junk", [128, D], BF16), A.alloc("ssq", [128, 4], F32), A.alloc("hf", [128, D], F32),
                    A.alloc("hTst", [128, KC, 128], BF16), P.buf("hTst"), P.buf("htmp"))
            ge = [A.alloc("ge", [128, 512], F32) for _ in range(4)]
            geB = P.bufs(4, "ge")
            gm = [A.alloc("gm", [128, 512], F32) for _ in range(4)]
            gmB = P.bufs(4, "gm")
            src_x = x_d if l == 0 else yscr

            wdeps = wcast_ops[l]

            def load_wr(c, sl):
                w, B = wr[sl], wrB[sl]
                P.op("sp", DMA(flat(w["ma"][:, :, :]), wm_b[l, c]), writes=[B], dma=B, extra_deps=wdeps)
                P.op("sp", DMA(flat(w["mb"][:, :, :]), wm_b[l, 16 + c]), writes=[B], dma=B, extra_deps=wdeps)
                P.op("sp", DMA(flat(w["oa"][:, :, :]), woa_b[l, c]), writes=[B], dma=B, extra_deps=wdeps)
                P.op("sp", DMA(flat(w["ob"][:, :, :]), wob_b[l, c]), writes=[B], dma=B, extra_deps=wdeps)

            def load_wo(blk, sl):
                P.op("sp", DMA(flat(wo[sl][:, :, :]), wout_b[l, blk]), writes=[woB[sl]], dma=woB[sl], extra_deps=wdeps)

            def stage_a_tile(grp, gi):
                t = grp[gi]
                for r in range(4):
                    for (src_, B_, c0_) in ((oagA, oagAB[t // 4], 0), (oagB_d, oagBB[t // 4], 256)):
                        P.op("pool", (lambda o_, i_, s_: (lambda e: e.indirect_dma_start(
                            out=o_, out_offset=None, in_=s_[:, :],
                            in_offset=bass.IndirectOffsetOnAxis(ap=i_, axis=0))))(
                                ogf[:, r * 512 + c0_:r * 512 + c0_ + 256], idx_t[:, t * 4 + r:t * 4 + r + 1], src_),
                            reads=[B_, Bc], writes=[ogB], dma=ogB)
                for q in range(4):
                    bk = psrr.next()
                    for jx in range(4):
                        c = q * 4 + jx
                        if c < 8:
                            col = (c // 2) * 512 + (c % 2) * 128
                        else:
                            cc = c - 8
                            col = (cc // 2) * 512 + 256 + (cc % 2) * 128
                        P.op("pe", TR(ps[bk][:, jx * 128:(jx + 1) * 128], ogf[:, col:col + 128], ID),
                             reads=[ogB, Bc], writes=[psB[bk]])
                    P.op("act", CP(oT[:, q * 4:(q + 1) * 4, gi * 128:(gi + 1) * 128], split(ps[bk][:, :], 4)),
                         reads=[psB[bk]], writes=[oTB])
                P.op("sp", DMA(hTg[:, :, gi * 128:(gi + 1) * 128], split(hsrc[t * 128:(t + 1) * 128, :], KC)),
                     reads=[hsrcB[t]], writes=[hTgB], dma=hTgB)

            def stage_b(grp, inter):
                ntok = len(grp) * 128
                load_wr(0, 0)
                for c in range(16):
                    sl = c % 2
                    if c + 1 < 16:
                        load_wr(c + 1, 1 - sl)
                    w, wB = wr[sl], wrB[sl]
                    bks = [psrr.next() for _ in range(4)]
                    for k in range(KC):
                        P.op("pe", MM(ps[bks[0]][:, 0:ntok], w["ma"][:, k, :], hTg[:, k, 0:ntok], start=(k == 0), stop=(k == KC - 1)),
                             reads=[wB, hTgB], writes=[psB[bks[0]]])
                    for k in range(KC):
                        P.op("pe", MM(ps[bks[1]][:, 0:ntok], w["mb"][:, k, :], hTg[:, k, 0:ntok], start=(k == 0), stop=(k == KC - 1)),
                             reads=[wB, hTgB], writes=[psB[bks[1]]])
                    for k in range(8):
                        P.op("pe", MM(ps[bks[2]][:, 0:ntok], w["oa"][:, k, :], oT[:, k, 0:ntok], start=(k == 0), stop=(k == 7)),
                             reads=[wB, oTB], writes=[psB[bks[2]]])
                    for k in range(8):
                        P.op("pe", MM(ps[bks[3]][:, 0:ntok], w["ob"][:, k, :], oT[:, 8 + k, 0:ntok], start=(k == 0), stop=(k == 7)),
                             reads=[wB, oTB], writes=[psB[bks[3]]])
                    for u in range(2):
                        gi_ = u + 2 * (c % 2)
                        gu = ge[gi_][:, 0:ntok]
                        P.op("act", ACT(gu, ps[bks[u]][:, 0:ntok], AF.Tanh, ZERO, 0.5), reads=[psB[bks[u]], Bc], writes=[geB[gi_]])
                        P.op("dve", STT(gm[gi_][:, 0:ntok], gu, 1.0, ps[bks[2 + u]][:, 0:ntok], ALU.add, ALU.mult),
                             reads=[psB[bks[2 + u]], geB[gi_]], writes=[gmB[gi_]])
                    g0, g1 = 2 * (c % 2), 1 + 2 * (c % 2)
                    P.op("dve", TT(mT[:, c, 0:ntok], gm[g0][:, 0:ntok], gm[g1][:, 0:ntok], ALU.add), reads=[gmB[g0], gmB[g1]], writes=[mTB])
                    if c in inter:
                        inter[c]()

            def stage_c(grp, inter):
                ng = len(grp)
                load_wo(0, 0)
                for blk in range(8):
                    sl = blk % 2
                    if blk + 1 < 8:
                        load_wo(blk + 1, 1 - sl)
                    for gi in range(ng):
                        bk = psrr.next()
                        for k in range(KC):
                            P.op("pe", MM(ps[bk][:, 0:256], mT[:, k, gi * 128:(gi + 1) * 128], wo[sl][:, k, :],
                                          start=(k == 0), stop=(k == KC - 1)), reads=[mTB, woB[sl]], writes=[psB[bk]])
                        P.op("act", MUL(z[:, gi, blk * 256:(blk + 1) * 256], ps[bk][:, 0:256], 0.5), reads=[psB[bk]], writes=[zB[gi]])
                    if blk in inter:
                        inter[blk]()

            def stage_d_tile(grp, gi):
                t = grp[gi]
                P.op("sp", DMA(xt[:], src_x[t * 128:(t + 1) * 128, :]), writes=[xB], dma=xB)
                P.op("dve", STT(htmp[0][:], z[:, gi, :], 1.0, z[:, gi, :], ALU.mult, ALU.mult, accum=ssqz[:, 0:1]),
                     reads=[zB[gi]], writes=[tB1, htmp[5]])
                rstd_of(ssqz, D, tB1)
                P.op("dve", STT(t1[:], z[:, gi, :], ssqz[:, 2:3], post_t[:], ALU.mult, ALU.mult), reads=[zB[gi], tB1, ppB], writes=[tB1])
                P.op("dve", TT(yt[:], t1[:], xt[:], ALU.add), reads=[tB1, xB], writes=[yB])
                if last:
                    P.op("sp", DMA(y_d[t * 128:(t + 1) * 128, :], yt[:]), reads=[yB], dma=yB)
                else:
                    P.op("sp", DMA(yscr[t * 128:(t + 1) * 128, :], yt[:]), reads=[yB], dma=yB)
                    hst.append(emit_h(yt, yB, pre_t, ppB, t, htmp))
                    if t % 2 == 1 or t == TL - 1:
                        ag_chunk(hsrc, hag, hagB, t // 2, TL * 128, 256, list(hst))
                        del hst[:]

            def spread(fns, slots):
                m = {}
                for n_, f in enumerate(fns):
                    sl_ = slots[min(n_, len(slots) - 1)]
                    m.setdefault(sl_, []).append(f)
                return {k_: (lambda fs=v_: [f() for f in fs]) for k_, v_ in m.items()}

            hst = []
            G = len(TGROUPS)
            for gi in range(len(TGROUPS[0])):
                stage_a_tile(TGROUPS[0], gi)
            for g in range(G):
                grp = TGROUPS[g]
                d_fns = [] if g == 0 else [(lambda pg=TGROUPS[g - 1], x_=x: stage_d_tile(pg, x_)) for x in range(len(TGROUPS[g - 1]))]
                stage_b(grp, spread(d_fns, [3, 7, 11, 15]))
                a_fns = [] if g + 1 == G else [(lambda ng_=TGROUPS[g + 1], x_=x: stage_a_tile(ng_, x_)) for x in range(len(TGROUPS[g + 1]))]
                stage_c(grp, spread(a_fns, [1, 3, 5, 7]))
            for x in range(len(TGROUPS[G - 1])):
                stage_d_tile(TGROUPS[G - 1], x)
            P.barrier()

        def run_all():
            prologue()
            if STOP_AFTER == "pro":
                return
            for l in range(DEPTH):
                pass_f(l)
                if STOP_AFTER == "F%d" % l:
                    return
                pass_g(l)
                if STOP_AFTER == "G%d" % l:
                    return
                if STOP_AFTER == "AG%d" % l:
                    return
                token_phase(l)
                if STOP_AFTER == "T%d" % l:
                    return

        run_all()
        P.barrier()
        stats = P.emit()
        stats["sbuf_peak"] = A.peak
    return nc, stats


_OFF = dict(fq=0, fk=1024, fv=2048, ff=3072, fg=3080, gq=4104, gk=5128, gv=6152, glr=7176, gg=7192, ma=8216, mb=10264)


def _pkc(w):
    K = w.shape[0] // 128
    return np.ascontiguousarray(w.reshape(K, 128, w.shape[1]).transpose(1, 0, 2))


def _chunks(w, cw):
    K = w.shape[0] // 128
    NC = w.shape[1] // cw
    return np.ascontiguousarray(w.reshape(K, 128, NC, cw).transpose(2, 1, 0, 3))


def _prep(inp):
    f = lambda a: np.asarray(a, dtype=np.float32)
    x_prompt, x_sample = f(inp["x_prompt"]), f(inp["x_sample"])
    cache_k, cache_v, cache_logf, state_gla = f(inp["cache_k"]), f(inp["cache_v"]), f(inp["cache_logf"]), f(inp["state_gla"])
    w_in, w_a2, b_a, b_f = f(inp["w_in"]), f(inp["w_a2"]), f(inp["b_a"]), f(inp["b_f"])
    gla_gain, w_oa, w_ob, w_out = f(inp["gla_gain"]), f(inp["w_oa"]), f(inp["w_ob"]), f(inp["w_out"])
    pre_norm, post_norm = f(inp["pre_norm"]), f(inp["post_norm"])

    shared = {}
    shared["wm"] = np.stack([_chunks(w_in[l][:, _OFF["ma"]:_OFF["ma"] + 4096], 128) for l in range(DEPTH)])
    shared["woa"] = np.stack([_chunks(w_oa[l], 128) for l in range(DEPTH)])
    shared["wob"] = np.stack([_chunks(w_ob[l], 128) for l in range(DEPTH)])
    shared["wout"] = np.stack([_chunks(w_out[l], 256) for l in range(DEPTH)])
    shared["gain"] = np.ascontiguousarray(np.broadcast_to(gla_gain[:, None, :], (DEPTH, 128, 256)))
    shared["pre"] = np.ascontiguousarray(np.broadcast_to(pre_norm[:, None, :], (DEPTH, 128, D)))
    shared["post"] = np.ascontiguousarray(np.broadcast_to(post_norm[:, None, :], (DEPTH, 128, D)))

    maps = []
    for c in range(8):
        b, g = c // 4, c % 4
        m = dict(shared)
        xl = np.zeros((TL * 128, D), np.float32)
        for k in range(16):
            xl[k * 128:(k + 1) * 128] = x_prompt[b, (4 * k + g) * 128:(4 * k + g + 1) * 128]
        xl[2048:2080] = x_sample[4 * b + g]
        m["x_loc"] = xl
        fs = slice(256 * g, 256 * g + 256)

        def cols(l, name, sl=fs):
            o = _OFF[name]
            return w_in[l][:, o + sl.start:o + sl.stop]

        m["wffm"] = np.stack([_pkc(cols(l, "fq")) for l in range(DEPTH)])
        m["wftm"] = np.stack([_pkc(np.concatenate([cols(l, "fk"), cols(l, "fv"), cols(l, "fg"),
                                                   cols(l, "ff", slice(2 * g, 2 * g + 2))], 1)) for l in range(DEPTH)])
        m["wgfm"] = np.stack([_pkc(np.concatenate([cols(l, "gq"), cols(l, "gk"), cols(l, "glr", slice(0, 16))], 1))
                              for l in range(DEPTH)])
        m["wgtm"] = np.stack([_pkc(np.concatenate([cols(l, "gv"), cols(l, "gg")], 1)) for l in range(DEPTH)])
        m["wa2"] = np.ascontiguousarray(w_a2[:, :, fs])
        m["ba"] = np.ascontiguousarray(b_a[:, fs].reshape(DEPTH, 2, 128).transpose(0, 2, 1))
        m["bfb"] = np.ascontiguousarray(np.broadcast_to(b_f[:, None, 2 * g:2 * g + 2], (DEPTH, 128, 2)))
        sbs = slice(4 * b, 4 * b + 4)
        m["ck"] = np.ascontiguousarray(cache_k[:, sbs, :, 2 * g:2 * g + 2, :].reshape(DEPTH, 4, 4096, 256))
        m["cv"] = np.ascontiguousarray(cache_v[:, sbs, :, 2 * g:2 * g + 2, :].reshape(DEPTH, 4, 4096, 256))
        m["clf"] = np.ascontiguousarray(cache_logf[:, sbs, :, 2 * g:2 * g + 2].reshape(DEPTH, 4, 32, 128, 2).transpose(0, 1, 3, 2, 4))
        m["sg"] = np.ascontiguousarray(state_gla[:, sbs, g].reshape(DEPTH, 4, 2, 128, 256).transpose(0, 1, 3, 2, 4))
        idx = np.zeros((128, TL * 4), np.int32)
        p = np.arange(128)
        for t in range(TL):
            tok = ((4 * t + g) * 128 + p) if t < 16 else (8192 + 128 * g + p)
            q, within = tok // 2048, tok % 2048
            nq = np.where(q < 4, 2048, 512)
            for r in range(4):
                idx[:, t * 4 + r] = q * 8192 + r * nq + within
        m["idx"] = idx
        maps.append(m)
    return maps


_CACHE = {}


def kernel(**inputs):
    if "nc" not in _CACHE:
        _CACHE["nc"], _CACHE["stats"] = build_program()
    nc = _CACHE["nc"]
    maps = _prep(inputs)
    res = run_bass_kernel_spmd(nc, maps, core_ids=list(range(8)))
    R = res.results
    B, SEQ, DB, DS = 2, 8192, 8, 32
    y_prompt = np.zeros((B, SEQ, D), np.float32)
    y_sample = np.zeros((DB, DS, D), np.float32)
    k_prompt = np.zeros((DEPTH, B, SEQ, 8, 128), np.float32)
    v_prompt = np.zeros((DEPTH, B, SEQ, 8, 128), np.float32)
    logf_prompt = np.zeros((DEPTH, B, SEQ, 8), np.float32)
    gla_prompt = np.zeros((DEPTH, B, 4, 256, 256), np.float32)
    k_sample = np.zeros((DEPTH, DB, DS, 8, 128), np.float32)
    v_sample = np.zeros((DEPTH, DB, DS, 8, 128), np.float32)
    logf_sample = np.zeros((DEPTH, DB, DS, 8), np.float32)
    gla_sample = np.zeros((DEPTH, DB, 4, 256, 256), np.float32)
    for c in range(8):
        b, g = c // 4, c % 4
        r = R[c]
        y = np.asarray(r["y_loc"])
        for k in range(16):
            y_prompt[b, (4 * k + g) * 128:(4 * k + g + 1) * 128] = y[k * 128:(k + 1) * 128]
        y_sample[4 * b + g] = y[2048:2080]
        ko, vo = np.asarray(r["k_out"]), np.asarray(r["v_out"])
        lf = np.asarray(r["lf_out"]).transpose(0, 2, 1, 3).reshape(DEPTH, NTOK, 2)
        gp, gs = np.asarray(r["gla_p"]), np.asarray(r["gla_s"])
        for l in range(DEPTH):
            k_prompt[l, b, :, 2 * g:2 * g + 2, :] = ko[l, :SEQ].reshape(SEQ, 2, 128)
            v_prompt[l, b, :, 2 * g:2 * g + 2, :] = vo[l, :SEQ].reshape(SEQ, 2, 128)
            logf_prompt[l, b, :, 2 * g:2 * g + 2] = lf[l, :SEQ]
            gla_prompt[l, b, g] = gp[l].transpose(1, 0, 2).reshape(256, 256)
            for k4 in range(4):
                o = SEQ + 128 * k4
                k_sample[l, 4 * b + k4, :, 2 * g:2 * g + 2, :] = ko[l, o:o + DS].reshape(DS, 2, 128)
                v_sample[l, 4 * b + k4, :, 2 * g:2 * g + 2, :] = vo[l, o:o + DS].reshape(DS, 2, 128)
                logf_sample[l, 4 * b + k4, :, 2 * g:2 * g + 2] = lf[l, o:o + DS]
                gla_sample[l, 4 * b + k4, g] = gs[l, k4].transpose(1, 0, 2).reshape(256, 256)
    return (y_prompt, y_sample, k_prompt, v_prompt, logf_prompt, gla_prompt,
            k_sample, v_sample, logf_sample, gla_sample)
```

```python
import math
import numpy as np
from contextlib import ExitStack
import concourse.bass as bass
import concourse.mybir as mybir
from concourse.bass_utils import run_bass_kernel_spmd

F32, BF16, I32 = mybir.dt.float32, mybir.dt.bfloat16, mybir.dt.int32
AF = mybir.ActivationFunctionType
ALU = mybir.AluOpType

D = 2048
KC = 16
NT = 68
NBLK = 17
TL = 17
NTOK = NT * 128
DEPTH = 2
SCALE_F = 128 ** -0.5
GROUPS = [[0, 1, 2, 3], [4, 5, 6, 7]]
STOP_AFTER = None
TGROUPS = [[0, 1, 2, 3], [4, 5, 6, 7], [8, 9, 10, 11], [12, 13, 14], [15, 16]]
SB_BASE = 16512
SB_END = 229344


class Buf:
    __slots__ = ("name", "w", "r", "dsem", "dcnt")

    def __init__(self, name):
        self.name = name
        self.w = None
        self.r = []
        self.dsem = None
        self.dcnt = 0


class Op:
    __slots__ = ("eng", "fn", "deps", "dma", "token", "mark", "pos")

    def __init__(self, eng, fn, deps, dma):
        self.eng = eng
        self.fn = fn
        self.deps = deps
        self.dma = dma
        self.token = None
        self.mark = False
        self.pos = 0


class Prog:
    ENG = ("pe", "act", "dve", "pool", "sp")

    def __init__(self, nc, stack):
        self.nc = nc
        self.stack = stack
        self.ops = []
        self.h = {"pe": nc.tensor, "act": nc.scalar, "dve": nc.vector, "pool": nc.gpsimd, "sp": nc.sync}
        self.esem = {e: stack.enter_context(nc.semaphore("es_" + e)) for e in self.ENG}
        self.ccsem = stack.enter_context(nc.semaphore("ccsem"))
        self.cccnt = 0
        self.last = {e: None for e in self.ENG}
        self.pending = []
        self.nbuf = 0
        self.nsem = 6

    def buf(self, name=None):
        self.nbuf += 1
        return Buf("%s_%d" % (name or "b", self.nbuf))

    def bufs(self, n, name="b"):
        return [self.buf(name) for _ in range(n)]

    def _dsem(self, b):
        if b.dsem is None:
            b.dsem = self.stack.enter_context(self.nc.semaphore("ds_" + b.name))
            self.nsem += 1
        return b.dsem

    def op(self, eng, fn, reads=(), writes=(), dma=None, cc=False, extra_deps=(), nobarrier=False):
        idx = len(self.ops)
        deps = set(extra_deps)
        for b in reads:
            if b.w is not None:
                deps.add(b.w)
            b.r.append(idx)
        for b in writes:
            if b.w is not None:
                pw = self.ops[b.w]
                if dma is not None and pw.dma is dma and not b.r:
                    deps.update(pw.deps)
                else:
                    deps.add(b.w)
            deps.update(b.r)
            b.r = []
            b.w = idx
        deps.discard(idx)
        latest = {}
        keep = []
        for di in deps:
            d = self.ops[di]
            if d.dma is None and d.fn is not None:
                if d.eng not in latest or latest[d.eng] < di:
                    latest[d.eng] = di
            else:
                keep.append(di)
        deps = keep + list(latest.values())
        o = Op(eng, fn, sorted(deps), dma)
        if dma is not None:
            sem = self._dsem(dma)
            dma.dcnt += 16
            o.token = (sem, dma.dcnt)
            if not nobarrier:
                self.pending.append(idx)
        elif cc:
            self.cccnt += 1
            o.token = (self.ccsem, self.cccnt)
            o.dma = "cc"
        self.ops.append(o)
        if fn is not None and not cc:
            self.last[eng] = idx
        return idx

    def barrier(self):
        deps = [v for v in self.last.values() if v is not None] + list(self.pending)
        self.pending = []
        for e in self.ENG:
            self.op(e, None, extra_deps=deps)

    def _needs_wait(self, o, d):
        if d.dma is not None:
            return True
        if d.eng != o.eng:
            return True
        if o.dma is not None:
            return True
        if o.eng == "pe":
            return False
        return True

    def emit(self):
        ops = self.ops
        pos = {e: 0 for e in self.ENG}
        for o in ops:
            if o.fn is not None:
                pos[o.eng] += 1
            o.pos = pos[o.eng]
        for o in ops:
            for di in o.deps:
                d = ops[di]
                if d.dma is None and d.fn is not None and self._needs_wait(o, d):
                    d.mark = True
        cnt = {e: 0 for e in self.ENG}
        for o in ops:
            if o.dma is None and o.mark:
                cnt[o.eng] += 1
                o.token = (self.esem[o.eng], cnt[o.eng])
        seen = {e: {} for e in self.ENG}
        nwait = 0
        for o in ops:
            E = self.h[o.eng]
            waits = {}
            for di in o.deps:
                d = ops[di]
                if d.fn is None or not self._needs_wait(o, d):
                    continue
                sem, val = d.token
                k = id(sem)
                if k not in waits or waits[k][1] < val:
                    waits[k] = (sem, val)
            for k, (sem, val) in waits.items():
                if seen[o.eng].get(k, 0) < val:
                    E.wait_ge(sem, val)
                    seen[o.eng][k] = val
                    nwait += 1
            if o.fn is None:
                continue
            ins = o.fn(E)
            if o.dma is not None:
                ins.then_inc(o.token[0], 1 if o.dma == "cc" else 16)
            elif o.mark:
                ins.then_inc(o.token[0], 1)
        return dict(n_ops=len(ops), n_wait=nwait, marked=cnt, nsem=self.nsem)


class Arena:
    def __init__(self, nc):
        self.nc = nc
        self.off = SB_BASE
        self.n = 0
        self.peak = 0

    def alloc(self, name, shape, dtype):
        nbytes = int(np.prod(shape[1:])) * (4 if dtype in (F32, I32) else 2)
        nbytes = (nbytes + 31) // 32 * 32
        assert self.off + nbytes <= SB_END, (name, self.off, nbytes)
        self.n += 1
        t = self.nc.alloc_sbuf_tensor_at("%s_%d" % (name, self.n), list(shape), dtype, offset=self.off)
        self.off += nbytes
        self.peak = max(self.peak, self.off)
        return t

    def mark(self):
        return self.off

    def reset(self, m):
        self.off = m


class RR:
    def __init__(self, items):
        self.items = list(items)
        self.i = 0

    def next(self):
        x = self.items[self.i % len(self.items)]
        self.i += 1
        return x


def MM(out, lhsT, rhs, start=True, stop=True, skip=False):
    return lambda e: e.matmul(out, lhsT=lhsT, rhs=rhs, start=start, stop=stop, skip_group_check=skip)


def TR(out, in_, ident):
    return lambda e: e.transpose(out=out, in_=in_, identity=ident)


def ACT(out, in_, func, bias, scale):
    return lambda e: e.activation(out=out, in_=in_, func=func, bias=bias, scale=scale)


def MUL(out, in_, c):
    return lambda e: e.mul(out, in_, c)


def CP(out, in_):
    return lambda e: e.copy(out=out, in_=in_)


def TC(out, in_):
    return lambda e: e.tensor_copy(out=out, in_=in_)


def TT(out, in0, in1, op):
    return lambda e: e.tensor_tensor(out=out, in0=in0, in1=in1, op=op)


def TS(out, in0, s1, s2, op0, op1=None):
    if op1 is None:
        return lambda e: e.tensor_scalar(out=out, in0=in0, scalar1=s1, scalar2=None, op0=op0)
    return lambda e: e.tensor_scalar(out=out, in0=in0, scalar1=s1, scalar2=s2, op0=op0, op1=op1)


def STT(out, in0, scalar, in1, op0, op1, accum=None):
    if accum is None:
        return lambda e: e.scalar_tensor_tensor(out=out, in0=in0, scalar=scalar, in1=in1, op0=op0, op1=op1)
    return lambda e: e.scalar_tensor_tensor(out=out, in0=in0, scalar=scalar, in1=in1, op0=op0, op1=op1, accum_out=accum)


def SCAN(out, d0, d1, init):
    return lambda e: e.tensor_tensor_scan(out=out, data0=d0, data1=d1, initial=init, op0=ALU.mult, op1=ALU.add)


def RCP(out, in_):
    return lambda e: e.reciprocal(out=out, in_=in_)


def MS(ap, v):
    return lambda e: e.memset(ap, v)


def DMA(out, in_):
    return lambda e: e.dma_start(out=out, in_=in_)


def flat(ap3):
    n = len(ap3.shape)
    if n == 3:
        return ap3.rearrange("p a b -> p (a b)")
    if n == 4:
        return ap3.rearrange("p a b c -> p (a b c)")
    return ap3


def split(ap2, a):
    return ap2.rearrange("p (a b) -> p a b", a=a)


def build_program():
    nc = bass.Bass("TRN2", target_bir_lowering=False)

    def din(name, shape, dt=F32):
        return nc.dram_tensor(name, list(shape), dt, kind="ExternalInput").ap()

    def dout(name, shape, dt=F32):
        return nc.dram_tensor(name, list(shape), dt, kind="ExternalOutput").ap()

    x_d = din("x_loc", [TL * 128, D])
    wffm_d = din("wffm", [DEPTH, 128, KC, 256])
    wftm_d = din("wftm", [DEPTH, 128, KC, 770])
    wgfm_d = din("wgfm", [DEPTH, 128, KC, 528])
    wgtm_d = din("wgtm", [DEPTH, 128, KC, 512])
    wm_d = din("wm", [DEPTH, 32, 128, KC, 128])
    woa_d = din("woa", [DEPTH, 16, 128, 8, 128])
    wob_d = din("wob", [DEPTH, 16, 128, 8, 128])
    wout_d = din("wout", [DEPTH, 8, 128, KC, 256])
    wa2_d = din("wa2", [DEPTH, 16, 256])
    ba_d = din("ba", [DEPTH, 128, 2])
    bfb_d = din("bfb", [DEPTH, 128, 2])
    gain_d = din("gain", [DEPTH, 128, 256])
    pre_d = din("pre", [DEPTH, 128, D])
    post_d = din("post", [DEPTH, 128, D])
    ck_d = din("ck", [DEPTH, 4, 4096, 256])
    cv_d = din("cv", [DEPTH, 4, 4096, 256])
    clf_d = din("clf", [DEPTH, 4, 128, 32, 2])
    sg_d = din("sg", [DEPTH, 4, 128, 2, 256])
    idx_d = din("idx", [128, TL * 4], I32)

    y_d = dout("y_loc", [TL * 128, D])
    ko_d = dout("k_out", [DEPTH, NTOK, 256])
    vo_d = dout("v_out", [DEPTH, NTOK, 256])
    lfo_d = dout("lf_out", [DEPTH, 128, NT, 2])
    gp_d = dout("gla_p", [DEPTH, 128, 2, 256])
    gs_d = dout("gla_s", [DEPTH, 4, 128, 2, 256])

    hsrc = nc.dram_tensor("hsrc", [TL * 128, D], BF16).ap()
    hag = nc.dram_tensor("hag", [9 * 1024, D], BF16).ap()
    osrcA = nc.dram_tensor("osrcA", [NTOK, 256], BF16).ap()
    osrcB = nc.dram_tensor("osrcB", [NTOK, 256], BF16).ap()
    oagA = nc.dram_tensor("oagA", [5 * 8192, 256], BF16).ap()
    oagB_d = nc.dram_tensor("oagB", [5 * 8192, 256], BF16).ap()
    yscr = nc.dram_tensor("yscr", [TL * 128, D], F32).ap()
    wm_b = nc.dram_tensor("wm_b", [DEPTH, 32, 128, KC * 128], BF16).ap()
    wffm_b = nc.dram_tensor("wffm_b", [DEPTH, 128, KC * 256], BF16).ap()
    wftm_b = nc.dram_tensor("wftm_b", [DEPTH, 128, KC * 770], BF16).ap()
    wgfm_b = nc.dram_tensor("wgfm_b", [DEPTH, 128, KC * 528], BF16).ap()
    wgtm_b = nc.dram_tensor("wgtm_b", [DEPTH, 128, KC * 512], BF16).ap()
    woa_b = nc.dram_tensor("woa_b", [DEPTH, 16, 128, 8 * 128], BF16).ap()
    wob_b = nc.dram_tensor("wob_b", [DEPTH, 16, 128, 8 * 128], BF16).ap()
    wout_b = nc.dram_tensor("wout_b", [DEPTH, 8, 128, KC * 256], BF16).ap()

    with ExitStack() as st:
        P = Prog(nc, st)
        A = Arena(nc)
        ps = [nc.alloc_psum_tensor("psb%d" % i, [128, 512], F32) for i in range(8)]
        psB = [P.buf("ps%d" % i) for i in range(8)]

        ones_f = A.alloc("ones_f", [128, 128], F32)
        ident_f = A.alloc("ident_f", [128, 128], F32)
        tri_f = A.alloc("tri_f", [128, 128], F32)
        tri_b = A.alloc("tri_b", [128, 128], BF16)
        resetm = A.alloc("resetm", [128, 512], F32)
        cst = A.alloc("cst", [128, 8], F32)
        idx_t = A.alloc("idx_t", [128, TL * 4], I32)
        Bc = P.buf("consts")
        ID = ident_f[:]

        P.op("pool", MS(ones_f[:], 1.0), writes=[Bc])
        P.op("pool", MS(ident_f[:], 0.0), writes=[Bc])
        P.op("pool", lambda e: e.affine_select(out=ident_f[:], in_=ones_f[:], pattern=[[-1, 128]],
                                               compare_op=ALU.is_equal, fill=0.0, base=0, channel_multiplier=1),
             writes=[Bc])
        P.op("pool", MS(tri_f[:], 0.0), writes=[Bc])
        P.op("pool", lambda e: e.affine_select(out=tri_f[:], in_=ones_f[:], pattern=[[1, 128]],
                                               compare_op=ALU.is_ge, fill=0.0, base=0, channel_multiplier=-1),
             writes=[Bc])
        P.op("pool", TC(tri_b[:], tri_f[:]), writes=[Bc])
        P.op("pool", MS(resetm[:], 1.0), writes=[Bc])
        for q in range(4):
            P.op("pool", MS(resetm[:, q * 128:q * 128 + 1], 0.0), writes=[Bc])
        P.op("pool", MS(cst[:, 0:1], 1e-6), writes=[Bc])
        P.op("pool", MS(cst[:, 1:2], 1.0), writes=[Bc])
        P.op("pool", MS(cst[:, 2:3], -math.log(16.0)), writes=[Bc])
        P.op("pool", MS(cst[:, 3:4], 0.0), writes=[Bc])
        P.op("pool", MS(cst[:, 4:5], math.log(0.5)), writes=[Bc])
        P.op("sp", DMA(idx_t[:], idx_d), writes=[Bc], dma=Bc)
        EPS, ONE, NL16, ZERO, LNH = cst[:, 0:1], cst[:, 1:2], cst[:, 2:3], cst[:, 3:4], cst[:, 4:5]

        hsrcB = P.bufs(TL, "hsrc")
        hagB = P.bufs(9, "hag")
        oagAB = P.bufs(5, "oagA")
        oagBB = P.bufs(5, "oagB")
        base_mark = A.mark()
        psrr = RR(range(8))

        def rstd_of(ssq, n, B, lnbias=None):
            P.op("act", ACT(ssq[:, 1:2], ssq[:, 0:1], AF.Ln, EPS, 1.0 / n), reads=[B, Bc], writes=[B])
            P.op("act", ACT(ssq[:, 2:3], ssq[:, 1:2], AF.Exp, ZERO if lnbias is None else lnbias, -0.5), reads=[B, Bc], writes=[B])

        def emit_h(y_t, yB, pre_t, preB, t, tmp):
            junk, ssq, hf, hTst, hTstB, tB = tmp
            P.op("dve", STT(junk[:], y_t[:], 1.0, y_t[:], ALU.mult, ALU.mult, accum=ssq[:, 0:1]), reads=[yB], writes=[tB])
            rstd_of(ssq, D, tB)
            P.op("dve", STT(hf[:], y_t[:], ssq[:, 2:3], pre_t[:], ALU.mult, ALU.mult), reads=[yB, tB, preB], writes=[tB])
            for q in range(4):
                bk = psrr.next()
                for j in range(4):
                    c = q * 4 + j
                    P.op("pe", TR(ps[bk][:, j * 128:(j + 1) * 128], hf[:, c * 128:(c + 1) * 128], ID),
                         reads=[tB, Bc], writes=[psB[bk]])
                P.op("act", CP(hTst[:, q * 4:(q + 1) * 4, :], split(ps[bk][:, :], 4)), reads=[psB[bk]], writes=[hTstB])
            return P.op("pool", DMA(hsrc[t * 128:(t + 1) * 128, :], flat(hTst[:, :, :])), reads=[hTstB], writes=[hsrcB[t]], dma=hTstB)

        def ag_chunk(src, dst, dstB, q, rows_total, rows_chunk, store_ops):
            r0 = q * rows_chunk
            n = min(rows_chunk, rows_total - r0)
            P.op("pool", (lambda i_, o_: (lambda e: e.collective_compute(
                "AllGather", ALU.bypass, replica_groups=GROUPS, ins=[i_], outs=[o_])))(
                    src[r0:r0 + n, :], dst[q * 4 * rows_chunk:q * 4 * rows_chunk + 4 * n, :]),
                writes=[dstB[q]], cc=True, extra_deps=store_ops)

        def cumsum_tiles(bkrr, lf_ap, n, carry_ap, out_ap, tmp, tB, lfB, outB, carry_out_ap=None):
            bk = bkrr.next()
            sb_, incl = tmp
            P.op("pe", MM(ps[bk][:, 0:2 * n], tri_f[:], flat(lf_ap)), reads=[lfB, Bc], writes=[psB[bk]])
            P.op("pe", MM(ps[bk][:, 2 * n:4 * n], ones_f[:], flat(lf_ap), start=False, stop=True, skip=True),
                 reads=[lfB, Bc], writes=[psB[bk]])
            P.op("act", CP(sb_[:, 0:4 * n], ps[bk][:, 0:4 * n]), reads=[psB[bk]], writes=[tB])
            tot = sb_[:, 2 * n:4 * n].rearrange("p (a b) -> p a b", b=2)
            loc = sb_[:, 0:2 * n].rearrange("p (a b) -> p a b", b=2)
            inc3 = incl[:, 0:2 * n].rearrange("p (a b) -> p a b", b=2)
            for hh in range(2):
                P.op("dve", SCAN(inc3[:, :, hh], ones_f[:, 0:n], tot[:, :, hh], carry_ap[:, hh:hh + 1]),
                     reads=[tB, Bc, outB], writes=[tB])
            P.op("dve", TT(loc, loc, tot, ALU.subtract), reads=[tB], writes=[tB])
            P.op("dve", TT(out_ap, loc, inc3, ALU.add), reads=[tB], writes=[outB])
            if carry_out_ap is not None:
                P.op("dve", TC(carry_out_ap, inc3[:, n - 1, :]), reads=[tB], writes=[outB])

        def load_hT_block(i, hTb_t, hTbB):
            q, tl = i // 2, i % 2
            nq = 256 if q < 8 else 128
            for tt in range(4):
                row = q * 1024 + tt * nq + tl * 128
                P.op("sp", DMA(hTb_t[:, :, tt * 128:(tt + 1) * 128], split(hag[row:row + 128, :], KC)),
                     reads=[hagB[q]], writes=[hTbB], dma=hTbB)

        hwB = {}
        hw_ops = {}
        hw_jobs = {}
        for l_ in range(DEPTH):
            for ps_, mats in (("F", ((wffm_b, wffm_d, 256, 8), (wftm_b, wftm_d, 770, 2))),
                              ("G", ((wgfm_b, wgfm_d, 528, 3), (wgtm_b, wgtm_d, 512, 4)))):
                hwB[(l_, ps_)] = P.buf("hwcast%d%s" % (l_, ps_))
                hw_ops[(l_, ps_)] = []
                jl = []
                for (dst_, src_, cols, kk) in mats:
                    for k0 in range(0, KC, kk):
                        k1 = min(KC, k0 + kk)
                        jl.append((dst_[l_, :, k0 * cols:k1 * cols], flat(src_[l_, :, k0:k1, :])))
                hw_jobs[(l_, ps_)] = jl

        def emit_hw_casts(key, n=None):
            jl = hw_jobs[key]
            for _ in range(len(jl) if n is None else min(n, len(jl))):
                o_, i_ = jl.pop(0)
                hw_ops[key].append(P.op("pool", DMA(o_, i_), dma=hwB[key], nobarrier=True))

        emit_hw_casts((0, "F"))

        def load_w_bf(dst_t, src_ap, B, key):
            emit_hw_casts(key)
            P.op("sp", DMA(flat(dst_t[:, :, :]), src_ap), writes=[B], dma=B, extra_deps=hw_ops[key])

        def silu2_from_psum(out_ap, ps_ap, et_ap, etB_, psBuf, outB):
            P.op("act", ACT(et_ap, ps_ap, AF.Tanh, ZERO, 0.5), reads=[psBuf, Bc], writes=[etB_])
            P.op("dve", STT(out_ap, et_ap, 1.0, ps_ap, ALU.add, ALU.mult), reads=[psBuf, etB_], writes=[outB])

        wcastB = [P.buf("wcast%d" % l_) for l_ in range(DEPTH)]
        wcast_ops = [[] for _ in range(DEPTH)]

        def cast_jobs(l):
            jobs = []
            for c in range(32):
                jobs.append((wm_b[l, c], flat(wm_d[l, c])))
            for c in range(16):
                jobs.append((woa_b[l, c], flat(woa_d[l, c])))
                jobs.append((wob_b[l, c], flat(wob_d[l, c])))
            for blk in range(8):
                for hf_ in range(2):
                    jobs.append((wout_b[l, blk, :, hf_ * 2048:(hf_ + 1) * 2048], flat(wout_d[l, blk, :, hf_ * 8:(hf_ + 1) * 8, :])))
            return jobs

        def emit_casts(l, jobs, n):
            for _ in range(min(n, len(jobs))):
                o_, i_ = jobs.pop(0)
                wcast_ops[l].append(P.op("pool", DMA(o_, i_), dma=wcastB[l], nobarrier=True))

        def prologue():
            A.reset(base_mark)
            pre_t = A.alloc("pre_t", [128, D], F32)
            preB = P.buf("pre")
            P.op("sp", DMA(pre_t[:], pre_d[0]), writes=[preB], dma=preB)
            xts = [A.alloc("xt", [128, D], F32) for _ in range(2)]
            xBs = P.bufs(2, "xt")
            tmps = []
            for _ in range(2):
                tmps.append((A.alloc("junk", [128, D], BF16), A.alloc("ssq", [128, 4], F32), A.alloc("hf", [128, D], F32),
                             A.alloc("hTst", [128, KC, 128], BF16), P.buf("hTst"), P.buf("htmp")))
            sts = []
            P.op("sp", DMA(xts[0][:], x_d[0:128, :]), writes=[xBs[0]], dma=xBs[0])
            for t in range(TL):
                s = t % 2
                if t + 1 < TL:
                    P.op("sp", DMA(xts[1 - s][:], x_d[(t + 1) * 128:(t + 2) * 128, :]), writes=[xBs[1 - s]], dma=xBs[1 - s])
                sts.append(emit_h(xts[s], xBs[s], pre_t, preB, t, tmps[s]))
                if t % 2 == 1 or t == TL - 1:
                    ag_chunk(hsrc, hag, hagB, t // 2, TL * 128, 256, sts)
                    sts = []
            P.barrier()

        def pass_f(l):
            A.reset(base_mark)
            WFfm = A.alloc("WFfm", [128, KC, 256], BF16)
            WFtm = A.alloc("WFtm", [128, KC, 770], BF16)
            WB_ = P.buf("WF")
            load_w_bf(WFfm, wffm_b[l], WB_, (l, "F"))
            load_w_bf(WFtm, wftm_b[l], WB_, (l, "F"))
            bfb = A.alloc("bfb", [128, 2], F32)
            P.op("sp", DMA(bfb[:], bfb_d[l]), writes=[WB_], dma=WB_)
            kT = A.alloc("kT", [128, 2, NTOK], BF16)
            Vg = A.alloc("Vg", [128, NT, 2, 129], BF16)
            kTB = P.bufs(NT, "kT")
            VB = P.bufs(NT, "V")
            P.op("pool", MS(flat(Vg[:, :, :, :]), 2.0), writes=VB)
            hTb = [A.alloc("hTb", [128, KC, 512], BF16) for _ in range(2)]
            hTbB = P.bufs(2, "hTb")
            qT = [A.alloc("qT", [128, 2, 512], BF16) for _ in range(2)]
            qTB = P.bufs(2, "qT")
            kvf = [A.alloc("kvf", [128, 512], F32) for _ in range(2)]
            kvfB = P.bufs(2, "kvf")
            et = [A.alloc("et", [128, 256], F32) for _ in range(2)]
            etB = P.bufs(2, "et")
            fgs = [A.alloc("fgs", [128, 4, 256], BF16) for _ in range(2)]
            fgsB = P.bufs(2, "fgs")
            lfpre = [A.alloc("lfpre", [128, 4, 2], F32) for _ in range(2)]
            lfpB = P.bufs(2, "lfp")
            LF = A.alloc("LF", [128, NT, 2], F32)
            LFB = P.buf("LF")
            C = A.alloc("C", [128, NT, 2], F32)
            CB = P.buf("C")
            carry = A.alloc("carry", [128, 2], F32)
            cref = A.alloc("cref", [128, NBLK, 2], F32)
            cs_tmp = (A.alloc("cs_sb", [128, 128], F32), A.alloc("cs_incl", [128, 64], F32))
            csB = P.buf("cstmp")
            btab = [A.alloc("btab", [128, 2, NT], F32) for _ in range(2)]
            btB = P.bufs(2, "btab")
            PT = [A.alloc("PT", [128, 512], BF16) for _ in range(4)]
            PTB = P.bufs(4, "PT")
            ptrr = RR(range(4))
            ost = [A.alloc("ost", [128, 256], BF16) for _ in range(8)]
            ostB = P.bufs(8, "ost")
            rec = [A.alloc("rec", [128, 4], F32) for _ in range(8)]
            ckf = [A.alloc("ckf", [128, 256], F32) for _ in range(3)]
            ckfB = P.bufs(3, "ckf")
            ckT = [A.alloc("ckT", [128, 2, 128], BF16) for _ in range(3)]
            ckTB = P.bufs(3, "ckT")
            cV = [A.alloc("cV", [128, 2, 129], BF16) for _ in range(3)]
            cVB = P.bufs(3, "cV")
            for s3 in range(3):
                P.op("pool", MS(flat(cV[s3][:, :, :]), 2.0), writes=[cVB[s3]])
            clf = A.alloc("clf", [128, 32, 2], F32)
            clfB = P.buf("clf")
            cC = A.alloc("cC", [128, 32, 2], F32)
            cCB = P.buf("cC")
            ccar = A.alloc("ccar", [128, 2], F32)
            Cn = A.alloc("Cn", [128, 1, 2], F32)
            btS = A.alloc("btS", [128, 2, 33], F32)
            btSB = P.buf("btS")
            PTs = [A.alloc("PTs", [128, 64], BF16) for _ in range(3)]
            PTsB = P.bufs(3, "PTs")

            P.op("pool", MS(carry[:], 0.0), writes=[CB])
            pj = RR([0, 1, 2, 3])
            pjs = RR([0])
            stb = RR([1, 2, 3])
            OB = [(4, 5), (6, 7)]

            def sample_attention(s):
                for k4 in range(4):
                    j = 64 + k4
                    P.op("sp", DMA(clf[:], clf_d[l, k4]), writes=[clfB], dma=clfB)
                    P.op("pool", MS(ccar[:], 0.0), writes=[cCB])
                    cumsum_tiles(pjs, clf[:, :, :], 32, ccar, cC[:, :, :], cs_tmp, csB, clfB, cCB, carry_out_ap=ccar[:, :])
                    cumsum_tiles(pjs, LF[:, j:j + 1, :], 1, ccar, Cn[:, :, :], cs_tmp, csB, LFB, cCB)
                    for h in range(2):
                        P.op("dve", TS(btS[:, h, 0:32], cC[:, :, h], -1.0, ccar[:, h:h + 1], ALU.mult, ALU.add),
                             reads=[cCB], writes=[btSB])
                        P.op("dve", TS(btS[:, h, 32:33], Cn[:, :, h], -1.0, ccar[:, h:h + 1], ALU.mult, ALU.add),
                             reads=[cCB], writes=[btSB])
                    bO = OB[k4 % 2][0]
                    qsl = slice(k4 * 128, k4 * 128 + 32)
                    for jt in range(33):
                        r3 = (k4 * 33 + jt) % 3
                        sk = stb.next()
                        if jt < 32:
                            P.op("sp", DMA(ckf[r3][:], ck_d[l, k4, jt * 128:(jt + 1) * 128, :]), writes=[ckfB[r3]], dma=ckfB[r3])
                            P.op("pool", DMA(cV[r3][:, :, 0:128], split(cv_d[l, k4, jt * 128:(jt + 1) * 128, :], 2)),
                                 writes=[cVB[r3]], dma=cVB[r3])
                            bk2 = pjs.next()
                            for h in range(2):
                                P.op("pe", TR(ps[bk2][:, h * 128:(h + 1) * 128], ckf[r3][:, h * 128:(h + 1) * 128], ID),
                                     reads=[ckfB[r3], Bc], writes=[psB[bk2]])
                            P.op("dve", TC(ckT[r3][:, :, :], split(ps[bk2][:, 0:256], 2)), reads=[psB[bk2]], writes=[ckTB[r3]])
                            for h in range(2):
                                P.op("pe", MM(ps[sk][:, h * 32:(h + 1) * 32], ckT[r3][:, h, :], qT[s][:, h, qsl],
                                              start=(h == 0), stop=True, skip=True),
                                     reads=[ckTB[r3], qTB[s]], writes=[psB[sk]])
                            for h in range(2):
                                P.op("act", ACT(PTs[r3][:, h * 32:(h + 1) * 32], ps[sk][:, h * 32:(h + 1) * 32], AF.Exp,
                                                btS[:, h, jt:jt + 1], SCALE_F), reads=[psB[sk], btSB], writes=[PTsB[r3]])
                            for h in range(2):
                                P.op("pe", MM(ps[bO][0:32, h * 129:(h + 1) * 129], PTs[r3][:, h * 32:(h + 1) * 32], cV[r3][:, h, :],
                                              start=(jt == 0 and h == 0), stop=False, skip=True),
                                     reads=[PTsB[r3], cVB[r3]], writes=[psB[bO]])
                        else:
                            for h in range(2):
                                P.op("pe", MM(ps[sk][0:32, h * 32:(h + 1) * 32], kT[:, h, j * 128:j * 128 + 32], qT[s][:, h, qsl],
                                              start=(h == 0), stop=True, skip=True),
                                     reads=[kTB[j], qTB[s]], writes=[psB[sk]])
                            for h in range(2):
                                P.op("act", ACT(PTs[r3][0:32, h * 32:(h + 1) * 32], ps[sk][0:32, h * 32:(h + 1) * 32], AF.Exp,
                                                btS[0:32, h, 32:33], SCALE_F), reads=[psB[sk], btSB], writes=[PTsB[r3]])
                                P.op("pool", TT(PTs[r3][0:32, h * 32:(h + 1) * 32], PTs[r3][0:32, h * 32:(h + 1) * 32],
                                                tri_b[0:32, 0:32], ALU.mult), reads=[PTsB[r3], Bc], writes=[PTsB[r3]])
                            for h in range(2):
                                P.op("pe", MM(ps[bO][0:32, h * 129:(h + 1) * 129], PTs[r3][0:32, h * 32:(h + 1) * 32],
                                              Vg[0:32, j, h, :], start=False, stop=True, skip=True),
                                     reads=[PTsB[r3], VB[j]], writes=[psB[bO]])
                    oi = j % 8
                    P.op("pool", MS(ost[oi][:], 0.0), writes=[ostB[oi]])
                    for h in range(2):
                        P.op("dve", RCP(rec[oi][0:32, h:h + 1], ps[bO][0:32, h * 129 + 128:h * 129 + 129]),
                             reads=[psB[bO]], writes=[ostB[oi]])
                        P.op("dve", STT(ost[oi][0:32, h * 128:(h + 1) * 128], ps[bO][0:32, h * 129:h * 129 + 128],
                                        rec[oi][0:32, h:h + 1], fgs[s][0:32, k4, h * 128:(h + 1) * 128], ALU.mult, ALU.mult),
                             reads=[psB[bO], fgsB[s], ostB[oi]], writes=[ostB[oi]])
                    oa_st.append(P.op("sp", DMA(osrcA[j * 128:(j + 1) * 128, :], ost[oi][:]), reads=[ostB[oi]], dma=ostB[oi]))

            def block(i):
                s = i % 2
                if i + 1 < NBLK:
                    load_hT_block(i + 1, hTb[1 - s], hTbB[1 - s])
                hb, hbB = hTb[s], hTbB[s]
                for h in range(2):
                    bk = pj.next()
                    for k in range(KC):
                        P.op("pe", MM(ps[bk][:, :], WFfm[:, k, h * 128:(h + 1) * 128], hb[:, k, :], start=(k == 0), stop=(k == KC - 1)),
                             reads=[WB_, hbB], writes=[psB[bk]])
                    P.op("dve", TC(qT[s][:, h, :], ps[bk][:, :]), reads=[psB[bk]], writes=[qTB[s]])
                for tt in range(4):
                    j = 4 * i + tt
                    ks = j % 2
                    tsl = slice(tt * 128, (tt + 1) * 128)
                    bk = pj.next()
                    for k in range(KC):
                        P.op("pe", MM(ps[bk][:, :], hb[:, k, tsl], WFtm[:, k, 0:512], start=(k == 0), stop=(k == KC - 1)),
                             reads=[WB_, hbB], writes=[psB[bk]])
                    P.op("dve", TC(kvf[ks][:], ps[bk][:, :]), reads=[psB[bk]], writes=[kvfB[ks]])
                    P.op("sp", DMA(ko_d[l, j * 128:(j + 1) * 128, :], kvf[ks][:, 0:256]), reads=[kvfB[ks]], dma=kvfB[ks])
                    P.op("sp", DMA(vo_d[l, j * 128:(j + 1) * 128, :], kvf[ks][:, 256:512]), reads=[kvfB[ks]], dma=kvfB[ks])
                    P.op("dve", TC(Vg[:, j, :, 0:128], split(kvf[ks][:, 256:512], 2)), reads=[kvfB[ks]], writes=[VB[j]])
                    bk = pj.next()
                    for k in range(KC):
                        P.op("pe", MM(ps[bk][:, 0:258], hb[:, k, tsl], WFtm[:, k, 512:770], start=(k == 0), stop=(k == KC - 1)),
                             reads=[WB_, hbB], writes=[psB[bk]])
                    bk2 = pj.next()
                    for h in range(2):
                        P.op("pe", TR(ps[bk2][:, h * 128:(h + 1) * 128], kvf[ks][:, h * 128:(h + 1) * 128], ID),
                             reads=[kvfB[ks], Bc], writes=[psB[bk2]])
                    P.op("dve", TC(kT[:, :, j * 128:(j + 1) * 128], split(ps[bk2][:, 0:256], 2)), reads=[psB[bk2]], writes=[kTB[j]])
                    silu2_from_psum(fgs[s][:, tt, :], ps[bk][:, 0:256], et[ks][:], etB[ks], psB[bk], fgsB[s])
                    P.op("dve", TT(lfpre[s][:, tt, :], ps[bk][:, 256:258], bfb[:], ALU.add), reads=[psB[bk], WB_], writes=[lfpB[s]])
                lfp2 = flat(lfpre[s][:, :, :])
                P.op("act", ACT(lfp2, lfp2, AF.Exp, ZERO, -1.0), reads=[lfpB[s], Bc], writes=[lfpB[s]])
                P.op("act", ACT(lfp2, lfp2, AF.Ln, ONE, 1.0), reads=[lfpB[s], Bc], writes=[lfpB[s]])
                P.op("dve", TS(flat(LF[:, 4 * i:4 * i + 4, :]), lfp2, -1.0, None, ALU.mult), reads=[lfpB[s]], writes=[LFB])
                if i == 16:
                    sample_attention(s)
                    return
                cumsum_tiles(pj, LF[:, 4 * i:4 * i + 4, :], 4, carry, C[:, 4 * i:4 * i + 4, :], cs_tmp, csB, LFB, CB,
                             carry_out_ap=carry[:, :])
                P.op("dve", TC(cref[:, i, :], cs_tmp[1][:, 0:8].rearrange("p (a b) -> p a b", b=2)[:, 1, :]), reads=[csB], writes=[CB])
                nkt = 4 * i + 4
                for h in range(2):
                    P.op("dve", TS(btab[s][:, h, 0:nkt], C[:, 0:nkt, h], -1.0, cref[:, i, h:h + 1], ALU.mult, ALU.add),
                         reads=[CB], writes=[btB[s]])
                for h in range(2):
                    bA, bB = OB[h]

                    def score(kt):
                        jj = kt - 4 * i
                        c0 = 0 if jj < 0 else 128 * jj
                        sk = stb.next()
                        P.op("pe", MM(ps[sk][:, c0:512], kT[:, h, kt * 128:(kt + 1) * 128], qT[s][:, h, c0:512]),
                             reads=[kTB[kt], qTB[s]], writes=[psB[sk]])
                        return sk

                    skq = [score(0)]
                    if nkt > 1:
                        skq.append(score(1))
                    for kt in range(nkt):
                        if kt + 2 < nkt:
                            skq.append(score(kt + 2))
                        sk = skq.pop(0)
                        jj = kt - 4 * i
                        c0 = 0 if jj < 0 else 128 * jj
                        pi = ptrr.next()
                        P.op("act", ACT(PT[pi][:, c0:512], ps[sk][:, c0:512], AF.Exp, btab[s][:, h, kt:kt + 1], SCALE_F),
                             reads=[psB[sk], btB[s]], writes=[PTB[pi]])
                        if jj >= 0:
                            P.op("dve", TT(PT[pi][:, c0:c0 + 128], PT[pi][:, c0:c0 + 128], tri_b[:], ALU.mult),
                                 reads=[PTB[pi], Bc], writes=[PTB[pi]])
                        for sub in range(max(jj, 0), 4):
                            bo = bA if sub < 2 else bB
                            oc = (sub % 2) * 129
                            P.op("pe", MM(ps[bo][:, oc:oc + 129], PT[pi][:, sub * 128:(sub + 1) * 128], Vg[:, kt, h, :],
                                          start=(kt == 0 and sub % 2 == 0), stop=(kt == 4 * i + sub), skip=True),
                                 reads=[PTB[pi], VB[kt]], writes=[psB[bo]])
                    for sub in range(4):
                        j = 4 * i + sub
                        bo = bA if sub < 2 else bB
                        oc = (sub % 2) * 129
                        oi = j % 8
                        P.op("dve", RCP(rec[oi][:, h:h + 1], ps[bo][:, oc + 128:oc + 129]), reads=[psB[bo]], writes=[ostB[oi]])
                        P.op("dve", STT(ost[oi][:, h * 128:(h + 1) * 128], ps[bo][:, oc:oc + 128], rec[oi][:, h:h + 1],
                                        fgs[s][:, sub, h * 128:(h + 1) * 128], ALU.mult, ALU.mult),
                             reads=[psB[bo], fgsB[s], ostB[oi]], writes=[ostB[oi]])
                for sub in range(4):
                    j = 4 * i + sub
                    oi = j % 8
                    oa_st.append(P.op("sp", DMA(osrcA[j * 128:(j + 1) * 128, :], ost[oi][:]), reads=[ostB[oi]], dma=ostB[oi]))

            oa_st = []
            jobs = cast_jobs(l)
            load_hT_block(0, hTb[0], hTbB[0])
            for i in range(NBLK):
                block(i)
                emit_hw_casts((l, "G"), 2)
                emit_casts(l, jobs, 5)
                if i >= 4 and i % 4 == 0:
                    q = i // 4 - 1
                    ag_chunk(osrcA, oagA, oagAB, q, NTOK, 2048, oa_st[0:16])
                    del oa_st[0:16]
            ag_chunk(osrcA, oagA, oagAB, 4, NTOK, 2048, list(oa_st))
            emit_casts(l, jobs, len(jobs))
            P.op("sp", DMA(lfo_d[l], LF[:, :, :]), reads=[LFB], dma=LFB)
            P.barrier()

        def pass_g(l):
            A.reset(base_mark)
            WGfm = A.alloc("WGfm", [128, KC, 528], BF16)
            WGtm = A.alloc("WGtm", [128, KC, 512], BF16)
            WB_ = P.buf("WG")
            load_w_bf(WGfm, wgfm_b[l], WB_, (l, "G"))
            load_w_bf(WGtm, wgtm_b[l], WB_, (l, "G"))
            wa2 = A.alloc("wa2", [16, 256], BF16)
            wa2B = P.buf("wa2")
            P.op("pool", DMA(wa2[:], wa2_d[l]), writes=[wa2B], dma=wa2B)
            nba = A.alloc("nba", [128, 2], F32)
            gain = A.alloc("gain", [128, 256], F32)
            P.op("sp", DMA(nba[:], ba_d[l]), writes=[WB_], dma=WB_)
            P.op("sp", DMA(gain[:], gain_d[l]), writes=[WB_], dma=WB_)
            P.op("pool", TS(nba[:], nba[:], -1.0, None, ALU.mult), reads=[WB_], writes=[WB_])
            hTb = [A.alloc("hTb", [128, KC, 512], BF16) for _ in range(2)]
            hTbB = P.bufs(2, "hTb")
            glrT = A.alloc("glrT", [16, 512], BF16)
            glB = P.buf("glrT")
            et2 = A.alloc("et2", [128, 512], F32)
            et2B = P.buf("et2")
            sp_ = A.alloc("sp", [128, 2, 512], F32)
            spB = P.buf("sp")
            bpos = A.alloc("bpos", [128, 2, 512], F32)
            bpB = P.buf("bpos")
            ebT = A.alloc("ebT", [128, 2, 512], BF16)
            enbT = A.alloc("enbT", [128, 2, 512], BF16)
            ebB = P.buf("eb")
            ebl = [A.alloc("ebl", [128, 2, 4], F32) for _ in range(2)]
            eblB = P.bufs(2, "ebl")
            qtl = [A.alloc("qtl", [128, 2, 512], BF16) for _ in range(2)]
            qtB = P.bufs(2, "qtl")
            ktf = [A.alloc("ktf", [128, 2, 512], F32) for _ in range(2)]
            ktfB = P.bufs(2, "ktf")
            ktl = [A.alloc("ktl", [128, 2, 512], BF16) for _ in range(2)]
            ktlB = P.bufs(2, "ktl")
            gvb = [A.alloc("gvb", [128, 256], BF16) for _ in range(8)]
            gvB = P.bufs(8, "gvb")
            ggs = [A.alloc("ggs", [128, 256], BF16) for _ in range(8)]
            ggB = P.bufs(8, "ggs")
            etg = [A.alloc("etg", [128, 256], F32) for _ in range(2)]
            etgB = P.bufs(2, "etg")
            kttm = [A.alloc("kttm", [128, 256], BF16) for _ in range(2)]
            kttmB = P.bufs(2, "kttm")
            AT = [A.alloc("AT", [128, 128], BF16) for _ in range(2)]
            ATB = P.bufs(2, "AT")
            osb = [A.alloc("osb", [128, 256], F32) for _ in range(2)]
            osbB = P.bufs(2, "osb")
            junk = A.alloc("junkg", [128, 256], BF16)
            ssq = [A.alloc("ssqg", [128, 4], F32) for _ in range(2)]
            t1 = [A.alloc("t1g", [128, 256], F32) for _ in range(2)]
            obst = [A.alloc("obst", [128, 256], BF16) for _ in range(4)]
            obB = P.bufs(4, "obst")
            S = A.alloc("S", [128, 2, 256], F32)
            SB_ = P.buf("S")
            Sb = A.alloc("Sb", [128, 2, 256], BF16)
            SbB = P.buf("Sb")
            tmpS = A.alloc("tmpS", [128, 2, 256], F32)
            tSB = P.buf("tmpS")
            P.op("pool", MS(flat(S[:, :, :]), 0.0), writes=[SB_])
            P.op("pool", MS(flat(Sb[:, :, :]), 0.0), writes=[SbB])
            ostores = []

            def proj(i):
                s = i % 2
                hb, hbB = hTb[s], hTbB[s]
                bk = psrr.next()
                for k in range(KC):
                    P.op("pe", MM(ps[bk][0:16, :], WGfm[:, k, 512:528], hb[:, k, :], start=(k == 0), stop=(k == KC - 1)),
                         reads=[WB_, hbB], writes=[psB[bk]])
                P.op("act", CP(glrT[:, :], ps[bk][0:16, :]), reads=[psB[bk]], writes=[glB])
                for c in range(2):
                    bk = psrr.next()
                    P.op("pe", MM(ps[bk][:, :], wa2[:, c * 128:(c + 1) * 128], glrT[:, :]), reads=[wa2B, glB], writes=[psB[bk]])
                    P.op("act", ACT(et2[:], ps[bk][:, :], AF.Exp, nba[:, c:c + 1], -1.0), reads=[psB[bk], WB_], writes=[et2B])
                    P.op("act", ACT(sp_[:, c, :], et2[:], AF.Ln, ONE, 1.0), reads=[et2B, Bc], writes=[spB])
                    P.op("dve", SCAN(bpos[:, c, :], resetm[:], sp_[:, c, :], 0.0), reads=[spB, Bc], writes=[bpB])
                bp2 = flat(bpos[:, :, :])
                P.op("act", ACT(flat(ebT[:, :, :]), bp2, AF.Exp, NL16, -1.0 / 16.0), reads=[bpB, Bc], writes=[ebB])
                P.op("act", ACT(flat(enbT[:, :, :]), bp2, AF.Exp, ZERO, 1.0 / 16.0), reads=[bpB, Bc], writes=[ebB])
                lastc = 31 if i == 16 else 127
                for c in range(2):
                    P.op("act", ACT(ebl[s][:, c, :], split(bpos[:, c, :], 4)[:, :, lastc], AF.Exp, ZERO, -1.0 / 16.0),
                         reads=[bpB, Bc], writes=[eblB[s]])
                for c in range(2):
                    bk = psrr.next()
                    for k in range(KC):
                        P.op("pe", MM(ps[bk][:, :], WGfm[:, k, c * 128:(c + 1) * 128], hb[:, k, :], start=(k == 0), stop=(k == KC - 1)),
                             reads=[WB_, hbB], writes=[psB[bk]])
                    P.op("dve", TT(qtl[s][:, c, :], ps[bk][:, :], ebT[:, c, :], ALU.mult), reads=[psB[bk], ebB], writes=[qtB[s]])
                for c in range(2):
                    bk = psrr.next()
                    for k in range(KC):
                        P.op("pe", MM(ps[bk][:, :], WGfm[:, k, 256 + c * 128:256 + (c + 1) * 128], hb[:, k, :],
                                      start=(k == 0), stop=(k == KC - 1)), reads=[WB_, hbB], writes=[psB[bk]])
                    P.op("dve", TT(ktf[s][:, c, :], ps[bk][:, :], enbT[:, c, :], ALU.mult), reads=[psB[bk], ebB], writes=[ktfB[s]])
                P.op("act", CP(flat(ktl[s][:, :, :]), flat(ktf[s][:, :, :])), reads=[ktfB[s]], writes=[ktlB[s]])
                for tt in range(4):
                    j = 4 * i + tt
                    tsl = slice(tt * 128, (tt + 1) * 128)
                    g8, g2 = j % 8, j % 2
                    bk = psrr.next()
                    for k in range(KC):
                        P.op("pe", MM(ps[bk][:, :], hb[:, k, tsl], WGtm[:, k, :], start=(k == 0), stop=(k == KC - 1)),
                             reads=[WB_, hbB], writes=[psB[bk]])
                    P.op("act", CP(gvb[g8][:], ps[bk][:, 0:256]), reads=[psB[bk]], writes=[gvB[g8]])
                    silu2_from_psum(ggs[g8][:], ps[bk][:, 256:512], etg[g2][:], etgB[g2], psB[bk], ggB[g8])

            def recur(i):
                s = i % 2
                for tt in range(4):
                    j = 4 * i + tt
                    tsl = slice(tt * 128, (tt + 1) * 128)
                    g8, g4, g2 = j % 8, j % 4, j % 2
                    if i == 16:
                        P.op("sp", DMA(S[:, :, :], sg_d[l, tt]), writes=[SB_], dma=SB_)
                        P.op("act", CP(flat(Sb[:, :, :]), flat(S[:, :, :])), reads=[SB_], writes=[SbB])
                    bk = psrr.next()
                    for c in range(2):
                        P.op("pe", TR(ps[bk][:, c * 128:(c + 1) * 128], ktf[s][:, c, tsl], ID), reads=[ktfB[s], Bc], writes=[psB[bk]])
                    P.op("act", CP(kttm[g2][:], ps[bk][:, 0:256]), reads=[psB[bk]], writes=[kttmB[g2]])
                    bk = psrr.next()
                    for c in range(2):
                        P.op("pe", MM(ps[bk][:, 0:128], ktl[s][:, c, tsl], qtl[s][:, c, tsl], start=(c == 0), stop=(c == 1)),
                             reads=[ktlB[s], qtB[s]], writes=[psB[bk]])
                    P.op("dve", TT(AT[g2][:], ps[bk][:, 0:128], tri_f[:], ALU.mult), reads=[psB[bk], Bc], writes=[ATB[g2]])
                    bk = psrr.next()
                    P.op("pe", MM(ps[bk][:, 0:256], AT[g2][:], gvb[g8][:], start=True, stop=False),
                         reads=[ATB[g2], gvB[g8]], writes=[psB[bk]])
                    for c in range(2):
                        P.op("pe", MM(ps[bk][:, 0:256], qtl[s][:, c, tsl], Sb[:, c, :], start=False, stop=(c == 1)),
                             reads=[qtB[s], SbB], writes=[psB[bk]])
                    P.op("act", CP(osb[g2][:], ps[bk][:, 0:256]), reads=[psB[bk]], writes=[osbB[g2]])
                    bk = psrr.next()
                    for c in range(2):
                        P.op("pe", MM(ps[bk][:, c * 256:(c + 1) * 256], kttm[g2][:, c * 128:(c + 1) * 128], gvb[g8][:],
                                      start=(c == 0), stop=True, skip=True), reads=[kttmB[g2], gvB[g8]], writes=[psB[bk]])
                    P.op("dve", TT(flat(tmpS[:, :, :]), ps[bk][:, :], flat(S[:, :, :]), ALU.add), reads=[psB[bk], SB_], writes=[tSB])
                    for c in range(2):
                        P.op("dve", TS(S[:, c, :], tmpS[:, c, :], ebl[s][:, c, tt:tt + 1], None, ALU.mult), reads=[tSB, eblB[s]], writes=[SB_])
                    P.op("act", CP(flat(Sb[:, :, :]), flat(S[:, :, :])), reads=[SB_], writes=[SbB])
                    if i == 16:
                        P.op("sp", DMA(gs_d[l, tt], S[:, :, :]), reads=[SB_], dma=SB_)
                    elif j == 63:
                        P.op("sp", DMA(gp_d[l], S[:, :, :]), reads=[SB_], dma=SB_)
                    P.op("dve", STT(junk[:], osb[g2][:], 1.0, osb[g2][:], ALU.mult, ALU.mult, accum=ssq[g2][:, 0:1]),
                         reads=[osbB[g2]], writes=[osbB[g2]])
                    rstd_of(ssq[g2], 256, osbB[g2], lnbias=LNH)
                    P.op("dve", STT(t1[g2][:], osb[g2][:], ssq[g2][:, 2:3], gain[:], ALU.mult, ALU.mult),
                         reads=[osbB[g2], WB_], writes=[osbB[g2]])
                    P.op("pool", TT(obst[g4][:], t1[g2][:], ggs[g8][:], ALU.mult), reads=[osbB[g2], ggB[g8]], writes=[obB[g4]])
                    ostores.append(P.op("sp", DMA(osrcB[j * 128:(j + 1) * 128, :], obst[g4][:]), reads=[obB[g4]], dma=obB[g4]))

            load_hT_block(0, hTb[0], hTbB[0])
            load_hT_block(1, hTb[1], hTbB[1])
            proj(0)
            for i in range(NBLK):
                if i + 1 < NBLK:
                    proj(i + 1)
                    if i + 2 < NBLK:
                        load_hT_block(i + 2, hTb[i % 2], hTbB[i % 2])
                recur(i)
                if l + 1 < DEPTH:
                    emit_hw_casts((l + 1, "F"), 1)
                    emit_hw_casts((l + 1, "G"), 1)
                if i >= 4 and i % 4 == 0:
                    q = i // 4 - 1
                    ag_chunk(osrcB, oagB_d, oagBB, q, NTOK, 2048, ostores[0:16])
                    del ostores[0:16]
            ag_chunk(osrcB, oagB_d, oagBB, 4, NTOK, 2048, list(ostores))
            P.barrier()

        def token_phase(l):
            last = (l == DEPTH - 1)
            A.reset(base_mark)
            hTg = A.alloc("hTg", [128, KC, 512], BF16)
            hTgB = P.buf("hTg")
            oT = A.alloc("oT", [128, KC, 512], BF16)
            oTB = P.buf("oT")
            mT = A.alloc("mT", [128, KC, 512], BF16)
            mTB = P.buf("mT")
            z = A.alloc("z", [128, 4, D], F32)
            zB = P.bufs(4, "z")
            ogf = A.alloc("ogf", [128, D], F32)
            ogB = P.buf("ogf")
            wr = [dict(ma=A.alloc("wma", [128, KC, 128], BF16), mb=A.alloc("wmb", [128, KC, 128], BF16),
                       oa=A.alloc("woa", [128, 8, 128], BF16), ob=A.alloc("wob", [128, 8, 128], BF16)) for _ in range(2)]
            wrB = P.bufs(2, "wr")
            wo = [A.alloc("wo", [128, KC, 256], BF16) for _ in range(2)]
            woB = P.bufs(2, "wo")
            xt = A.alloc("xt", [128, D], F32)
            xB = P.buf("xt")
            yt = A.alloc("yt", [128, D], F32)
            yB = P.buf("yt")
            t1 = A.alloc("t1", [128, D], F32)
            tB1 = P.buf("t1")
            ssqz = A.alloc("ssqz", [128, 4], F32)
            post_t = A.alloc("post_t", [128, D], F32)
            pre_t = A.alloc("pre_t", [128, D], F32)
            ppB = P.buf("pp")
            P.op("sp", DMA(post_t[:], post_d[l]), writes=[ppB], dma=ppB)
            if not last:
                P.op("sp", DMA(pre_t[:], pre_d[l + 1]), writes=[ppB], dma=ppB)
            htmp = (A.alloc("junk", [128, D], BF16), A.alloc("ssq", [128, 4], F32), A.alloc("hf", [128, D], F32),
                    A.alloc("hTst", [128, KC, 128], BF16), P.buf("hTst"), P.buf("htmp"))
            ge = [A.alloc("ge", [128, 512], F32) for _ in range(4)]
            geB = P.bufs(4, "ge")
            gm = [A.alloc("gm", [128, 512], F32) for _ in range(4)]
            gmB = P.bufs(4, "gm")
            src_x = x_d if l == 0 else yscr

            wdeps = wcast_ops[l]

            def load_wr(c, sl):
                w, B = wr[sl], wrB[sl]
                P.op("sp", DMA(flat(w["ma"][:, :, :]), wm_b[l, c]), writes=[B], dma=B, extra_deps=wdeps)
                P.op("sp", DMA(flat(w["mb"][:, :, :]), wm_b[l, 16 + c]), writes=[B], dma=B, extra_deps=wdeps)
                P.op("sp", DMA(flat(w["oa"][:, :, :]), woa_b[l, c]), writes=[B], dma=B, extra_deps=wdeps)
                P.op("sp", DMA(flat(w["ob"][:, :, :]), wob_b[l, c]), writes=[B], dma=B, extra_deps=wdeps)

            def load_wo(blk, sl):
                P.op("sp", DMA(flat(wo[sl][:, :, :]), wout_b[l, blk]), writes=[woB[sl]], dma=woB[sl], extra_deps=wdeps)

            def stage_a_tile(grp, gi):
                t = grp[gi]
                for r in range(4):
                    for (src_, B_, c0_) in ((oagA, oagAB[t // 4], 0), (oagB_d, oagBB[t // 4], 256)):
                        P.op("pool", (lambda o_, i_, s_: (lambda e: e.indirect_dma_start(
                            out=o_, out_offset=None, in_=s_[:, :],
                            in_offset=bass.IndirectOffsetOnAxis(ap=i_, axis=0))))(
                                ogf[:, r * 512 + c0_:r * 512 + c0_ + 256], idx_t[:, t * 4 + r:t * 4 + r + 1], src_),
                            reads=[B_, Bc], writes=[ogB], dma=ogB)
                for q in range(4):
                    bk = psrr.next()
                    for jx in range(4):
                        c = q * 4 + jx
                        if c < 8:
                            col = (c // 2) * 512 + (c % 2) * 128
                        else:
                            cc = c - 8
                            col = (cc // 2) * 512 + 256 + (cc % 2) * 128
                        P.op("pe", TR(ps[bk][:, jx * 128:(jx + 1) * 128], ogf[:, col:col + 128], ID),
                             reads=[ogB, Bc], writes=[psB[bk]])
                    P.op("act", CP(oT[:, q * 4:(q + 1) * 4, gi * 128:(gi + 1) * 128], split(ps[bk][:, :], 4)),
                         reads=[psB[bk]], writes=[oTB])
                P.op("sp", DMA(hTg[:, :, gi * 128:(gi + 1) * 128], split(hsrc[t * 128:(t + 1) * 128, :], KC)),
                     reads=[hsrcB[t]], writes=[hTgB], dma=hTgB)

            def stage_b(grp, inter):
                ntok = len(grp) * 128
                load_wr(0, 0)
                for c in range(16):
                    sl = c % 2
                    if c + 1 < 16:
                        load_wr(c + 1, 1 - sl)
                    w, wB = wr[sl], wrB[sl]
                    bks = [psrr.next() for _ in range(4)]
                    for k in range(KC):
                        P.op("pe", MM(ps[bks[0]][:, 0:ntok], w["ma"][:, k, :], hTg[:, k, 0:ntok], start=(k == 0), stop=(k == KC - 1)),
                             reads=[wB, hTgB], writes=[psB[bks[0]]])
                    for k in range(KC):
                        P.op("pe", MM(ps[bks[1]][:, 0:ntok], w["mb"][:, k, :], hTg[:, k, 0:ntok], start=(k == 0), stop=(k == KC - 1)),
                             reads=[wB, hTgB], writes=[psB[bks[1]]])
                    for k in range(8):
                        P.op("pe", MM(ps[bks[2]][:, 0:ntok], w["oa"][:, k, :], oT[:, k, 0:ntok], start=(k == 0), stop=(k == 7)),
                             reads=[wB, oTB], writes=[psB[bks[2]]])
                    for k in range(8):
                        P.op("pe", MM(ps[bks[3]][:, 0:ntok], w["ob"][:, k, :], oT[:, 8 + k, 0:ntok], start=(k == 0), stop=(k == 7)),
                             reads=[wB, oTB], writes=[psB[bks[3]]])
                    for u in range(2):
                        gi_ = u + 2 * (c % 2)
                        gu = ge[gi_][:, 0:ntok]
                        P.op("act", ACT(gu, ps[bks[u]][:, 0:ntok], AF.Tanh, ZERO, 0.5), reads=[psB[bks[u]], Bc], writes=[geB[gi_]])
                        P.op("dve", STT(gm[gi_][:, 0:ntok], gu, 1.0, ps[bks[2 + u]][:, 0:ntok], ALU.add, ALU.mult),
                             reads=[psB[bks[2 + u]], geB[gi_]], writes=[gmB[gi_]])
                    g0, g1 = 2 * (c % 2), 1 + 2 * (c % 2)
                    P.op("dve", TT(mT[:, c, 0:ntok], gm[g0][:, 0:ntok], gm[g1][:, 0:ntok], ALU.add), reads=[gmB[g0], gmB[g1]], writes=[mTB])
                    if c in inter:
                        inter[c]()

            def stage_c(grp, inter):
                ng = len(grp)
                load_wo(0, 0)
                for blk in range(8):
                    sl = blk % 2
                    if blk + 1 < 8:
                        load_wo(blk + 1, 1 - sl)
                    for gi in range(ng):
                        bk = psrr.next()
                        for k in range(KC):
                            P.op("pe", MM(ps[bk][:, 0:256], mT[:, k, gi * 128:(gi + 1) * 128], wo[sl][:, k, :],
                                          start=(k == 0), stop=(k == KC - 1)), reads=[mTB, woB[sl]], writes=[psB[bk]])
                        P.op("act", MUL(z[:, gi, blk * 256:(blk + 1) * 256], ps[bk][:, 0:256], 0.5), reads=[psB[bk]], writes=[zB[gi]])
                    if blk in inter:
                        inter[blk]()

            def stage_d_tile(grp, gi):
                t = grp[gi]
                P.op("pool", DMA(xt[:], src_x[t * 128:(t + 1) * 128, :]), writes=[xB], dma=xB)
                P.op("dve", STT(htmp[0][:], z[:, gi, :], 1.0, z[:, gi, :], ALU.mult, ALU.mult, accum=ssqz[:, 0:1]),
                     reads=[zB[gi]], writes=[tB1, htmp[5]])
                rstd_of(ssqz, D, tB1)
                P.op("dve", STT(t1[:], z[:, gi, :], ssqz[:, 2:3], post_t[:], ALU.mult, ALU.mult), reads=[zB[gi], tB1, ppB], writes=[tB1])
                P.op("dve", TT(yt[:], t1[:], xt[:], ALU.add), reads=[tB1, xB], writes=[yB])
                if last:
                    P.op("pool", DMA(y_d[t * 128:(t + 1) * 128, :], yt[:]), reads=[yB], dma=yB)
                else:
                    P.op("pool", DMA(yscr[t * 128:(t + 1) * 128, :], yt[:]), reads=[yB], dma=yB)
                    hst.append(emit_h(yt, yB, pre_t, ppB, t, htmp))
                    if t % 2 == 1 or t == TL - 1:
                        ag_chunk(hsrc, hag, hagB, t // 2, TL * 128, 256, list(hst))
                        del hst[:]

            def spread(fns, slots):
                m = {}
                for n_, f in enumerate(fns):
                    sl_ = slots[min(n_, len(slots) - 1)]
                    m.setdefault(sl_, []).append(f)
                return {k_: (lambda fs=v_: [f() for f in fs]) for k_, v_ in m.items()}

            hst = []
            G = len(TGROUPS)
            for gi in range(len(TGROUPS[0])):
                stage_a_tile(TGROUPS[0], gi)
            for g in range(G):
                grp = TGROUPS[g]
                d_fns = [] if g == 0 else [(lambda pg=TGROUPS[g - 1], x_=x: stage_d_tile(pg, x_)) for x in range(len(TGROUPS[g - 1]))]
                stage_b(grp, spread(d_fns, [3, 7, 11, 15]))
                a_fns = [] if g + 1 == G else [(lambda ng_=TGROUPS[g + 1], x_=x: stage_a_tile(ng_, x_)) for x in range(len(TGROUPS[g + 1]))]
                stage_c(grp, spread(a_fns, [1, 3, 5, 7]))
            for x in range(len(TGROUPS[G - 1])):
                stage_d_tile(TGROUPS[G - 1], x)
            P.barrier()

        def run_all():
            prologue()
            if STOP_AFTER == "pro":
                return
            for l in range(DEPTH):
                pass_f(l)
                if STOP_AFTER == "F%d" % l:
                    return
                pass_g(l)
                if STOP_AFTER == "G%d" % l:
                    return
                if STOP_AFTER == "AG%d" % l:
                    return
                token_phase(l)
                if STOP_AFTER == "T%d" % l:
                    return

        run_all()
        P.barrier()
        stats = P.emit()
        stats["sbuf_peak"] = A.peak
    return nc, stats


_OFF = dict(fq=0, fk=1024, fv=2048, ff=3072, fg=3080, gq=4104, gk=5128, gv=6152, glr=7176, gg=7192, ma=8216, mb=10264)


def _pkc(w):
    K = w.shape[0] // 128
    return np.ascontiguousarray(w.reshape(K, 128, w.shape[1]).transpose(1, 0, 2))


def _chunks(w, cw):
    K = w.shape[0] // 128
    NC = w.shape[1] // cw
    return np.ascontiguousarray(w.reshape(K, 128, NC, cw).transpose(2, 1, 0, 3))


def _prep(inp):
    f = lambda a: np.asarray(a, dtype=np.float32)
    x_prompt, x_sample = f(inp["x_prompt"]), f(inp["x_sample"])
    cache_k, cache_v, cache_logf, state_gla = f(inp["cache_k"]), f(inp["cache_v"]), f(inp["cache_logf"]), f(inp["state_gla"])
    w_in, w_a2, b_a, b_f = f(inp["w_in"]), f(inp["w_a2"]), f(inp["b_a"]), f(inp["b_f"])
    gla_gain, w_oa, w_ob, w_out = f(inp["gla_gain"]), f(inp["w_oa"]), f(inp["w_ob"]), f(inp["w_out"])
    pre_norm, post_norm = f(inp["pre_norm"]), f(inp["post_norm"])

    shared = {}
    shared["wm"] = np.stack([_chunks(w_in[l][:, _OFF["ma"]:_OFF["ma"] + 4096], 128) for l in range(DEPTH)])
    shared["woa"] = np.stack([_chunks(w_oa[l], 128) for l in range(DEPTH)])
    shared["wob"] = np.stack([_chunks(w_ob[l], 128) for l in range(DEPTH)])
    shared["wout"] = np.stack([_chunks(w_out[l], 256) for l in range(DEPTH)])
    shared["gain"] = np.ascontiguousarray(np.broadcast_to(gla_gain[:, None, :], (DEPTH, 128, 256)))
    shared["pre"] = np.ascontiguousarray(np.broadcast_to(pre_norm[:, None, :], (DEPTH, 128, D)))
    shared["post"] = np.ascontiguousarray(np.broadcast_to(post_norm[:, None, :], (DEPTH, 128, D)))

    maps = []
    for c in range(8):
        b, g = c // 4, c % 4
        m = dict(shared)
        xl = np.zeros((TL * 128, D), np.float32)
        for k in range(16):
            xl[k * 128:(k + 1) * 128] = x_prompt[b, (4 * k + g) * 128:(4 * k + g + 1) * 128]
        xl[2048:2080] = x_sample[4 * b + g]
        m["x_loc"] = xl
        fs = slice(256 * g, 256 * g + 256)

        def cols(l, name, sl=fs):
            o = _OFF[name]
            return w_in[l][:, o + sl.start:o + sl.stop]

        m["wffm"] = np.stack([_pkc(cols(l, "fq")) for l in range(DEPTH)])
        m["wftm"] = np.stack([_pkc(np.concatenate([cols(l, "fk"), cols(l, "fv"), cols(l, "fg"),
                                                   cols(l, "ff", slice(2 * g, 2 * g + 2))], 1)) for l in range(DEPTH)])
        m["wgfm"] = np.stack([_pkc(np.concatenate([cols(l, "gq"), cols(l, "gk"), cols(l, "glr", slice(0, 16))], 1))
                              for l in range(DEPTH)])
        m["wgtm"] = np.stack([_pkc(np.concatenate([cols(l, "gv"), cols(l, "gg")], 1)) for l in range(DEPTH)])
        m["wa2"] = np.ascontiguousarray(w_a2[:, :, fs])
        m["ba"] = np.ascontiguousarray(b_a[:, fs].reshape(DEPTH, 2, 128).transpose(0, 2, 1))
        m["bfb"] = np.ascontiguousarray(np.broadcast_to(b_f[:, None, 2 * g:2 * g + 2], (DEPTH, 128, 2)))
        sbs = slice(4 * b, 4 * b + 4)
        m["ck"] = np.ascontiguousarray(cache_k[:, sbs, :, 2 * g:2 * g + 2, :].reshape(DEPTH, 4, 4096, 256))
        m["cv"] = np.ascontiguousarray(cache_v[:, sbs, :, 2 * g:2 * g + 2, :].reshape(DEPTH, 4, 4096, 256))
        m["clf"] = np.ascontiguousarray(cache_logf[:, sbs, :, 2 * g:2 * g + 2].reshape(DEPTH, 4, 32, 128, 2).transpose(0, 1, 3, 2, 4))
        m["sg"] = np.ascontiguousarray(state_gla[:, sbs, g].reshape(DEPTH, 4, 2, 128, 256).transpose(0, 1, 3, 2, 4))
        idx = np.zeros((128, TL * 4), np.int32)
        p = np.arange(128)
        for t in range(TL):
            tok = ((4 * t + g) * 128 + p) if t < 16 else (8192 + 128 * g + p)
            q, within = tok // 2048, tok % 2048
            nq = np.where(q < 4, 2048, 512)
            for r in range(4):
                idx[:, t * 4 + r] = q * 8192 + r * nq + within
        m["idx"] = idx
        maps.append(m)
    return maps


_CACHE = {}


def kernel(**inputs):
    if "nc" not in _CACHE:
        _CACHE["nc"], _CACHE["stats"] = build_program()
    nc = _CACHE["nc"]
    maps = _prep(inputs)
    res = run_bass_kernel_spmd(nc, maps, core_ids=list(range(8)))
    R = res.results
    B, SEQ, DB, DS = 2, 8192, 8, 32
    y_prompt = np.zeros((B, SEQ, D), np.float32)
    y_sample = np.zeros((DB, DS, D), np.float32)
    k_prompt = np.zeros((DEPTH, B, SEQ, 8, 128), np.float32)
    v_prompt = np.zeros((DEPTH, B, SEQ, 8, 128), np.float32)
    logf_prompt = np.zeros((DEPTH, B, SEQ, 8), np.float32)
    gla_prompt = np.zeros((DEPTH, B, 4, 256, 256), np.float32)
    k_sample = np.zeros((DEPTH, DB, DS, 8, 128), np.float32)
    v_sample = np.zeros((DEPTH, DB, DS, 8, 128), np.float32)
    logf_sample = np.zeros((DEPTH, DB, DS, 8), np.float32)
    gla_sample = np.zeros((DEPTH, DB, 4, 256, 256), np.float32)
    for c in range(8):
        b, g = c // 4, c % 4
        r = R[c]
        y = np.asarray(r["y_loc"])
        for k in range(16):
            y_prompt[b, (4 * k + g) * 128:(4 * k + g + 1) * 128] = y[k * 128:(k + 1) * 128]
        y_sample[4 * b + g] = y[2048:2080]
        ko, vo = np.asarray(r["k_out"]), np.asarray(r["v_out"])
        lf = np.asarray(r["lf_out"]).transpose(0, 2, 1, 3).reshape(DEPTH, NTOK, 2)
        gp, gs = np.asarray(r["gla_p"]), np.asarray(r["gla_s"])
        for l in range(DEPTH):
            k_prompt[l, b, :, 2 * g:2 * g + 2, :] = ko[l, :SEQ].reshape(SEQ, 2, 128)
            v_prompt[l, b, :, 2 * g:2 * g + 2, :] = vo[l, :SEQ].reshape(SEQ, 2, 128)
            logf_prompt[l, b, :, 2 * g:2 * g + 2] = lf[l, :SEQ]
            gla_prompt[l, b, g] = gp[l].transpose(1, 0, 2).reshape(256, 256)
            for k4 in range(4):
                o = SEQ + 128 * k4
                k_sample[l, 4 * b + k4, :, 2 * g:2 * g + 2, :] = ko[l, o:o + DS].reshape(DS, 2, 128)
                v_sample[l, 4 * b + k4, :, 2 * g:2 * g + 2, :] = vo[l, o:o + DS].reshape(DS, 2, 128)
                logf_sample[l, 4 * b + k4, :, 2 * g:2 * g + 2] = lf[l, o:o + DS]
                gla_sample[l, 4 * b + k4, g] = gs[l, k4].transpose(1, 0, 2).reshape(256, 256)
    return (y_prompt, y_sample, k_prompt, v_prompt, logf_prompt, gla_prompt,
            k_sample, v_sample, logf_sample, gla_sample)
```

```python
import math
import numpy as np
from contextlib import ExitStack
import concourse.bass as bass
import concourse.mybir as mybir
from concourse.bass_utils import run_bass_kernel_spmd

F32, BF16, I32 = mybir.dt.float32, mybir.dt.bfloat16, mybir.dt.int32
AF = mybir.ActivationFunctionType
ALU = mybir.AluOpType

D = 2048
KC = 16
NT = 68
NBLK = 17
TL = 17
NTOK = NT * 128
DEPTH = 2
SCALE_F = 128 ** -0.5
GROUPS = [[0, 1, 2, 3], [4, 5, 6, 7]]
STOP_AFTER = None
TGROUPS = [[0, 1, 2, 3], [4, 5, 6, 7], [8, 9, 10, 11], [12, 13, 14], [15, 16]]
SB_BASE = 16512
SB_END = 229344


class Buf:
    __slots__ = ("name", "w", "r", "dsem", "dcnt")

    def __init__(self, name):
        self.name = name
        self.w = None
        self.r = []
        self.dsem = None
        self.dcnt = 0


class Op:
    __slots__ = ("eng", "fn", "deps", "dma", "token", "mark", "pos")

    def __init__(self, eng, fn, deps, dma):
        self.eng = eng
        self.fn = fn
        self.deps = deps
        self.dma = dma
        self.token = None
        self.mark = False
        self.pos = 0


class Prog:
    ENG = ("pe", "act", "dve", "pool", "sp")

    def __init__(self, nc, stack):
        self.nc = nc
        self.stack = stack
        self.ops = []
        self.h = {"pe": nc.tensor, "act": nc.scalar, "dve": nc.vector, "pool": nc.gpsimd, "sp": nc.sync}
        self.esem = {e: stack.enter_context(nc.semaphore("es_" + e)) for e in self.ENG}
        self.ccsem = stack.enter_context(nc.semaphore("ccsem"))
        self.cccnt = 0
        self.last = {e: None for e in self.ENG}
        self.pending = []
        self.nbuf = 0
        self.nsem = 6

    def buf(self, name=None):
        self.nbuf += 1
        return Buf("%s_%d" % (name or "b", self.nbuf))

    def bufs(self, n, name="b"):
        return [self.buf(name) for _ in range(n)]

    def _dsem(self, b):
        if b.dsem is None:
            b.dsem = self.stack.enter_context(self.nc.semaphore("ds_" + b.name))
            self.nsem += 1
        return b.dsem

    def op(self, eng, fn, reads=(), writes=(), dma=None, cc=False, extra_deps=(), nobarrier=False):
        idx = len(self.ops)
        deps = set(extra_deps)
        for b in reads:
            if b.w is not None:
                deps.add(b.w)
            b.r.append(idx)
        for b in writes:
            if b.w is not None:
                pw = self.ops[b.w]
                if dma is not None and pw.dma is dma and not b.r:
                    deps.update(pw.deps)
                else:
                    deps.add(b.w)
            deps.update(b.r)
            b.r = []
            b.w = idx
        deps.discard(idx)
        latest = {}
        keep = []
        for di in deps:
            d = self.ops[di]
            if d.dma is None and d.fn is not None:
                if d.eng not in latest or latest[d.eng] < di:
                    latest[d.eng] = di
            else:
                keep.append(di)
        deps = keep + list(latest.values())
        o = Op(eng, fn, sorted(deps), dma)
        if dma is not None:
            sem = self._dsem(dma)
            dma.dcnt += 16
            o.token = (sem, dma.dcnt)
            if not nobarrier:
                self.pending.append(idx)
        elif cc:
            self.cccnt += 1
            o.token = (self.ccsem, self.cccnt)
            o.dma = "cc"
        self.ops.append(o)
        if fn is not None and not cc:
            self.last[eng] = idx
        return idx

    def barrier(self):
        deps = [v for v in self.last.values() if v is not None] + list(self.pending)
        self.pending = []
        for e in self.ENG:
            self.op(e, None, extra_deps=deps)

    def _needs_wait(self, o, d):
        if d.dma is not None:
            return True
        if d.eng != o.eng:
            return True
        if o.dma is not None:
            return True
        if o.eng == "pe":
            return False
        return True

    def emit(self):
        ops = self.ops
        pos = {e: 0 for e in self.ENG}
        for o in ops:
            if o.fn is not None:
                pos[o.eng] += 1
            o.pos = pos[o.eng]
        for o in ops:
            for di in o.deps:
                d = ops[di]
                if d.dma is None and d.fn is not None and self._needs_wait(o, d):
                    d.mark = True
        cnt = {e: 0 for e in self.ENG}
        for o in ops:
            if o.dma is None and o.mark:
                cnt[o.eng] += 1
                o.token = (self.esem[o.eng], cnt[o.eng])
        seen = {e: {} for e in self.ENG}
        nwait = 0
        for o in ops:
            E = self.h[o.eng]
            waits = {}
            for di in o.deps:
                d = ops[di]
                if d.fn is None or not self._needs_wait(o, d):
                    continue
                sem, val = d.token
                k = id(sem)
                if k not in waits or waits[k][1] < val:
                    waits[k] = (sem, val)
            for k, (sem, val) in waits.items():
                if seen[o.eng].get(k, 0) < val:
                    E.wait_ge(sem, val)
                    seen[o.eng][k] = val
                    nwait += 1
            if o.fn is None:
                continue
            ins = o.fn(E)
            if o.dma is not None:
                ins.then_inc(o.token[0], 1 if o.dma == "cc" else 16)
            elif o.mark:
                ins.then_inc(o.token[0], 1)
        return dict(n_ops=len(ops), n_wait=nwait, marked=cnt, nsem=self.nsem)


class Arena:
    def __init__(self, nc):
        self.nc = nc
        self.off = SB_BASE
        self.n = 0
        self.peak = 0

    def alloc(self, name, shape, dtype):
        nbytes = int(np.prod(shape[1:])) * (4 if dtype in (F32, I32) else 2)
        nbytes = (nbytes + 31) // 32 * 32
        assert self.off + nbytes <= SB_END, (name, self.off, nbytes)
        self.n += 1
        t = self.nc.alloc_sbuf_tensor_at("%s_%d" % (name, self.n), list(shape), dtype, offset=self.off)
        self.off += nbytes
        self.peak = max(self.peak, self.off)
        return t

    def mark(self):
        return self.off

    def reset(self, m):
        self.off = m


class RR:
    def __init__(self, items):
        self.items = list(items)
        self.i = 0

    def next(self):
        x = self.items[self.i % len(self.items)]
        self.i += 1
        return x


def MM(out, lhsT, rhs, start=True, stop=True, skip=False):
    return lambda e: e.matmul(out, lhsT=lhsT, rhs=rhs, start=start, stop=stop, skip_group_check=skip)


def TR(out, in_, ident):
    return lambda e: e.transpose(out=out, in_=in_, identity=ident)


def ACT(out, in_, func, bias, scale):
    return lambda e: e.activation(out=out, in_=in_, func=func, bias=bias, scale=scale)


def MUL(out, in_, c):
    return lambda e: e.mul(out, in_, c)


def CP(out, in_):
    return lambda e: e.copy(out=out, in_=in_)


def TC(out, in_):
    return lambda e: e.tensor_copy(out=out, in_=in_)


def TT(out, in0, in1, op):
    return lambda e: e.tensor_tensor(out=out, in0=in0, in1=in1, op=op)


def TS(out, in0, s1, s2, op0, op1=None):
    if op1 is None:
        return lambda e: e.tensor_scalar(out=out, in0=in0, scalar1=s1, scalar2=None, op0=op0)
    return lambda e: e.tensor_scalar(out=out, in0=in0, scalar1=s1, scalar2=s2, op0=op0, op1=op1)


def STT(out, in0, scalar, in1, op0, op1, accum=None):
    if accum is None:
        return lambda e: e.scalar_tensor_tensor(out=out, in0=in0, scalar=scalar, in1=in1, op0=op0, op1=op1)
    return lambda e: e.scalar_tensor_tensor(out=out, in0=in0, scalar=scalar, in1=in1, op0=op0, op1=op1, accum_out=accum)


def SCAN(out, d0, d1, init):
    return lambda e: e.tensor_tensor_scan(out=out, data0=d0, data1=d1, initial=init, op0=ALU.mult, op1=ALU.add)


def RCP(out, in_):
    return lambda e: e.reciprocal(out=out, in_=in_)


def MS(ap, v):
    return lambda e: e.memset(ap, v)


def DMA(out, in_):
    return lambda e: e.dma_start(out=out, in_=in_)


def flat(ap3):
    n = len(ap3.shape)
    if n == 3:
        return ap3.rearrange("p a b -> p (a b)")
    if n == 4:
        return ap3.rearrange("p a b c -> p (a b c)")
    return ap3


def split(ap2, a):
    return ap2.rearrange("p (a b) -> p a b", a=a)


def build_program():
    nc = bass.Bass("TRN2", target_bir_lowering=False)

    def din(name, shape, dt=F32):
        return nc.dram_tensor(name, list(shape), dt, kind="ExternalInput").ap()

    def dout(name, shape, dt=F32):
        return nc.dram_tensor(name, list(shape), dt, kind="ExternalOutput").ap()

    x_d = din("x_loc", [TL * 128, D])
    wffm_d = din("wffm", [DEPTH, 128, KC, 256])
    wftm_d = din("wftm", [DEPTH, 128, KC, 770])
    wgfm_d = din("wgfm", [DEPTH, 128, KC, 528])
    wgtm_d = din("wgtm", [DEPTH, 128, KC, 512])
    wm_d = din("wm", [DEPTH, 32, 128, KC, 128])
    woa_d = din("woa", [DEPTH, 16, 128, 8, 128])
    wob_d = din("wob", [DEPTH, 16, 128, 8, 128])
    wout_d = din("wout", [DEPTH, 8, 128, KC, 256])
    wa2_d = din("wa2", [DEPTH, 16, 256])
    ba_d = din("ba", [DEPTH, 128, 2])
    bfb_d = din("bfb", [DEPTH, 128, 2])
    gain_d = din("gain", [DEPTH, 128, 256])
    pre_d = din("pre", [DEPTH, 128, D])
    post_d = din("post", [DEPTH, 128, D])
    ck_d = din("ck", [DEPTH, 4, 4096, 256])
    cv_d = din("cv", [DEPTH, 4, 4096, 256])
    clf_d = din("clf", [DEPTH, 4, 128, 32, 2])
    sg_d = din("sg", [DEPTH, 4, 128, 2, 256])
    idx_d = din("idx", [128, TL * 4], I32)

    y_d = dout("y_loc", [TL * 128, D])
    ko_d = dout("k_out", [DEPTH, NTOK, 256])
    vo_d = dout("v_out", [DEPTH, NTOK, 256])
    lfo_d = dout("lf_out", [DEPTH, 128, NT, 2])
    gp_d = dout("gla_p", [DEPTH, 128, 2, 256])
    gs_d = dout("gla_s", [DEPTH, 4, 128, 2, 256])

    hsrc = nc.dram_tensor("hsrc", [TL * 128, D], BF16).ap()
    hag = nc.dram_tensor("hag", [4 * TL * 128, D], BF16).ap()
    osrcA = nc.dram_tensor("osrcA", [NTOK, 256], BF16).ap()
    osrcB = nc.dram_tensor("osrcB", [NTOK, 256], BF16).ap()
    oagA = nc.dram_tensor("oagA", [4 * NTOK, 256], BF16).ap()
    oagB_d = nc.dram_tensor("oagB", [4 * NTOK, 256], BF16).ap()
    yscr = nc.dram_tensor("yscr", [TL * 128, D], F32).ap()
    wm_b = nc.dram_tensor("wm_b", [DEPTH, 32, 128, KC * 128], BF16).ap()
    wffm_b = nc.dram_tensor("wffm_b", [DEPTH, 128, KC * 256], BF16).ap()
    wftm_b = nc.dram_tensor("wftm_b", [DEPTH, 128, KC * 770], BF16).ap()
    wgfm_b = nc.dram_tensor("wgfm_b", [DEPTH, 128, KC * 528], BF16).ap()
    wgtm_b = nc.dram_tensor("wgtm_b", [DEPTH, 128, KC * 512], BF16).ap()
    woa_b = nc.dram_tensor("woa_b", [DEPTH, 16, 128, 8 * 128], BF16).ap()
    wob_b = nc.dram_tensor("wob_b", [DEPTH, 16, 128, 8 * 128], BF16).ap()
    wout_b = nc.dram_tensor("wout_b", [DEPTH, 8, 128, KC * 256], BF16).ap()

    with ExitStack() as st:
        P = Prog(nc, st)
        A = Arena(nc)
        ps = [nc.alloc_psum_tensor("psb%d" % i, [128, 512], F32) for i in range(8)]
        psB = [P.buf("ps%d" % i) for i in range(8)]

        ones_f = A.alloc("ones_f", [128, 128], F32)
        ident_f = A.alloc("ident_f", [128, 128], F32)
        tri_f = A.alloc("tri_f", [128, 128], F32)
        tri_b = A.alloc("tri_b", [128, 128], BF16)
        resetm = A.alloc("resetm", [128, 512], F32)
        cst = A.alloc("cst", [128, 8], F32)
        idx_t = A.alloc("idx_t", [128, TL * 4], I32)
        Bc = P.buf("consts")
        ID = ident_f[:]

        P.op("pool", MS(ones_f[:], 1.0), writes=[Bc])
        P.op("pool", MS(ident_f[:], 0.0), writes=[Bc])
        P.op("pool", lambda e: e.affine_select(out=ident_f[:], in_=ones_f[:], pattern=[[-1, 128]],
                                               compare_op=ALU.is_equal, fill=0.0, base=0, channel_multiplier=1),
             writes=[Bc])
        P.op("pool", MS(tri_f[:], 0.0), writes=[Bc])
        P.op("pool", lambda e: e.affine_select(out=tri_f[:], in_=ones_f[:], pattern=[[1, 128]],
                                               compare_op=ALU.is_ge, fill=0.0, base=0, channel_multiplier=-1),
             writes=[Bc])
        P.op("pool", TC(tri_b[:], tri_f[:]), writes=[Bc])
        P.op("pool", MS(resetm[:], 1.0), writes=[Bc])
        for q in range(4):
            P.op("pool", MS(resetm[:, q * 128:q * 128 + 1], 0.0), writes=[Bc])
        P.op("pool", MS(cst[:, 0:1], 1e-6), writes=[Bc])
        P.op("pool", MS(cst[:, 1:2], 1.0), writes=[Bc])
        P.op("pool", MS(cst[:, 2:3], -math.log(16.0)), writes=[Bc])
        P.op("pool", MS(cst[:, 3:4], 0.0), writes=[Bc])
        P.op("pool", MS(cst[:, 4:5], math.log(0.5)), writes=[Bc])
        P.op("sp", DMA(idx_t[:], idx_d), writes=[Bc], dma=Bc)
        EPS, ONE, NL16, ZERO, LNH = cst[:, 0:1], cst[:, 1:2], cst[:, 2:3], cst[:, 3:4], cst[:, 4:5]

        hsrcB = P.bufs(TL, "hsrc")
        hagB = P.bufs(9, "hag")
        oagAB = P.bufs(5, "oagA")
        oagBB = P.bufs(5, "oagB")
        base_mark = A.mark()
        psrr = RR(range(8))

        def rstd_of(ssq, n, B, lnbias=None):
            P.op("act", ACT(ssq[:, 1:2], ssq[:, 0:1], AF.Ln, EPS, 1.0 / n), reads=[B, Bc], writes=[B])
            P.op("act", ACT(ssq[:, 2:3], ssq[:, 1:2], AF.Exp, ZERO if lnbias is None else lnbias, -0.5), reads=[B, Bc], writes=[B])

        def emit_h(y_t, yB, pre_t, preB, t, tmp, q_="pool"):
            junk, ssq, hf, hTst, hTstB, tB = tmp
            P.op("dve", STT(junk[:], y_t[:], 1.0, y_t[:], ALU.mult, ALU.mult, accum=ssq[:, 0:1]), reads=[yB], writes=[tB])
            rstd_of(ssq, D, tB)
            P.op("dve", STT(hf[:], y_t[:], ssq[:, 2:3], pre_t[:], ALU.mult, ALU.mult), reads=[yB, tB, preB], writes=[tB])
            for q in range(4):
                bk = psrr.next()
                for j in range(4):
                    c = q * 4 + j
                    P.op("pe", TR(ps[bk][:, j * 128:(j + 1) * 128], hf[:, c * 128:(c + 1) * 128], ID),
                         reads=[tB, Bc], writes=[psB[bk]])
                P.op("act", CP(hTst[:, q * 4:(q + 1) * 4, :], split(ps[bk][:, :], 4)), reads=[psB[bk]], writes=[hTstB])
            return P.op(q_, DMA(hsrc[t * 128:(t + 1) * 128, :], flat(hTst[:, :, :])), reads=[hTstB], writes=[hsrcB[t]], dma=hTstB)

        def ag_chunk(src, dst, dstB, q, rows_total, rows_chunk, store_ops):
            r0 = q * rows_chunk
            n = min(rows_chunk, rows_total - r0)
            P.op("pool", (lambda i_, o_: (lambda e: e.collective_compute(
                "AllGather", ALU.bypass, replica_groups=GROUPS, ins=[i_], outs=[o_])))(
                    src[r0:r0 + n, :], dst[q * 4 * rows_chunk:q * 4 * rows_chunk + 4 * n, :]),
                writes=[dstB[q]], cc=True, extra_deps=store_ops)

        def cumsum_tiles(bkrr, lf_ap, n, carry_ap, out_ap, tmp, tB, lfB, outB, carry_out_ap=None):
            bk = bkrr.next()
            sb_, incl = tmp
            P.op("pe", MM(ps[bk][:, 0:2 * n], tri_f[:], flat(lf_ap)), reads=[lfB, Bc], writes=[psB[bk]])
            P.op("pe", MM(ps[bk][:, 2 * n:4 * n], ones_f[:], flat(lf_ap), start=False, stop=True, skip=True),
                 reads=[lfB, Bc], writes=[psB[bk]])
            P.op("act", CP(sb_[:, 0:4 * n], ps[bk][:, 0:4 * n]), reads=[psB[bk]], writes=[tB])
            tot = sb_[:, 2 * n:4 * n].rearrange("p (a b) -> p a b", b=2)
            loc = sb_[:, 0:2 * n].rearrange("p (a b) -> p a b", b=2)
            inc3 = incl[:, 0:2 * n].rearrange("p (a b) -> p a b", b=2)
            for hh in range(2):
                P.op("dve", SCAN(inc3[:, :, hh], ones_f[:, 0:n], tot[:, :, hh], carry_ap[:, hh:hh + 1]),
                     reads=[tB, Bc, outB], writes=[tB])
            P.op("dve", TT(loc, loc, tot, ALU.subtract), reads=[tB], writes=[tB])
            P.op("dve", TT(out_ap, loc, inc3, ALU.add), reads=[tB], writes=[outB])
            if carry_out_ap is not None:
                P.op("dve", TC(carry_out_ap, inc3[:, n - 1, :]), reads=[tB], writes=[outB])

        def load_hT_block(i, hTb_t, hTbB):
            q, tl = i // 2, i % 2
            nq = 256 if q < 8 else 128
            for tt in range(4):
                row = q * 1024 + tt * nq + tl * 128
                P.op("sp", DMA(hTb_t[:, :, tt * 128:(tt + 1) * 128], split(hag[row:row + 128, :], KC)),
                     reads=[hagB[q]], writes=[hTbB], dma=hTbB)

        hwB = {}
        hw_ops = {}
        hw_jobs = {}
        for l_ in range(DEPTH):
            for ps_, mats in (("F", ((wffm_b, wffm_d, 256, 8), (wftm_b, wftm_d, 770, 2))),
                              ("G", ((wgfm_b, wgfm_d, 528, 3), (wgtm_b, wgtm_d, 512, 4)))):
                hwB[(l_, ps_)] = P.buf("hwcast%d%s" % (l_, ps_))
                hw_ops[(l_, ps_)] = []
                jl = []
                for (dst_, src_, cols, kk) in mats:
                    for k0 in range(0, KC, kk):
                        k1 = min(KC, k0 + kk)
                        jl.append((dst_[l_, :, k0 * cols:k1 * cols], flat(src_[l_, :, k0:k1, :])))
                hw_jobs[(l_, ps_)] = jl

        def emit_hw_casts(key, n=None):
            jl = hw_jobs[key]
            for _ in range(len(jl) if n is None else min(n, len(jl))):
                o_, i_ = jl.pop(0)
                hw_ops[key].append(P.op("pool", DMA(o_, i_), dma=hwB[key], nobarrier=True))

        emit_hw_casts((0, "F"))

        def load_w_bf(dst_t, src_ap, B, key):
            emit_hw_casts(key)
            P.op("sp", DMA(flat(dst_t[:, :, :]), src_ap), writes=[B], dma=B, extra_deps=hw_ops[key])

        def silu2_from_psum(out_ap, ps_ap, et_ap, etB_, psBuf, outB):
            P.op("act", ACT(et_ap, ps_ap, AF.Tanh, ZERO, 0.5), reads=[psBuf, Bc], writes=[etB_])
            P.op("dve", STT(out_ap, et_ap, 1.0, ps_ap, ALU.add, ALU.mult), reads=[psBuf, etB_], writes=[outB])

        wcastB = [P.buf("wcast%d" % l_) for l_ in range(DEPTH)]
        wcast_ops = [[] for _ in range(DEPTH)]

        def cast_jobs(l):
            jobs = []
            for c in range(32):
                jobs.append((wm_b[l, c], flat(wm_d[l, c])))
            for c in range(16):
                jobs.append((woa_b[l, c], flat(woa_d[l, c])))
                jobs.append((wob_b[l, c], flat(wob_d[l, c])))
            for blk in range(8):
                for hf_ in range(2):
                    jobs.append((wout_b[l, blk, :, hf_ * 2048:(hf_ + 1) * 2048], flat(wout_d[l, blk, :, hf_ * 8:(hf_ + 1) * 8, :])))
            return jobs

        def emit_casts(l, jobs, n):
            for _ in range(min(n, len(jobs))):
                o_, i_ = jobs.pop(0)
                wcast_ops[l].append(P.op("pool", DMA(o_, i_), dma=wcastB[l], nobarrier=True))

        def prologue():
            A.reset(base_mark)
            pre_t = A.alloc("pre_t", [128, D], F32)
            preB = P.buf("pre")
            P.op("sp", DMA(pre_t[:], pre_d[0]), writes=[preB], dma=preB)
            xts = [A.alloc("xt", [128, D], F32) for _ in range(2)]
            xBs = P.bufs(2, "xt")
            tmps = []
            for _ in range(2):
                tmps.append((A.alloc("junk", [128, D], BF16), A.alloc("ssq", [128, 4], F32), A.alloc("hf", [128, D], F32),
                             A.alloc("hTst", [128, KC, 128], BF16), P.buf("hTst"), P.buf("htmp")))
            sts = []
            P.op("sp", DMA(xts[0][:], x_d[0:128, :]), writes=[xBs[0]], dma=xBs[0])
            for t in range(TL):
                s = t % 2
                if t + 1 < TL:
                    P.op("sp", DMA(xts[1 - s][:], x_d[(t + 1) * 128:(t + 2) * 128, :]), writes=[xBs[1 - s]], dma=xBs[1 - s])
                sts.append(emit_h(xts[s], xBs[s], pre_t, preB, t, tmps[s], q_="sp"))
                if t % 2 == 1 or t == TL - 1:
                    ag_chunk(hsrc, hag, hagB, t // 2, TL * 128, 256, sts)
                    sts = []
            P.barrier()

        def pass_f(l):
            A.reset(base_mark)
            WFfm = A.alloc("WFfm", [128, KC, 256], BF16)
            WFtm = A.alloc("WFtm", [128, KC, 770], BF16)
            WB_ = P.buf("WF")
            load_w_bf(WFfm, wffm_b[l], WB_, (l, "F"))
            load_w_bf(WFtm, wftm_b[l], WB_, (l, "F"))
            bfb = A.alloc("bfb", [128, 2], F32)
            P.op("sp", DMA(bfb[:], bfb_d[l]), writes=[WB_], dma=WB_)
            kT = A.alloc("kT", [128, 2, NTOK], BF16)
            Vg = A.alloc("Vg", [128, NT, 2, 129], BF16)
            kTB = P.bufs(NT, "kT")
            VB = P.bufs(NT, "V")
            P.op("pool", MS(flat(Vg[:, :, :, :]), 2.0), writes=VB)
            hTb = [A.alloc("hTb", [128, KC, 512], BF16) for _ in range(2)]
            hTbB = P.bufs(2, "hTb")
            qT = [A.alloc("qT", [128, 2, 512], BF16) for _ in range(2)]
            qTB = P.bufs(2, "qT")
            kvf = [A.alloc("kvf", [128, 512], F32) for _ in range(2)]
            kvfB = P.bufs(2, "kvf")
            et = [A.alloc("et", [128, 256], F32) for _ in range(2)]
            etB = P.bufs(2, "et")
            fgs = [A.alloc("fgs", [128, 4, 256], BF16) for _ in range(2)]
            fgsB = P.bufs(2, "fgs")
            lfpre = [A.alloc("lfpre", [128, 4, 2], F32) for _ in range(2)]
            lfpB = P.bufs(2, "lfp")
            LF = A.alloc("LF", [128, NT, 2], F32)
            LFB = P.buf("LF")
            C = A.alloc("C", [128, NT, 2], F32)
            CB = P.buf("C")
            carry = A.alloc("carry", [128, 2], F32)
            cref = A.alloc("cref", [128, NBLK, 2], F32)
            cs_tmp = (A.alloc("cs_sb", [128, 128], F32), A.alloc("cs_incl", [128, 64], F32))
            csB = P.buf("cstmp")
            btab = [A.alloc("btab", [128, 2, NT], F32) for _ in range(2)]
            btB = P.bufs(2, "btab")
            PT = [A.alloc("PT", [128, 512], BF16) for _ in range(4)]
            PTB = P.bufs(4, "PT")
            ptrr = RR(range(4))
            ost = [A.alloc("ost", [128, 256], BF16) for _ in range(8)]
            ostB = P.bufs(8, "ost")
            rec = [A.alloc("rec", [128, 4], F32) for _ in range(8)]
            ckf = [A.alloc("ckf", [128, 256], F32) for _ in range(3)]
            ckfB = P.bufs(3, "ckf")
            ckT = [A.alloc("ckT", [128, 2, 128], BF16) for _ in range(3)]
            ckTB = P.bufs(3, "ckT")
            cV = [A.alloc("cV", [128, 2, 129], BF16) for _ in range(3)]
            cVB = P.bufs(3, "cV")
            for s3 in range(3):
                P.op("pool", MS(flat(cV[s3][:, :, :]), 2.0), writes=[cVB[s3]])
            clf = A.alloc("clf", [128, 32, 2], F32)
            clfB = P.buf("clf")
            cC = A.alloc("cC", [128, 32, 2], F32)
            cCB = P.buf("cC")
            ccar = A.alloc("ccar", [128, 2], F32)
            Cn = A.alloc("Cn", [128, 1, 2], F32)
            btS = A.alloc("btS", [128, 2, 33], F32)
            btSB = P.buf("btS")
            PTs = [A.alloc("PTs", [128, 64], BF16) for _ in range(3)]
            PTsB = P.bufs(3, "PTs")

            P.op("pool", MS(carry[:], 0.0), writes=[CB])
            pj = RR([0, 1, 2, 3])
            pjs = RR([0])
            stb = RR([1, 2, 3])
            OB = [(4, 5), (6, 7)]

            def sample_attention(s):
                for k4 in range(4):
                    j = 64 + k4
                    P.op("sp", DMA(clf[:], clf_d[l, k4]), writes=[clfB], dma=clfB)
                    P.op("pool", MS(ccar[:], 0.0), writes=[cCB])
                    cumsum_tiles(pjs, clf[:, :, :], 32, ccar, cC[:, :, :], cs_tmp, csB, clfB, cCB, carry_out_ap=ccar[:, :])
                    cumsum_tiles(pjs, LF[:, j:j + 1, :], 1, ccar, Cn[:, :, :], cs_tmp, csB, LFB, cCB)
                    for h in range(2):
                        P.op("dve", TS(btS[:, h, 0:32], cC[:, :, h], -1.0, ccar[:, h:h + 1], ALU.mult, ALU.add),
                             reads=[cCB], writes=[btSB])
                        P.op("dve", TS(btS[:, h, 32:33], Cn[:, :, h], -1.0, ccar[:, h:h + 1], ALU.mult, ALU.add),
                             reads=[cCB], writes=[btSB])
                    bO = OB[k4 % 2][0]
                    qsl = slice(k4 * 128, k4 * 128 + 32)
                    for jt in range(33):
                        r3 = (k4 * 33 + jt) % 3
                        sk = stb.next()
                        if jt < 32:
                            P.op("sp", DMA(ckf[r3][:], ck_d[l, k4, jt * 128:(jt + 1) * 128, :]), writes=[ckfB[r3]], dma=ckfB[r3])
                            P.op("pool", DMA(cV[r3][:, :, 0:128], split(cv_d[l, k4, jt * 128:(jt + 1) * 128, :], 2)),
                                 writes=[cVB[r3]], dma=cVB[r3])
                            bk2 = pjs.next()
                            for h in range(2):
                                P.op("pe", TR(ps[bk2][:, h * 128:(h + 1) * 128], ckf[r3][:, h * 128:(h + 1) * 128], ID),
                                     reads=[ckfB[r3], Bc], writes=[psB[bk2]])
                            P.op("dve", TC(ckT[r3][:, :, :], split(ps[bk2][:, 0:256], 2)), reads=[psB[bk2]], writes=[ckTB[r3]])
                            for h in range(2):
                                P.op("pe", MM(ps[sk][:, h * 32:(h + 1) * 32], ckT[r3][:, h, :], qT[s][:, h, qsl],
                                              start=(h == 0), stop=True, skip=True),
                                     reads=[ckTB[r3], qTB[s]], writes=[psB[sk]])
                            for h in range(2):
                                P.op("act", ACT(PTs[r3][:, h * 32:(h + 1) * 32], ps[sk][:, h * 32:(h + 1) * 32], AF.Exp,
                                                btS[:, h, jt:jt + 1], SCALE_F), reads=[psB[sk], btSB], writes=[PTsB[r3]])
                            for h in range(2):
                                P.op("pe", MM(ps[bO][0:32, h * 129:(h + 1) * 129], PTs[r3][:, h * 32:(h + 1) * 32], cV[r3][:, h, :],
                                              start=(jt == 0 and h == 0), stop=False, skip=True),
                                     reads=[PTsB[r3], cVB[r3]], writes=[psB[bO]])
                        else:
                            for h in range(2):
                                P.op("pe", MM(ps[sk][0:32, h * 32:(h + 1) * 32], kT[:, h, j * 128:j * 128 + 32], qT[s][:, h, qsl],
                                              start=(h == 0), stop=True, skip=True),
                                     reads=[kTB[j], qTB[s]], writes=[psB[sk]])
                            for h in range(2):
                                P.op("act", ACT(PTs[r3][0:32, h * 32:(h + 1) * 32], ps[sk][0:32, h * 32:(h + 1) * 32], AF.Exp,
                                                btS[0:32, h, 32:33], SCALE_F), reads=[psB[sk], btSB], writes=[PTsB[r3]])
                                P.op("pool", TT(PTs[r3][0:32, h * 32:(h + 1) * 32], PTs[r3][0:32, h * 32:(h + 1) * 32],
                                                tri_b[0:32, 0:32], ALU.mult), reads=[PTsB[r3], Bc], writes=[PTsB[r3]])
                            for h in range(2):
                                P.op("pe", MM(ps[bO][0:32, h * 129:(h + 1) * 129], PTs[r3][0:32, h * 32:(h + 1) * 32],
                                              Vg[0:32, j, h, :], start=False, stop=True, skip=True),
                                     reads=[PTsB[r3], VB[j]], writes=[psB[bO]])
                    oi = j % 8
                    P.op("pool", MS(ost[oi][:], 0.0), writes=[ostB[oi]])
                    for h in range(2):
                        P.op("dve", RCP(rec[oi][0:32, h:h + 1], ps[bO][0:32, h * 129 + 128:h * 129 + 129]),
                             reads=[psB[bO]], writes=[ostB[oi]])
                        P.op("dve", STT(ost[oi][0:32, h * 128:(h + 1) * 128], ps[bO][0:32, h * 129:h * 129 + 128],
                                        rec[oi][0:32, h:h + 1], fgs[s][0:32, k4, h * 128:(h + 1) * 128], ALU.mult, ALU.mult),
                             reads=[psB[bO], fgsB[s], ostB[oi]], writes=[ostB[oi]])
                    oa_st.append(P.op("sp", DMA(osrcA[j * 128:(j + 1) * 128, :], ost[oi][:]), reads=[ostB[oi]], dma=ostB[oi]))

            def block(i):
                s = i % 2
                if i + 1 < NBLK:
                    load_hT_block(i + 1, hTb[1 - s], hTbB[1 - s])
                hb, hbB = hTb[s], hTbB[s]
                for h in range(2):
                    bk = pj.next()
                    for k in range(KC):
                        P.op("pe", MM(ps[bk][:, :], WFfm[:, k, h * 128:(h + 1) * 128], hb[:, k, :], start=(k == 0), stop=(k == KC - 1)),
                             reads=[WB_, hbB], writes=[psB[bk]])
                    P.op("dve", TC(qT[s][:, h, :], ps[bk][:, :]), reads=[psB[bk]], writes=[qTB[s]])
                for tt in range(4):
                    j = 4 * i + tt
                    ks = j % 2
                    tsl = slice(tt * 128, (tt + 1) * 128)
                    bk = pj.next()
                    for k in range(KC):
                        P.op("pe", MM(ps[bk][:, :], hb[:, k, tsl], WFtm[:, k, 0:512], start=(k == 0), stop=(k == KC - 1)),
                             reads=[WB_, hbB], writes=[psB[bk]])
                    P.op("dve", TC(kvf[ks][:], ps[bk][:, :]), reads=[psB[bk]], writes=[kvfB[ks]])
                    P.op("sp", DMA(ko_d[l, j * 128:(j + 1) * 128, :], kvf[ks][:, 0:256]), reads=[kvfB[ks]], dma=kvfB[ks])
                    P.op("sp", DMA(vo_d[l, j * 128:(j + 1) * 128, :], kvf[ks][:, 256:512]), reads=[kvfB[ks]], dma=kvfB[ks])
                    P.op("pool", TC(Vg[:, j, :, 0:128], split(kvf[ks][:, 256:512], 2)), reads=[kvfB[ks]], writes=[VB[j]])
                    bk = pj.next()
                    for k in range(KC):
                        P.op("pe", MM(ps[bk][:, 0:258], hb[:, k, tsl], WFtm[:, k, 512:770], start=(k == 0), stop=(k == KC - 1)),
                             reads=[WB_, hbB], writes=[psB[bk]])
                    bk2 = pj.next()
                    for h in range(2):
                        P.op("pe", TR(ps[bk2][:, h * 128:(h + 1) * 128], kvf[ks][:, h * 128:(h + 1) * 128], ID),
                             reads=[kvfB[ks], Bc], writes=[psB[bk2]])
                    P.op("dve", TC(kT[:, :, j * 128:(j + 1) * 128], split(ps[bk2][:, 0:256], 2)), reads=[psB[bk2]], writes=[kTB[j]])
                    silu2_from_psum(fgs[s][:, tt, :], ps[bk][:, 0:256], et[ks][:], etB[ks], psB[bk], fgsB[s])
                    P.op("dve", TT(lfpre[s][:, tt, :], ps[bk][:, 256:258], bfb[:], ALU.add), reads=[psB[bk], WB_], writes=[lfpB[s]])
                lfp2 = flat(lfpre[s][:, :, :])
                P.op("act", ACT(lfp2, lfp2, AF.Exp, ZERO, -1.0), reads=[lfpB[s], Bc], writes=[lfpB[s]])
                P.op("act", ACT(lfp2, lfp2, AF.Ln, ONE, 1.0), reads=[lfpB[s], Bc], writes=[lfpB[s]])
                P.op("pool", TS(flat(LF[:, 4 * i:4 * i + 4, :]), lfp2, -1.0, None, ALU.mult), reads=[lfpB[s]], writes=[LFB])
                if i == 16:
                    sample_attention(s)
                    return
                cumsum_tiles(pj, LF[:, 4 * i:4 * i + 4, :], 4, carry, C[:, 4 * i:4 * i + 4, :], cs_tmp, csB, LFB, CB,
                             carry_out_ap=carry[:, :])
                P.op("dve", TC(cref[:, i, :], cs_tmp[1][:, 0:8].rearrange("p (a b) -> p a b", b=2)[:, 1, :]), reads=[csB], writes=[CB])
                nkt = 4 * i + 4
                for h in range(2):
                    P.op("dve", TS(btab[s][:, h, 0:nkt], C[:, 0:nkt, h], -1.0, cref[:, i, h:h + 1], ALU.mult, ALU.add),
                         reads=[CB], writes=[btB[s]])
                for h in range(2):
                    bA, bB = OB[h]

                    def score(kt):
                        jj = kt - 4 * i
                        c0 = 0 if jj < 0 else 128 * jj
                        sk = stb.next()
                        P.op("pe", MM(ps[sk][:, c0:512], kT[:, h, kt * 128:(kt + 1) * 128], qT[s][:, h, c0:512]),
                             reads=[kTB[kt], qTB[s]], writes=[psB[sk]])
                        return sk

                    skq = [score(0)]
                    if nkt > 1:
                        skq.append(score(1))
                    for kt in range(nkt):
                        if kt + 2 < nkt:
                            skq.append(score(kt + 2))
                        sk = skq.pop(0)
                        jj = kt - 4 * i
                        c0 = 0 if jj < 0 else 128 * jj
                        pi = ptrr.next()
                        P.op("act", ACT(PT[pi][:, c0:512], ps[sk][:, c0:512], AF.Exp, btab[s][:, h, kt:kt + 1], SCALE_F),
                             reads=[psB[sk], btB[s]], writes=[PTB[pi]])
                        if jj >= 0:
                            P.op("pool", TT(PT[pi][:, c0:c0 + 128], PT[pi][:, c0:c0 + 128], tri_b[:], ALU.mult),
                                 reads=[PTB[pi], Bc], writes=[PTB[pi]])
                        for sub in range(max(jj, 0), 4):
                            bo = bA if sub < 2 else bB
                            oc = (sub % 2) * 129
                            P.op("pe", MM(ps[bo][:, oc:oc + 129], PT[pi][:, sub * 128:(sub + 1) * 128], Vg[:, kt, h, :],
                                          start=(kt == 0 and sub % 2 == 0), stop=(kt == 4 * i + sub), skip=True),
                                 reads=[PTB[pi], VB[kt]], writes=[psB[bo]])
                    for sub in range(4):
                        j = 4 * i + sub
                        bo = bA if sub < 2 else bB
                        oc = (sub % 2) * 129
                        oi = j % 8
                        P.op("dve", RCP(rec[oi][:, h:h + 1], ps[bo][:, oc + 128:oc + 129]), reads=[psB[bo]], writes=[ostB[oi]])
                        P.op("dve", STT(ost[oi][:, h * 128:(h + 1) * 128], ps[bo][:, oc:oc + 128], rec[oi][:, h:h + 1],
                                        fgs[s][:, sub, h * 128:(h + 1) * 128], ALU.mult, ALU.mult),
                             reads=[psB[bo], fgsB[s], ostB[oi]], writes=[ostB[oi]])
                for sub in range(4):
                    j = 4 * i + sub
                    oi = j % 8
                    oa_st.append(P.op("sp", DMA(osrcA[j * 128:(j + 1) * 128, :], ost[oi][:]), reads=[ostB[oi]], dma=ostB[oi]))

            oa_st = []
            jobs = cast_jobs(l)
            load_hT_block(0, hTb[0], hTbB[0])
            for i in range(NBLK):
                block(i)
                emit_hw_casts((l, "G"), 2)
                emit_casts(l, jobs, 5)
            for q in range(5):
                ag_chunk(osrcA, oagA, oagAB, q, NTOK, 2048, oa_st[16 * q:16 * q + 16])
            emit_casts(l, jobs, len(jobs))
            P.op("sp", DMA(lfo_d[l], LF[:, :, :]), reads=[LFB], dma=LFB)
            P.barrier()

        def pass_g(l):
            A.reset(base_mark)
            WGfm = A.alloc("WGfm", [128, KC, 528], BF16)
            WGtm = A.alloc("WGtm", [128, KC, 512], BF16)
            WB_ = P.buf("WG")
            load_w_bf(WGfm, wgfm_b[l], WB_, (l, "G"))
            load_w_bf(WGtm, wgtm_b[l], WB_, (l, "G"))
            wa2 = A.alloc("wa2", [16, 256], BF16)
            wa2B = P.buf("wa2")
            P.op("pool", DMA(wa2[:], wa2_d[l]), writes=[wa2B], dma=wa2B)
            nba = A.alloc("nba", [128, 2], F32)
            gain = A.alloc("gain", [128, 256], F32)
            P.op("sp", DMA(nba[:], ba_d[l]), writes=[WB_], dma=WB_)
            P.op("sp", DMA(gain[:], gain_d[l]), writes=[WB_], dma=WB_)
            P.op("pool", TS(nba[:], nba[:], -1.0, None, ALU.mult), reads=[WB_], writes=[WB_])
            hTb = [A.alloc("hTb", [128, KC, 512], BF16) for _ in range(2)]
            hTbB = P.bufs(2, "hTb")
            glrT = A.alloc("glrT", [16, 512], BF16)
            glB = P.buf("glrT")
            et2 = A.alloc("et2", [128, 512], F32)
            et2B = P.buf("et2")
            sp_ = A.alloc("sp", [128, 2, 512], F32)
            spB = P.buf("sp")
            bpos = A.alloc("bpos", [128, 2, 512], F32)
            bpB = P.buf("bpos")
            ebT = A.alloc("ebT", [128, 2, 512], BF16)
            enbT = A.alloc("enbT", [128, 2, 512], BF16)
            ebB = P.buf("eb")
            ebl = [A.alloc("ebl", [128, 2, 4], F32) for _ in range(2)]
            eblB = P.bufs(2, "ebl")
            qtl = [A.alloc("qtl", [128, 2, 512], BF16) for _ in range(2)]
            qtB = P.bufs(2, "qtl")
            ktf = [A.alloc("ktf", [128, 2, 512], F32) for _ in range(2)]
            ktfB = P.bufs(2, "ktf")
            ktl = [A.alloc("ktl", [128, 2, 512], BF16) for _ in range(2)]
            ktlB = P.bufs(2, "ktl")
            gvb = [A.alloc("gvb", [128, 256], BF16) for _ in range(8)]
            gvB = P.bufs(8, "gvb")
            ggs = [A.alloc("ggs", [128, 256], BF16) for _ in range(8)]
            ggB = P.bufs(8, "ggs")
            etg = [A.alloc("etg", [128, 256], F32) for _ in range(2)]
            etgB = P.bufs(2, "etg")
            kttm = [A.alloc("kttm", [128, 256], BF16) for _ in range(2)]
            kttmB = P.bufs(2, "kttm")
            AT = [A.alloc("AT", [128, 128], BF16) for _ in range(2)]
            ATB = P.bufs(2, "AT")
            osb = [A.alloc("osb", [128, 256], F32) for _ in range(2)]
            osbB = P.bufs(2, "osb")
            junk = A.alloc("junkg", [128, 256], BF16)
            ssq = [A.alloc("ssqg", [128, 4], F32) for _ in range(2)]
            t1 = [A.alloc("t1g", [128, 256], F32) for _ in range(2)]
            obst = [A.alloc("obst", [128, 256], BF16) for _ in range(4)]
            obB = P.bufs(4, "obst")
            S = A.alloc("S", [128, 2, 256], F32)
            SB_ = P.buf("S")
            Sb = A.alloc("Sb", [128, 2, 256], BF16)
            SbB = P.buf("Sb")
            tmpS = A.alloc("tmpS", [128, 2, 256], F32)
            tSB = P.buf("tmpS")
            P.op("pool", MS(flat(S[:, :, :]), 0.0), writes=[SB_])
            P.op("pool", MS(flat(Sb[:, :, :]), 0.0), writes=[SbB])
            ostores = []

            def proj(i):
                s = i % 2
                hb, hbB = hTb[s], hTbB[s]
                bk = psrr.next()
                for k in range(KC):
                    P.op("pe", MM(ps[bk][0:16, :], WGfm[:, k, 512:528], hb[:, k, :], start=(k == 0), stop=(k == KC - 1)),
                         reads=[WB_, hbB], writes=[psB[bk]])
                P.op("act", CP(glrT[:, :], ps[bk][0:16, :]), reads=[psB[bk]], writes=[glB])
                for c in range(2):
                    bk = psrr.next()
                    P.op("pe", MM(ps[bk][:, :], wa2[:, c * 128:(c + 1) * 128], glrT[:, :]), reads=[wa2B, glB], writes=[psB[bk]])
                    P.op("act", ACT(et2[:], ps[bk][:, :], AF.Exp, nba[:, c:c + 1], -1.0), reads=[psB[bk], WB_], writes=[et2B])
                    P.op("act", ACT(sp_[:, c, :], et2[:], AF.Ln, ONE, 1.0), reads=[et2B, Bc], writes=[spB])
                    P.op("dve", SCAN(bpos[:, c, :], resetm[:], sp_[:, c, :], 0.0), reads=[spB, Bc], writes=[bpB])
                bp2 = flat(bpos[:, :, :])
                P.op("act", ACT(flat(ebT[:, :, :]), bp2, AF.Exp, NL16, -1.0 / 16.0), reads=[bpB, Bc], writes=[ebB])
                P.op("act", ACT(flat(enbT[:, :, :]), bp2, AF.Exp, ZERO, 1.0 / 16.0), reads=[bpB, Bc], writes=[ebB])
                lastc = 31 if i == 16 else 127
                for c in range(2):
                    P.op("act", ACT(ebl[s][:, c, :], split(bpos[:, c, :], 4)[:, :, lastc], AF.Exp, ZERO, -1.0 / 16.0),
                         reads=[bpB, Bc], writes=[eblB[s]])
                for c in range(2):
                    bk = psrr.next()
                    for k in range(KC):
                        P.op("pe", MM(ps[bk][:, :], WGfm[:, k, c * 128:(c + 1) * 128], hb[:, k, :], start=(k == 0), stop=(k == KC - 1)),
                             reads=[WB_, hbB], writes=[psB[bk]])
                    P.op("dve", TT(qtl[s][:, c, :], ps[bk][:, :], ebT[:, c, :], ALU.mult), reads=[psB[bk], ebB], writes=[qtB[s]])
                for c in range(2):
                    bk = psrr.next()
                    for k in range(KC):
                        P.op("pe", MM(ps[bk][:, :], WGfm[:, k, 256 + c * 128:256 + (c + 1) * 128], hb[:, k, :],
                                      start=(k == 0), stop=(k == KC - 1)), reads=[WB_, hbB], writes=[psB[bk]])
                    P.op("dve", TT(ktf[s][:, c, :], ps[bk][:, :], enbT[:, c, :], ALU.mult), reads=[psB[bk], ebB], writes=[ktfB[s]])
                P.op("act", CP(flat(ktl[s][:, :, :]), flat(ktf[s][:, :, :])), reads=[ktfB[s]], writes=[ktlB[s]])
                for tt in range(4):
                    j = 4 * i + tt
                    tsl = slice(tt * 128, (tt + 1) * 128)
                    g8, g2 = j % 8, j % 2
                    bk = psrr.next()
                    for k in range(KC):
                        P.op("pe", MM(ps[bk][:, :], hb[:, k, tsl], WGtm[:, k, :], start=(k == 0), stop=(k == KC - 1)),
                             reads=[WB_, hbB], writes=[psB[bk]])
                    P.op("act", CP(gvb[g8][:], ps[bk][:, 0:256]), reads=[psB[bk]], writes=[gvB[g8]])
                    silu2_from_psum(ggs[g8][:], ps[bk][:, 256:512], etg[g2][:], etgB[g2], psB[bk], ggB[g8])

            def recur(i):
                s = i % 2
                for tt in range(4):
                    j = 4 * i + tt
                    tsl = slice(tt * 128, (tt + 1) * 128)
                    g8, g4, g2 = j % 8, j % 4, j % 2
                    if i == 16:
                        P.op("sp", DMA(S[:, :, :], sg_d[l, tt]), writes=[SB_], dma=SB_)
                        P.op("act", CP(flat(Sb[:, :, :]), flat(S[:, :, :])), reads=[SB_], writes=[SbB])
                    bk = psrr.next()
                    for c in range(2):
                        P.op("pe", TR(ps[bk][:, c * 128:(c + 1) * 128], ktf[s][:, c, tsl], ID), reads=[ktfB[s], Bc], writes=[psB[bk]])
                    P.op("act", CP(kttm[g2][:], ps[bk][:, 0:256]), reads=[psB[bk]], writes=[kttmB[g2]])
                    bk = psrr.next()
                    for c in range(2):
                        P.op("pe", MM(ps[bk][:, 0:128], ktl[s][:, c, tsl], qtl[s][:, c, tsl], start=(c == 0), stop=(c == 1)),
                             reads=[ktlB[s], qtB[s]], writes=[psB[bk]])
                    P.op("dve", TT(AT[g2][:], ps[bk][:, 0:128], tri_f[:], ALU.mult), reads=[psB[bk], Bc], writes=[ATB[g2]])
                    bk = psrr.next()
                    P.op("pe", MM(ps[bk][:, 0:256], AT[g2][:], gvb[g8][:], start=True, stop=False),
                         reads=[ATB[g2], gvB[g8]], writes=[psB[bk]])
                    for c in range(2):
                        P.op("pe", MM(ps[bk][:, 0:256], qtl[s][:, c, tsl], Sb[:, c, :], start=False, stop=(c == 1)),
                             reads=[qtB[s], SbB], writes=[psB[bk]])
                    P.op("act", CP(osb[g2][:], ps[bk][:, 0:256]), reads=[psB[bk]], writes=[osbB[g2]])
                    bk = psrr.next()
                    for c in range(2):
                        P.op("pe", MM(ps[bk][:, c * 256:(c + 1) * 256], kttm[g2][:, c * 128:(c + 1) * 128], gvb[g8][:],
                                      start=(c == 0), stop=True, skip=True), reads=[kttmB[g2], gvB[g8]], writes=[psB[bk]])
                    P.op("dve", TT(flat(tmpS[:, :, :]), ps[bk][:, :], flat(S[:, :, :]), ALU.add), reads=[psB[bk], SB_], writes=[tSB])
                    for c in range(2):
                        P.op("dve", TS(S[:, c, :], tmpS[:, c, :], ebl[s][:, c, tt:tt + 1], None, ALU.mult), reads=[tSB, eblB[s]], writes=[SB_])
                    P.op("act", CP(flat(Sb[:, :, :]), flat(S[:, :, :])), reads=[SB_], writes=[SbB])
                    if i == 16:
                        P.op("sp", DMA(gs_d[l, tt], S[:, :, :]), reads=[SB_], dma=SB_)
                    elif j == 63:
                        P.op("sp", DMA(gp_d[l], S[:, :, :]), reads=[SB_], dma=SB_)
                    P.op("dve", STT(junk[:], osb[g2][:], 1.0, osb[g2][:], ALU.mult, ALU.mult, accum=ssq[g2][:, 0:1]),
                         reads=[osbB[g2]], writes=[osbB[g2]])
                    rstd_of(ssq[g2], 256, osbB[g2], lnbias=LNH)
                    P.op("dve", STT(t1[g2][:], osb[g2][:], ssq[g2][:, 2:3], gain[:], ALU.mult, ALU.mult),
                         reads=[osbB[g2], WB_], writes=[osbB[g2]])
                    P.op("pool", TT(obst[g4][:], t1[g2][:], ggs[g8][:], ALU.mult), reads=[osbB[g2], ggB[g8]], writes=[obB[g4]])
                    ostores.append(P.op("sp", DMA(osrcB[j * 128:(j + 1) * 128, :], obst[g4][:]), reads=[obB[g4]], dma=obB[g4]))

            load_hT_block(0, hTb[0], hTbB[0])
            load_hT_block(1, hTb[1], hTbB[1])
            proj(0)
            for i in range(NBLK):
                if i + 1 < NBLK:
                    proj(i + 1)
                    if i + 2 < NBLK:
                        load_hT_block(i + 2, hTb[i % 2], hTbB[i % 2])
                recur(i)
                if l + 1 < DEPTH:
                    emit_hw_casts((l + 1, "F"), 1)
                    emit_hw_casts((l + 1, "G"), 1)
                if i >= 4 and i % 4 == 0:
                    q = i // 4 - 1
                    ag_chunk(osrcB, oagB_d, oagBB, q, NTOK, 2048, ostores[0:16])
                    del ostores[0:16]
            ag_chunk(osrcB, oagB_d, oagBB, 4, NTOK, 2048, list(ostores))
            P.barrier()

        def token_phase(l):
            last = (l == DEPTH - 1)
            A.reset(base_mark)
            hTg = A.alloc("hTg", [128, KC, 512], BF16)
            hTgB = P.buf("hTg")
            oT = A.alloc("oT", [128, KC, 512], BF16)
            oTB = P.buf("oT")
            mT = A.alloc("mT", [128, KC, 512], BF16)
            mTB = P.buf("mT")
            z = A.alloc("z", [128, 4, D], F32)
            zB = P.bufs(4, "z")
            ogf = A.alloc("ogf", [128, D], F32)
            ogB = P.buf("ogf")
            wr = [dict(ma=A.alloc("wma", [128, KC, 128], BF16), mb=A.alloc("wmb", [128, KC, 128], BF16),
                       oa=A.alloc("woa", [128, 8, 128], BF16), ob=A.alloc("wob", [128, 8, 128], BF16)) for _ in range(2)]
            wrB = P.bufs(2, "wr")
            wo = [A.alloc("wo", [128, KC, 256], BF16) for _ in range(2)]
            woB = P.bufs(2, "wo")
            xt = A.alloc("xt", [128, D], F32)
            xB = P.buf("xt")
            yt = A.alloc("yt", [128, D], F32)
            yB = P.buf("yt")
            t1 = A.alloc("t1", [128, D], F32)
            tB1 = P.buf("t1")
            ssqz = A.alloc("ssqz", [128, 4], F32)
            post_t = A.alloc("post_t", [128, D], F32)
            pre_t = A.alloc("pre_t", [128, D], F32)
            ppB = P.buf("pp")
            P.op("sp", DMA(post_t[:], post_d[l]), writes=[ppB], dma=ppB)
            if not last:
                P.op("sp", DMA(pre_t[:], pre_d[l + 1]), writes=[ppB], dma=ppB)
            htmp = (A.alloc("junk", [128, D], BF16), A.alloc("ssq", [128, 4], F32), A.alloc("hf", [128, D], F32),
                    A.alloc("hTst", [128, KC, 128], BF16), P.buf("hTst"), P.buf("htmp"))
            ge = [A.alloc("ge", [128, 512], F32) for _ in range(4)]
            geB = P.bufs(4, "ge")
            gm = [A.alloc("gm", [128, 512], F32) for _ in range(4)]
            gmB = P.bufs(4, "gm")
            src_x = x_d if l == 0 else yscr

            wdeps = wcast_ops[l]

            def load_wr(c, sl):
                w, B = wr[sl], wrB[sl]
                P.op("sp", DMA(flat(w["ma"][:, :, :]), wm_b[l, c]), writes=[B], dma=B, extra_deps=wdeps)
                P.op("sp", DMA(flat(w["mb"][:, :, :]), wm_b[l, 16 + c]), writes=[B], dma=B, extra_deps=wdeps)
                P.op("sp", DMA(flat(w["oa"][:, :, :]), woa_b[l, c]), writes=[B], dma=B, extra_deps=wdeps)
                P.op("sp", DMA(flat(w["ob"][:, :, :]), wob_b[l, c]), writes=[B], dma=B, extra_deps=wdeps)

            def load_wo(blk, sl):
                P.op("sp", DMA(flat(wo[sl][:, :, :]), wout_b[l, blk]), writes=[woB[sl]], dma=woB[sl], extra_deps=wdeps)

            def stage_a_tile(grp, gi):
                t = grp[gi]
                for r in range(4):
                    for (src_, B_, c0_) in ((oagA, oagAB[t // 4], 0), (oagB_d, oagBB[t // 4], 256)):
                        P.op("pool", (lambda o_, i_, s_: (lambda e: e.indirect_dma_start(
                            out=o_, out_offset=None, in_=s_[:, :],
                            in_offset=bass.IndirectOffsetOnAxis(ap=i_, axis=0))))(
                                ogf[:, r * 512 + c0_:r * 512 + c0_ + 256], idx_t[:, t * 4 + r:t * 4 + r + 1], src_),
                            reads=[B_, Bc], writes=[ogB], dma=ogB)
                for q in range(4):
                    bk = psrr.next()
                    for jx in range(4):
                        c = q * 4 + jx
                        if c < 8:
                            col = (c // 2) * 512 + (c % 2) * 128
                        else:
                            cc = c - 8
                            col = (cc // 2) * 512 + 256 + (cc % 2) * 128
                        P.op("pe", TR(ps[bk][:, jx * 128:(jx + 1) * 128], ogf[:, col:col + 128], ID),
                             reads=[ogB, Bc], writes=[psB[bk]])
                    P.op("act", CP(oT[:, q * 4:(q + 1) * 4, gi * 128:(gi + 1) * 128], split(ps[bk][:, :], 4)),
                         reads=[psB[bk]], writes=[oTB])
                P.op("sp", DMA(hTg[:, :, gi * 128:(gi + 1) * 128], split(hsrc[t * 128:(t + 1) * 128, :], KC)),
                     reads=[hsrcB[t]], writes=[hTgB], dma=hTgB)

            def stage_b(grp, inter):
                ntok = len(grp) * 128
                load_wr(0, 0)
                for c in range(16):
                    sl = c % 2
                    if c + 1 < 16:
                        load_wr(c + 1, 1 - sl)
                    w, wB = wr[sl], wrB[sl]
                    bks = [psrr.next() for _ in range(4)]
                    for k in range(KC):
                        P.op("pe", MM(ps[bks[0]][:, 0:ntok], w["ma"][:, k, :], hTg[:, k, 0:ntok], start=(k == 0), stop=(k == KC - 1)),
                             reads=[wB, hTgB], writes=[psB[bks[0]]])
                    for k in range(KC):
                        P.op("pe", MM(ps[bks[1]][:, 0:ntok], w["mb"][:, k, :], hTg[:, k, 0:ntok], start=(k == 0), stop=(k == KC - 1)),
                             reads=[wB, hTgB], writes=[psB[bks[1]]])
                    for k in range(8):
                        P.op("pe", MM(ps[bks[2]][:, 0:ntok], w["oa"][:, k, :], oT[:, k, 0:ntok], start=(k == 0), stop=(k == 7)),
                             reads=[wB, oTB], writes=[psB[bks[2]]])
                    for k in range(8):
                        P.op("pe", MM(ps[bks[3]][:, 0:ntok], w["ob"][:, k, :], oT[:, 8 + k, 0:ntok], start=(k == 0), stop=(k == 7)),
                             reads=[wB, oTB], writes=[psB[bks[3]]])
                    for u in range(2):
                        gi_ = u + 2 * (c % 2)
                        gu = ge[gi_][:, 0:ntok]
                        P.op("act", ACT(gu, ps[bks[u]][:, 0:ntok], AF.Tanh, ZERO, 0.5), reads=[psB[bks[u]], Bc], writes=[geB[gi_]])
                        P.op("dve", STT(gm[gi_][:, 0:ntok], gu, 1.0, ps[bks[2 + u]][:, 0:ntok], ALU.add, ALU.mult),
                             reads=[psB[bks[2 + u]], geB[gi_]], writes=[gmB[gi_]])
                    g0, g1 = 2 * (c % 2), 1 + 2 * (c % 2)
                    P.op("dve", TT(mT[:, c, 0:ntok], gm[g0][:, 0:ntok], gm[g1][:, 0:ntok], ALU.add), reads=[gmB[g0], gmB[g1]], writes=[mTB])
                    if c in inter:
                        inter[c]()

            def stage_c(grp, inter):
                ng = len(grp)
                load_wo(0, 0)
                for blk in range(8):
                    sl = blk % 2
                    if blk + 1 < 8:
                        load_wo(blk + 1, 1 - sl)
                    for gi in range(ng):
                        bk = psrr.next()
                        for k in range(KC):
                            P.op("pe", MM(ps[bk][:, 0:256], mT[:, k, gi * 128:(gi + 1) * 128], wo[sl][:, k, :],
                                          start=(k == 0), stop=(k == KC - 1)), reads=[mTB, woB[sl]], writes=[psB[bk]])
                        P.op("act", MUL(z[:, gi, blk * 256:(blk + 1) * 256], ps[bk][:, 0:256], 0.5), reads=[psB[bk]], writes=[zB[gi]])
                    if blk in inter:
                        inter[blk]()

            def stage_d_tile(grp, gi):
                t = grp[gi]
                P.op("pool", DMA(xt[:], src_x[t * 128:(t + 1) * 128, :]), writes=[xB], dma=xB)
                P.op("dve", STT(htmp[0][:], z[:, gi, :], 1.0, z[:, gi, :], ALU.mult, ALU.mult, accum=ssqz[:, 0:1]),
                     reads=[zB[gi]], writes=[tB1, htmp[5]])
                rstd_of(ssqz, D, tB1)
                P.op("dve", STT(t1[:], z[:, gi, :], ssqz[:, 2:3], post_t[:], ALU.mult, ALU.mult), reads=[zB[gi], tB1, ppB], writes=[tB1])
                P.op("dve", TT(yt[:], t1[:], xt[:], ALU.add), reads=[tB1, xB], writes=[yB])
                if last:
                    P.op("pool", DMA(y_d[t * 128:(t + 1) * 128, :], yt[:]), reads=[yB], dma=yB)
                else:
                    P.op("pool", DMA(yscr[t * 128:(t + 1) * 128, :], yt[:]), reads=[yB], dma=yB)
                    hst[t] = emit_h(yt, yB, pre_t, ppB, t, htmp)

            def spread(fns, slots):
                m = {}
                for n_, f in enumerate(fns):
                    sl_ = slots[min(n_, len(slots) - 1)]
                    m.setdefault(sl_, []).append(f)
                return {k_: (lambda fs=v_: [f() for f in fs]) for k_, v_ in m.items()}

            hst = {}
            G = len(TGROUPS)
            for gi in range(len(TGROUPS[0])):
                stage_a_tile(TGROUPS[0], gi)
            for g in range(G):
                grp = TGROUPS[g]
                d_fns = [] if g == 0 else [(lambda pg=TGROUPS[g - 1], x_=x: stage_d_tile(pg, x_)) for x in range(len(TGROUPS[g - 1]))]
                stage_b(grp, spread(d_fns, [3, 7, 11, 15]))
                a_fns = [] if g + 1 == G else [(lambda ng_=TGROUPS[g + 1], x_=x: stage_a_tile(ng_, x_)) for x in range(len(TGROUPS[g + 1]))]
                stage_c(grp, spread(a_fns, [1, 3, 5, 7]))
            for x in range(len(TGROUPS[G - 1])):
                stage_d_tile(TGROUPS[G - 1], x)
            if not last:
                for q in range(9):
                    ag_chunk(hsrc, hag, hagB, q, TL * 128, 256, [hst[t] for t in range(2 * q, min(2 * q + 2, TL))])
            P.barrier()

        def run_all():
            prologue()
            if STOP_AFTER == "pro":
                return
            for l in range(DEPTH):
                pass_f(l)
                if STOP_AFTER == "F%d" % l:
                    return
                pass_g(l)
                if STOP_AFTER == "G%d" % l:
                    return
                if STOP_AFTER == "AG%d" % l:
                    return
                token_phase(l)
                if STOP_AFTER == "T%d" % l:
                    return

        run_all()
        P.barrier()
        stats = P.emit()
        stats["sbuf_peak"] = A.peak
    return nc, stats


_OFF = dict(fq=0, fk=1024, fv=2048, ff=3072, fg=3080, gq=4104, gk=5128, gv=6152, glr=7176, gg=7192, ma=8216, mb=10264)


def _pkc(w):
    K = w.shape[0] // 128
    return np.ascontiguousarray(w.reshape(K, 128, w.shape[1]).transpose(1, 0, 2))


def _chunks(w, cw):
    K = w.shape[0] // 128
    NC = w.shape[1] // cw
    return np.ascontiguousarray(w.reshape(K, 128, NC, cw).transpose(2, 1, 0, 3))


def _prep(inp):
    f = lambda a: np.asarray(a, dtype=np.float32)
    x_prompt, x_sample = f(inp["x_prompt"]), f(inp["x_sample"])
    cache_k, cache_v, cache_logf, state_gla = f(inp["cache_k"]), f(inp["cache_v"]), f(inp["cache_logf"]), f(inp["state_gla"])
    w_in, w_a2, b_a, b_f = f(inp["w_in"]), f(inp["w_a2"]), f(inp["b_a"]), f(inp["b_f"])
    gla_gain, w_oa, w_ob, w_out = f(inp["gla_gain"]), f(inp["w_oa"]), f(inp["w_ob"]), f(inp["w_out"])
    pre_norm, post_norm = f(inp["pre_norm"]), f(inp["post_norm"])

    shared = {}
    shared["wm"] = np.stack([_chunks(w_in[l][:, _OFF["ma"]:_OFF["ma"] + 4096], 128) for l in range(DEPTH)])
    shared["woa"] = np.stack([_chunks(w_oa[l], 128) for l in range(DEPTH)])
    shared["wob"] = np.stack([_chunks(w_ob[l], 128) for l in range(DEPTH)])
    shared["wout"] = np.stack([_chunks(w_out[l], 256) for l in range(DEPTH)])
    shared["gain"] = np.ascontiguousarray(np.broadcast_to(gla_gain[:, None, :], (DEPTH, 128, 256)))
    shared["pre"] = np.ascontiguousarray(np.broadcast_to(pre_norm[:, None, :], (DEPTH, 128, D)))
    shared["post"] = np.ascontiguousarray(np.broadcast_to(post_norm[:, None, :], (DEPTH, 128, D)))

    maps = []
    for c in range(8):
        b, g = c // 4, c % 4
        m = dict(shared)
        xl = np.zeros((TL * 128, D), np.float32)
        for k in range(16):
            xl[k * 128:(k + 1) * 128] = x_prompt[b, (4 * k + g) * 128:(4 * k + g + 1) * 128]
        xl[2048:2080] = x_sample[4 * b + g]
        m["x_loc"] = xl
        fs = slice(256 * g, 256 * g + 256)

        def cols(l, name, sl=fs):
            o = _OFF[name]
            return w_in[l][:, o + sl.start:o + sl.stop]

        m["wffm"] = np.stack([_pkc(cols(l, "fq")) for l in range(DEPTH)])
        m["wftm"] = np.stack([_pkc(np.concatenate([cols(l, "fk"), cols(l, "fv"), cols(l, "fg"),
                                                   cols(l, "ff", slice(2 * g, 2 * g + 2))], 1)) for l in range(DEPTH)])
        m["wgfm"] = np.stack([_pkc(np.concatenate([cols(l, "gq"), cols(l, "gk"), cols(l, "glr", slice(0, 16))], 1))
                              for l in range(DEPTH)])
        m["wgtm"] = np.stack([_pkc(np.concatenate([cols(l, "gv"), cols(l, "gg")], 1)) for l in range(DEPTH)])
        m["wa2"] = np.ascontiguousarray(w_a2[:, :, fs])
        m["ba"] = np.ascontiguousarray(b_a[:, fs].reshape(DEPTH, 2, 128).transpose(0, 2, 1))
        m["bfb"] = np.ascontiguousarray(np.broadcast_to(b_f[:, None, 2 * g:2 * g + 2], (DEPTH, 128, 2)))
        sbs = slice(4 * b, 4 * b + 4)
        m["ck"] = np.ascontiguousarray(cache_k[:, sbs, :, 2 * g:2 * g + 2, :].reshape(DEPTH, 4, 4096, 256))
        m["cv"] = np.ascontiguousarray(cache_v[:, sbs, :, 2 * g:2 * g + 2, :].reshape(DEPTH, 4, 4096, 256))
        m["clf"] = np.ascontiguousarray(cache_logf[:, sbs, :, 2 * g:2 * g + 2].reshape(DEPTH, 4, 32, 128, 2).transpose(0, 1, 3, 2, 4))
        m["sg"] = np.ascontiguousarray(state_gla[:, sbs, g].reshape(DEPTH, 4, 2, 128, 256).transpose(0, 1, 3, 2, 4))
        idx = np.zeros((128, TL * 4), np.int32)
        p = np.arange(128)
        for t in range(TL):
            tok = ((4 * t + g) * 128 + p) if t < 16 else (8192 + 128 * g + p)
            q, within = tok // 2048, tok % 2048
            nq = np.where(q < 4, 2048, 512)
            for r in range(4):
                idx[:, t * 4 + r] = q * 8192 + r * nq + within
        m["idx"] = idx
        maps.append(m)
    return maps


_CACHE = {}


def kernel(**inputs):
    if "nc" not in _CACHE:
        _CACHE["nc"], _CACHE["stats"] = build_program()
    nc = _CACHE["nc"]
    maps = _prep(inputs)
    res = run_bass_kernel_spmd(nc, maps, core_ids=list(range(8)))
    R = res.results
    B, SEQ, DB, DS = 2, 8192, 8, 32
    y_prompt = np.zeros((B, SEQ, D), np.float32)
    y_sample = np.zeros((DB, DS, D), np.float32)
    k_prompt = np.zeros((DEPTH, B, SEQ, 8, 128), np.float32)
    v_prompt = np.zeros((DEPTH, B, SEQ, 8, 128), np.float32)
    logf_prompt = np.zeros((DEPTH, B, SEQ, 8), np.float32)
    gla_prompt = np.zeros((DEPTH, B, 4, 256, 256), np.float32)
    k_sample = np.zeros((DEPTH, DB, DS, 8, 128), np.float32)
    v_sample = np.zeros((DEPTH, DB, DS, 8, 128), np.float32)
    logf_sample = np.zeros((DEPTH, DB, DS, 8), np.float32)
    gla_sample = np.zeros((DEPTH, DB, 4, 256, 256), np.float32)
    for c in range(8):
        b, g = c // 4, c % 4
        r = R[c]
        y = np.asarray(r["y_loc"])
        for k in range(16):
            y_prompt[b, (4 * k + g) * 128:(4 * k + g + 1) * 128] = y[k * 128:(k + 1) * 128]
        y_sample[4 * b + g] = y[2048:2080]
        ko, vo = np.asarray(r["k_out"]), np.asarray(r["v_out"])
        lf = np.asarray(r["lf_out"]).transpose(0, 2, 1, 3).reshape(DEPTH, NTOK, 2)
        gp, gs = np.asarray(r["gla_p"]), np.asarray(r["gla_s"])
        for l in range(DEPTH):
            k_prompt[l, b, :, 2 * g:2 * g + 2, :] = ko[l, :SEQ].reshape(SEQ, 2, 128)
            v_prompt[l, b, :, 2 * g:2 * g + 2, :] = vo[l, :SEQ].reshape(SEQ, 2, 128)
            logf_prompt[l, b, :, 2 * g:2 * g + 2] = lf[l, :SEQ]
            gla_prompt[l, b, g] = gp[l].transpose(1, 0, 2).reshape(256, 256)
            for k4 in range(4):
                o = SEQ + 128 * k4
                k_sample[l, 4 * b + k4, :, 2 * g:2 * g + 2, :] = ko[l, o:o + DS].reshape(DS, 2, 128)
                v_sample[l, 4 * b + k4, :, 2 * g:2 * g + 2, :] = vo[l, o:o + DS].reshape(DS, 2, 128)
                logf_sample[l, 4 * b + k4, :, 2 * g:2 * g + 2] = lf[l, o:o + DS]
                gla_sample[l, 4 * b + k4, g] = gs[l, k4].transpose(1, 0, 2).reshape(256, 256)
    return (y_prompt, y_sample, k_prompt, v_prompt, logf_prompt, gla_prompt,
            k_sample, v_sample, logf_sample, gla_sample)
```

```python
import math
import numpy as np
from contextlib import ExitStack
import concourse.bass as bass
import concourse.mybir as mybir
from concourse.bass_utils import run_bass_kernel_spmd

F32, BF16, I32 = mybir.dt.float32, mybir.dt.bfloat16, mybir.dt.int32
AF = mybir.ActivationFunctionType
ALU = mybir.AluOpType

D = 2048
KC = 16
NT = 68
NBLK = 17
TL = 17
NTOK = NT * 128
DEPTH = 2
SCALE_F = 128 ** -0.5
GROUPS = [[0, 1, 2, 3], [4, 5, 6, 7]]
STOP_AFTER = None
TGROUPS = [[0, 1, 2, 3], [4, 5, 6, 7], [8, 9, 10, 11], [12, 13, 14], [15, 16]]
SB_BASE = 16512
SB_END = 229344


class Buf:
    __slots__ = ("name", "w", "r", "dsem", "dcnt")

    def __init__(self, name):
        self.name = name
        self.w = None
        self.r = []
        self.dsem = None
        self.dcnt = 0


class Op:
    __slots__ = ("eng", "fn", "deps", "dma", "token", "mark", "pos")

    def __init__(self, eng, fn, deps, dma):
        self.eng = eng
        self.fn = fn
        self.deps = deps
        self.dma = dma
        self.token = None
        self.mark = False
        self.pos = 0


class Prog:
    ENG = ("pe", "act", "dve", "pool", "sp")

    def __init__(self, nc, stack):
        self.nc = nc
        self.stack = stack
        self.ops = []
        self.h = {"pe": nc.tensor, "act": nc.scalar, "dve": nc.vector, "pool": nc.gpsimd, "sp": nc.sync}
        self.esem = {e: stack.enter_context(nc.semaphore("es_" + e)) for e in self.ENG}
        self.ccsem = stack.enter_context(nc.semaphore("ccsem"))
        self.cccnt = 0
        self.last = {e: None for e in self.ENG}
        self.pending = []
        self.nbuf = 0
        self.nsem = 6

    def buf(self, name=None):
        self.nbuf += 1
        return Buf("%s_%d" % (name or "b", self.nbuf))

    def bufs(self, n, name="b"):
        return [self.buf(name) for _ in range(n)]

    def _dsem(self, b):
        if b.dsem is None:
            b.dsem = self.stack.enter_context(self.nc.semaphore("ds_" + b.name))
            self.nsem += 1
        return b.dsem

    def op(self, eng, fn, reads=(), writes=(), dma=None, cc=False, extra_deps=(), nobarrier=False):
        idx = len(self.ops)
        deps = set(extra_deps)
        for b in reads:
            if b.w is not None:
                deps.add(b.w)
            b.r.append(idx)
        for b in writes:
            if b.w is not None:
                pw = self.ops[b.w]
                if dma is not None and pw.dma is dma and not b.r:
                    deps.update(pw.deps)
                else:
                    deps.add(b.w)
            deps.update(b.r)
            b.r = []
            b.w = idx
        deps.discard(idx)
        latest = {}
        keep = []
        for di in deps:
            d = self.ops[di]
            if d.dma is None and d.fn is not None:
                if d.eng not in latest or latest[d.eng] < di:
                    latest[d.eng] = di
            else:
                keep.append(di)
        deps = keep + list(latest.values())
        o = Op(eng, fn, sorted(deps), dma)
        if dma is not None:
            sem = self._dsem(dma)
            dma.dcnt += 16
            o.token = (sem, dma.dcnt)
            if not nobarrier:
                self.pending.append(idx)
        elif cc:
            self.cccnt += 1
            o.token = (self.ccsem, self.cccnt)
            o.dma = "cc"
        self.ops.append(o)
        if fn is not None and not cc:
            self.last[eng] = idx
        return idx

    def barrier(self):
        deps = [v for v in self.last.values() if v is not None] + list(self.pending)
        self.pending = []
        for e in self.ENG:
            self.op(e, None, extra_deps=deps)

    def _needs_wait(self, o, d):
        if d.dma is not None:
            return True
        if d.eng != o.eng:
            return True
        if o.dma is not None:
            return True
        if o.eng == "pe":
            return False
        return True

    def emit(self):
        ops = self.ops
        pos = {e: 0 for e in self.ENG}
        for o in ops:
            if o.fn is not None:
                pos[o.eng] += 1
            o.pos = pos[o.eng]
        for o in ops:
            for di in o.deps:
                d = ops[di]
                if d.dma is None and d.fn is not None and self._needs_wait(o, d):
                    d.mark = True
        cnt = {e: 0 for e in self.ENG}
        for o in ops:
            if o.dma is None and o.mark:
                cnt[o.eng] += 1
                o.token = (self.esem[o.eng], cnt[o.eng])
        seen = {e: {} for e in self.ENG}
        nwait = 0
        for o in ops:
            E = self.h[o.eng]
            waits = {}
            for di in o.deps:
                d = ops[di]
                if d.fn is None or not self._needs_wait(o, d):
                    continue
                sem, val = d.token
                k = id(sem)
                if k not in waits or waits[k][1] < val:
                    waits[k] = (sem, val)
            for k, (sem, val) in waits.items():
                if seen[o.eng].get(k, 0) < val:
                    E.wait_ge(sem, val)
                    seen[o.eng][k] = val
                    nwait += 1
            if o.fn is None:
                continue
            ins = o.fn(E)
            if o.dma is not None:
                ins.then_inc(o.token[0], 1 if o.dma == "cc" else 16)
            elif o.mark:
                ins.then_inc(o.token[0], 1)
        return dict(n_ops=len(ops), n_wait=nwait, marked=cnt, nsem=self.nsem)


class Arena:
    def __init__(self, nc):
        self.nc = nc
        self.off = SB_BASE
        self.n = 0
        self.peak = 0

    def alloc(self, name, shape, dtype):
        nbytes = int(np.prod(shape[1:])) * (4 if dtype in (F32, I32) else 2)
        nbytes = (nbytes + 31) // 32 * 32
        assert self.off + nbytes <= SB_END, (name, self.off, nbytes)
        self.n += 1
        t = self.nc.alloc_sbuf_tensor_at("%s_%d" % (name, self.n), list(shape), dtype, offset=self.off)
        self.off += nbytes
        self.peak = max(self.peak, self.off)
        return t

    def mark(self):
        return self.off

    def reset(self, m):
        self.off = m


class RR:
    def __init__(self, items):
        self.items = list(items)
        self.i = 0

    def next(self):
        x = self.items[self.i % len(self.items)]
        self.i += 1
        return x


def MM(out, lhsT, rhs, start=True, stop=True, skip=False):
    return lambda e: e.matmul(out, lhsT=lhsT, rhs=rhs, start=start, stop=stop, skip_group_check=skip)


def TR(out, in_, ident):
    return lambda e: e.transpose(out=out, in_=in_, identity=ident)


def ACT(out, in_, func, bias, scale):
    return lambda e: e.activation(out=out, in_=in_, func=func, bias=bias, scale=scale)


def MUL(out, in_, c):
    return lambda e: e.mul(out, in_, c)


def CP(out, in_):
    return lambda e: e.copy(out=out, in_=in_)


def TC(out, in_):
    return lambda e: e.tensor_copy(out=out, in_=in_)


def TT(out, in0, in1, op):
    return lambda e: e.tensor_tensor(out=out, in0=in0, in1=in1, op=op)


def TS(out, in0, s1, s2, op0, op1=None):
    if op1 is None:
        return lambda e: e.tensor_scalar(out=out, in0=in0, scalar1=s1, scalar2=None, op0=op0)
    return lambda e: e.tensor_scalar(out=out, in0=in0, scalar1=s1, scalar2=s2, op0=op0, op1=op1)


def STT(out, in0, scalar, in1, op0, op1, accum=None):
    if accum is None:
        return lambda e: e.scalar_tensor_tensor(out=out, in0=in0, scalar=scalar, in1=in1, op0=op0, op1=op1)
    return lambda e: e.scalar_tensor_tensor(out=out, in0=in0, scalar=scalar, in1=in1, op0=op0, op1=op1, accum_out=accum)


def SCAN(out, d0, d1, init):
    return lambda e: e.tensor_tensor_scan(out=out, data0=d0, data1=d1, initial=init, op0=ALU.mult, op1=ALU.add)


def RCP(out, in_):
    return lambda e: e.reciprocal(out=out, in_=in_)


def MS(ap, v):
    return lambda e: e.memset(ap, v)


def DMA(out, in_):
    return lambda e: e.dma_start(out=out, in_=in_)


def flat(ap3):
    n = len(ap3.shape)
    if n == 3:
        return ap3.rearrange("p a b -> p (a b)")
    if n == 4:
        return ap3.rearrange("p a b c -> p (a b c)")
    return ap3


def split(ap2, a):
    return ap2.rearrange("p (a b) -> p a b", a=a)


def build_program():
    nc = bass.Bass("TRN2", target_bir_lowering=False)

    def din(name, shape, dt=F32):
        return nc.dram_tensor(name, list(shape), dt, kind="ExternalInput").ap()

    def dout(name, shape, dt=F32):
        return nc.dram_tensor(name, list(shape), dt, kind="ExternalOutput").ap()

    x_d = din("x_loc", [TL * 128, D])
    wffm_d = din("wffm", [DEPTH, 128, KC, 256])
    wftm_d = din("wftm", [DEPTH, 128, KC, 770])
    wgfm_d = din("wgfm", [DEPTH, 128, KC, 528])
    wgtm_d = din("wgtm", [DEPTH, 128, KC, 512])
    wm_d = din("wm", [DEPTH, 32, 128, KC, 128])
    woa_d = din("woa", [DEPTH, 16, 128, 8, 128])
    wob_d = din("wob", [DEPTH, 16, 128, 8, 128])
    wout_d = din("wout", [DEPTH, 8, 128, KC, 256])
    wa2_d = din("wa2", [DEPTH, 16, 256])
    ba_d = din("ba", [DEPTH, 128, 2])
    bfb_d = din("bfb", [DEPTH, 128, 2])
    gain_d = din("gain", [DEPTH, 128, 256])
    pre_d = din("pre", [DEPTH, 128, D])
    post_d = din("post", [DEPTH, 128, D])
    ck_d = din("ck", [DEPTH, 4, 4096, 256])
    cv_d = din("cv", [DEPTH, 4, 4096, 256])
    clf_d = din("clf", [DEPTH, 4, 128, 32, 2])
    sg_d = din("sg", [DEPTH, 4, 128, 2, 256])
    idx_d = din("idx", [128, TL * 4], I32)

    y_d = dout("y_loc", [TL * 128, D])
    ko_d = dout("k_out", [DEPTH, NTOK, 256])
    vo_d = dout("v_out", [DEPTH, NTOK, 256])
    lfo_d = dout("lf_out", [DEPTH, 128, NT, 2])
    gp_d = dout("gla_p", [DEPTH, 128, 2, 256])
    gs_d = dout("gla_s", [DEPTH, 4, 128, 2, 256])

    hsrc = nc.dram_tensor("hsrc", [TL * 128, D], BF16).ap()
    hag = nc.dram_tensor("hag", [4 * TL * 128, D], BF16).ap()
    osrcA = nc.dram_tensor("osrcA", [NTOK, 256], BF16).ap()
    osrcB = nc.dram_tensor("osrcB", [NTOK, 256], BF16).ap()
    oagA = nc.dram_tensor("oagA", [4 * NTOK, 256], BF16).ap()
    oagB_d = nc.dram_tensor("oagB", [4 * NTOK, 256], BF16).ap()
    yscr = nc.dram_tensor("yscr", [TL * 128, D], F32).ap()
    wm_b = nc.dram_tensor("wm_b", [DEPTH, 32, 128, KC * 128], BF16).ap()
    wffm_b = nc.dram_tensor("wffm_b", [DEPTH, 128, KC * 256], BF16).ap()
    wftm_b = nc.dram_tensor("wftm_b", [DEPTH, 128, KC * 770], BF16).ap()
    wgfm_b = nc.dram_tensor("wgfm_b", [DEPTH, 128, KC * 528], BF16).ap()
    wgtm_b = nc.dram_tensor("wgtm_b", [DEPTH, 128, KC * 512], BF16).ap()
    woa_b = nc.dram_tensor("woa_b", [DEPTH, 16, 128, 8 * 128], BF16).ap()
    wob_b = nc.dram_tensor("wob_b", [DEPTH, 16, 128, 8 * 128], BF16).ap()
    wout_b = nc.dram_tensor("wout_b", [DEPTH, 8, 128, KC * 256], BF16).ap()

    with ExitStack() as st:
        P = Prog(nc, st)
        A = Arena(nc)
        ps = [nc.alloc_psum_tensor("psb%d" % i, [128, 512], F32) for i in range(8)]
        psB = [P.buf("ps%d" % i) for i in range(8)]

        ones_f = A.alloc("ones_f", [128, 128], F32)
        ident_f = A.alloc("ident_f", [128, 128], F32)
        tri_f = A.alloc("tri_f", [128, 128], F32)
        tri_b = A.alloc("tri_b", [128, 128], BF16)
        resetm = A.alloc("resetm", [128, 512], F32)
        cst = A.alloc("cst", [128, 8], F32)
        idx_t = A.alloc("idx_t", [128, TL * 4], I32)
        Bc = P.buf("consts")
        ID = ident_f[:]

        P.op("pool", MS(ones_f[:], 1.0), writes=[Bc])
        P.op("pool", MS(ident_f[:], 0.0), writes=[Bc])
        P.op("pool", lambda e: e.affine_select(out=ident_f[:], in_=ones_f[:], pattern=[[-1, 128]],
                                               compare_op=ALU.is_equal, fill=0.0, base=0, channel_multiplier=1),
             writes=[Bc])
        P.op("pool", MS(tri_f[:], 0.0), writes=[Bc])
        P.op("pool", lambda e: e.affine_select(out=tri_f[:], in_=ones_f[:], pattern=[[1, 128]],
                                               compare_op=ALU.is_ge, fill=0.0, base=0, channel_multiplier=-1),
             writes=[Bc])
        P.op("pool", TC(tri_b[:], tri_f[:]), writes=[Bc])
        P.op("pool", MS(resetm[:], 1.0), writes=[Bc])
        for q in range(4):
            P.op("pool", MS(resetm[:, q * 128:q * 128 + 1], 0.0), writes=[Bc])
        P.op("pool", MS(cst[:, 0:1], 1e-6), writes=[Bc])
        P.op("pool", MS(cst[:, 1:2], 1.0), writes=[Bc])
        P.op("pool", MS(cst[:, 2:3], -math.log(16.0)), writes=[Bc])
        P.op("pool", MS(cst[:, 3:4], 0.0), writes=[Bc])
        P.op("pool", MS(cst[:, 4:5], math.log(0.5)), writes=[Bc])
        P.op("sp", DMA(idx_t[:], idx_d), writes=[Bc], dma=Bc)
        EPS, ONE, NL16, ZERO, LNH = cst[:, 0:1], cst[:, 1:2], cst[:, 2:3], cst[:, 3:4], cst[:, 4:5]

        hsrcB = P.bufs(TL, "hsrc")
        hagB = P.bufs(9, "hag")
        oagAB = P.bufs(5, "oagA")
        oagBB = P.bufs(5, "oagB")
        base_mark = A.mark()
        psrr = RR(range(8))

        def rstd_of(ssq, n, B, lnbias=None):
            P.op("act", ACT(ssq[:, 1:2], ssq[:, 0:1], AF.Ln, EPS, 1.0 / n), reads=[B, Bc], writes=[B])
            P.op("act", ACT(ssq[:, 2:3], ssq[:, 1:2], AF.Exp, ZERO if lnbias is None else lnbias, -0.5), reads=[B, Bc], writes=[B])

        def emit_h(y_t, yB, pre_t, preB, t, tmp, q_="pool"):
            junk, ssq, hf, hTst, hTstB, tB = tmp
            P.op("dve", STT(junk[:], y_t[:], 1.0, y_t[:], ALU.mult, ALU.mult, accum=ssq[:, 0:1]), reads=[yB], writes=[tB])
            rstd_of(ssq, D, tB)
            P.op("dve", STT(hf[:], y_t[:], ssq[:, 2:3], pre_t[:], ALU.mult, ALU.mult), reads=[yB, tB, preB], writes=[tB])
            for q in range(4):
                bk = psrr.next()
                for j in range(4):
                    c = q * 4 + j
                    P.op("pe", TR(ps[bk][:, j * 128:(j + 1) * 128], hf[:, c * 128:(c + 1) * 128], ID),
                         reads=[tB, Bc], writes=[psB[bk]])
                P.op("act", CP(hTst[:, q * 4:(q + 1) * 4, :], split(ps[bk][:, :], 4)), reads=[psB[bk]], writes=[hTstB])
            return P.op(q_, DMA(hsrc[t * 128:(t + 1) * 128, :], flat(hTst[:, :, :])), reads=[hTstB], writes=[hsrcB[t]], dma=hTstB)

        def ag_chunk(src, dst, dstB, q, rows_total, rows_chunk, store_ops):
            r0 = q * rows_chunk
            n = min(rows_chunk, rows_total - r0)
            P.op("pool", (lambda i_, o_: (lambda e: e.collective_compute(
                "AllGather", ALU.bypass, replica_groups=GROUPS, ins=[i_], outs=[o_])))(
                    src[r0:r0 + n, :], dst[q * 4 * rows_chunk:q * 4 * rows_chunk + 4 * n, :]),
                writes=[dstB[q]], cc=True, extra_deps=store_ops)

        def cumsum_tiles(bkrr, lf_ap, n, carry_ap, out_ap, tmp, tB, lfB, outB, carry_out_ap=None):
            bk = bkrr.next()
            sb_, incl = tmp
            P.op("pe", MM(ps[bk][:, 0:2 * n], tri_f[:], flat(lf_ap)), reads=[lfB, Bc], writes=[psB[bk]])
            P.op("pe", MM(ps[bk][:, 2 * n:4 * n], ones_f[:], flat(lf_ap), start=False, stop=True, skip=True),
                 reads=[lfB, Bc], writes=[psB[bk]])
            P.op("act", CP(sb_[:, 0:4 * n], ps[bk][:, 0:4 * n]), reads=[psB[bk]], writes=[tB])
            tot = sb_[:, 2 * n:4 * n].rearrange("p (a b) -> p a b", b=2)
            loc = sb_[:, 0:2 * n].rearrange("p (a b) -> p a b", b=2)
            inc3 = incl[:, 0:2 * n].rearrange("p (a b) -> p a b", b=2)
            for hh in range(2):
                P.op("dve", SCAN(inc3[:, :, hh], ones_f[:, 0:n], tot[:, :, hh], carry_ap[:, hh:hh + 1]),
                     reads=[tB, Bc, outB], writes=[tB])
            P.op("dve", TT(loc, loc, tot, ALU.subtract), reads=[tB], writes=[tB])
            P.op("dve", TT(out_ap, loc, inc3, ALU.add), reads=[tB], writes=[outB])
            if carry_out_ap is not None:
                P.op("dve", TC(carry_out_ap, inc3[:, n - 1, :]), reads=[tB], writes=[outB])

        def load_hT_block(i, hTb_t, hTbB):
            q, tl = i // 2, i % 2
            nq = 256 if q < 8 else 128
            for tt in range(4):
                row = q * 1024 + tt * nq + tl * 128
                P.op("sp", DMA(hTb_t[:, :, tt * 128:(tt + 1) * 128], split(hag[row:row + 128, :], KC)),
                     reads=[hagB[q]], writes=[hTbB], dma=hTbB)

        hwB = {}
        hw_ops = {}
        hw_jobs = {}
        for l_ in range(DEPTH):
            for ps_, mats in (("F", ((wffm_b, wffm_d, 256, 8), (wftm_b, wftm_d, 770, 2))),
                              ("G", ((wgfm_b, wgfm_d, 528, 3), (wgtm_b, wgtm_d, 512, 4)))):
                hwB[(l_, ps_)] = P.buf("hwcast%d%s" % (l_, ps_))
                hw_ops[(l_, ps_)] = []
                jl = []
                for (dst_, src_, cols, kk) in mats:
                    for k0 in range(0, KC, kk):
                        k1 = min(KC, k0 + kk)
                        jl.append((dst_[l_, :, k0 * cols:k1 * cols], flat(src_[l_, :, k0:k1, :])))
                hw_jobs[(l_, ps_)] = jl

        def emit_hw_casts(key, n=None):
            jl = hw_jobs[key]
            for _ in range(len(jl) if n is None else min(n, len(jl))):
                o_, i_ = jl.pop(0)
                hw_ops[key].append(P.op("pool", DMA(o_, i_), dma=hwB[key], nobarrier=True))

        emit_hw_casts((0, "F"))

        def load_w_bf(dst_t, src_ap, B, key):
            emit_hw_casts(key)
            P.op("sp", DMA(flat(dst_t[:, :, :]), src_ap), writes=[B], dma=B, extra_deps=hw_ops[key])

        def silu2_from_psum(out_ap, ps_ap, et_ap, etB_, psBuf, outB):
            P.op("act", ACT(et_ap, ps_ap, AF.Tanh, ZERO, 0.5), reads=[psBuf, Bc], writes=[etB_])
            P.op("dve", STT(out_ap, et_ap, 1.0, ps_ap, ALU.add, ALU.mult), reads=[psBuf, etB_], writes=[outB])

        wcastB = [P.buf("wcast%d" % l_) for l_ in range(DEPTH)]
        wcast_ops = [[] for _ in range(DEPTH)]

        def cast_jobs(l):
            jobs = []
            for c in range(32):
                jobs.append((wm_b[l, c], flat(wm_d[l, c])))
            for c in range(16):
                jobs.append((woa_b[l, c], flat(woa_d[l, c])))
                jobs.append((wob_b[l, c], flat(wob_d[l, c])))
            for blk in range(8):
                for hf_ in range(2):
                    jobs.append((wout_b[l, blk, :, hf_ * 2048:(hf_ + 1) * 2048], flat(wout_d[l, blk, :, hf_ * 8:(hf_ + 1) * 8, :])))
            return jobs

        def emit_casts(l, jobs, n):
            for _ in range(min(n, len(jobs))):
                o_, i_ = jobs.pop(0)
                wcast_ops[l].append(P.op("pool", DMA(o_, i_), dma=wcastB[l], nobarrier=True))

        def prologue():
            A.reset(base_mark)
            pre_t = A.alloc("pre_t", [128, D], F32)
            preB = P.buf("pre")
            P.op("sp", DMA(pre_t[:], pre_d[0]), writes=[preB], dma=preB)
            xts = [A.alloc("xt", [128, D], F32) for _ in range(2)]
            xBs = P.bufs(2, "xt")
            tmps = []
            for _ in range(2):
                tmps.append((A.alloc("junk", [128, D], BF16), A.alloc("ssq", [128, 4], F32), A.alloc("hf", [128, D], F32),
                             A.alloc("hTst", [128, KC, 128], BF16), P.buf("hTst"), P.buf("htmp")))
            sts = []
            P.op("sp", DMA(xts[0][:], x_d[0:128, :]), writes=[xBs[0]], dma=xBs[0])
            for t in range(TL):
                s = t % 2
                if t + 1 < TL:
                    P.op("sp", DMA(xts[1 - s][:], x_d[(t + 1) * 128:(t + 2) * 128, :]), writes=[xBs[1 - s]], dma=xBs[1 - s])
                sts.append(emit_h(xts[s], xBs[s], pre_t, preB, t, tmps[s], q_="sp"))
                if t % 2 == 1 or t == TL - 1:
                    ag_chunk(hsrc, hag, hagB, t // 2, TL * 128, 256, sts)
                    sts = []
            P.barrier()

        def pass_f(l):
            A.reset(base_mark)
            WFfm = A.alloc("WFfm", [128, KC, 256], BF16)
            WFtm = A.alloc("WFtm", [128, KC, 770], BF16)
            WB_ = P.buf("WF")
            load_w_bf(WFfm, wffm_b[l], WB_, (l, "F"))
            load_w_bf(WFtm, wftm_b[l], WB_, (l, "F"))
            bfb = A.alloc("bfb", [128, 2], F32)
            P.op("sp", DMA(bfb[:], bfb_d[l]), writes=[WB_], dma=WB_)
            kT = A.alloc("kT", [128, 2, NTOK], BF16)
            Vg = A.alloc("Vg", [128, NT, 2, 129], BF16)
            kTB = P.bufs(NT, "kT")
            VB = P.bufs(NT, "V")
            P.op("dve", MS(flat(Vg[:, :, :, :]), 2.0), writes=VB)
            hTb = [A.alloc("hTb", [128, KC, 512], BF16) for _ in range(2)]
            hTbB = P.bufs(2, "hTb")
            qT = [A.alloc("qT", [128, 2, 512], BF16) for _ in range(2)]
            qTB = P.bufs(2, "qT")
            kvf = [A.alloc("kvf", [128, 512], F32) for _ in range(2)]
            kvfB = P.bufs(2, "kvf")
            et = [A.alloc("et", [128, 256], F32) for _ in range(2)]
            etB = P.bufs(2, "et")
            fgs = [A.alloc("fgs", [128, 4, 256], BF16) for _ in range(2)]
            fgsB = P.bufs(2, "fgs")
            lfpre = [A.alloc("lfpre", [128, 4, 2], F32) for _ in range(2)]
            lfpB = P.bufs(2, "lfp")
            LF = A.alloc("LF", [128, NT, 2], F32)
            LFB = P.buf("LF")
            C = A.alloc("C", [128, NT, 2], F32)
            CB = P.buf("C")
            carry = A.alloc("carry", [128, 2], F32)
            cref = A.alloc("cref", [128, NBLK, 2], F32)
            cs_tmp = (A.alloc("cs_sb", [128, 128], F32), A.alloc("cs_incl", [128, 64], F32))
            csB = P.buf("cstmp")
            btab = [A.alloc("btab", [128, 2, NT], F32) for _ in range(2)]
            btB = P.bufs(2, "btab")
            PT = [A.alloc("PT", [128, 512], BF16) for _ in range(4)]
            PTB = P.bufs(4, "PT")
            ptrr = RR(range(4))
            ost = [A.alloc("ost", [128, 256], BF16) for _ in range(8)]
            ostB = P.bufs(8, "ost")
            rec = [A.alloc("rec", [128, 4], F32) for _ in range(8)]
            ckf = [A.alloc("ckf", [128, 256], F32) for _ in range(3)]
            ckfB = P.bufs(3, "ckf")
            ckT = [A.alloc("ckT", [128, 2, 128], BF16) for _ in range(3)]
            ckTB = P.bufs(3, "ckT")
            cV = [A.alloc("cV", [128, 2, 129], BF16) for _ in range(3)]
            cVB = P.bufs(3, "cV")
            for s3 in range(3):
                P.op("dve", MS(flat(cV[s3][:, :, :]), 2.0), writes=[cVB[s3]])
            clf = A.alloc("clf", [128, 32, 2], F32)
            clfB = P.buf("clf")
            cC = A.alloc("cC", [128, 32, 2], F32)
            cCB = P.buf("cC")
            ccar = A.alloc("ccar", [128, 2], F32)
            Cn = A.alloc("Cn", [128, 1, 2], F32)
            btS = A.alloc("btS", [128, 2, 33], F32)
            btSB = P.buf("btS")
            PTs = [A.alloc("PTs", [128, 64], BF16) for _ in range(3)]
            PTsB = P.bufs(3, "PTs")

            P.op("dve", MS(carry[:], 0.0), writes=[CB])
            pj = RR([0, 1, 2, 3])
            pjs = RR([0])
            stb = RR([1, 2, 3])
            OB = [(4, 5), (6, 7)]

            def sample_attention(s):
                for k4 in range(4):
                    j = 64 + k4
                    P.op("sp", DMA(clf[:], clf_d[l, k4]), writes=[clfB], dma=clfB)
                    P.op("dve", MS(ccar[:], 0.0), writes=[cCB])
                    cumsum_tiles(pjs, clf[:, :, :], 32, ccar, cC[:, :, :], cs_tmp, csB, clfB, cCB, carry_out_ap=ccar[:, :])
                    cumsum_tiles(pjs, LF[:, j:j + 1, :], 1, ccar, Cn[:, :, :], cs_tmp, csB, LFB, cCB)
                    for h in range(2):
                        P.op("dve", TS(btS[:, h, 0:32], cC[:, :, h], -1.0, ccar[:, h:h + 1], ALU.mult, ALU.add),
                             reads=[cCB], writes=[btSB])
                        P.op("dve", TS(btS[:, h, 32:33], Cn[:, :, h], -1.0, ccar[:, h:h + 1], ALU.mult, ALU.add),
                             reads=[cCB], writes=[btSB])
                    bO = OB[k4 % 2][0]
                    qsl = slice(k4 * 128, k4 * 128 + 32)
                    for jt in range(33):
                        r3 = (k4 * 33 + jt) % 3
                        sk = stb.next()
                        if jt < 32:
                            P.op("sp", DMA(ckf[r3][:], ck_d[l, k4, jt * 128:(jt + 1) * 128, :]), writes=[ckfB[r3]], dma=ckfB[r3])
                            P.op("pool", DMA(cV[r3][:, :, 0:128], split(cv_d[l, k4, jt * 128:(jt + 1) * 128, :], 2)),
                                 writes=[cVB[r3]], dma=cVB[r3])
                            bk2 = pjs.next()
                            for h in range(2):
                                P.op("pe", TR(ps[bk2][:, h * 128:(h + 1) * 128], ckf[r3][:, h * 128:(h + 1) * 128], ID),
                                     reads=[ckfB[r3], Bc], writes=[psB[bk2]])
                            P.op("dve", TC(ckT[r3][:, :, :], split(ps[bk2][:, 0:256], 2)), reads=[psB[bk2]], writes=[ckTB[r3]])
                            for h in range(2):
                                P.op("pe", MM(ps[sk][:, h * 32:(h + 1) * 32], ckT[r3][:, h, :], qT[s][:, h, qsl],
                                              start=(h == 0), stop=True, skip=True),
                                     reads=[ckTB[r3], qTB[s]], writes=[psB[sk]])
                            for h in range(2):
                                P.op("act", ACT(PTs[r3][:, h * 32:(h + 1) * 32], ps[sk][:, h * 32:(h + 1) * 32], AF.Exp,
                                                btS[:, h, jt:jt + 1], SCALE_F), reads=[psB[sk], btSB], writes=[PTsB[r3]])
                            for h in range(2):
                                P.op("pe", MM(ps[bO][0:32, h * 129:(h + 1) * 129], PTs[r3][:, h * 32:(h + 1) * 32], cV[r3][:, h, :],
                                              start=(jt == 0 and h == 0), stop=False, skip=True),
                                     reads=[PTsB[r3], cVB[r3]], writes=[psB[bO]])
                        else:
                            for h in range(2):
                                P.op("pe", MM(ps[sk][0:32, h * 32:(h + 1) * 32], kT[:, h, j * 128:j * 128 + 32], qT[s][:, h, qsl],
                                              start=(h == 0), stop=True, skip=True),
                                     reads=[kTB[j], qTB[s]], writes=[psB[sk]])
                            for h in range(2):
                                P.op("act", ACT(PTs[r3][0:32, h * 32:(h + 1) * 32], ps[sk][0:32, h * 32:(h + 1) * 32], AF.Exp,
                                                btS[0:32, h, 32:33], SCALE_F), reads=[psB[sk], btSB], writes=[PTsB[r3]])
                                P.op("pool", TT(PTs[r3][0:32, h * 32:(h + 1) * 32], PTs[r3][0:32, h * 32:(h + 1) * 32],
                                                tri_b[0:32, 0:32], ALU.mult), reads=[PTsB[r3], Bc], writes=[PTsB[r3]])
                            for h in range(2):
                                P.op("pe", MM(ps[bO][0:32, h * 129:(h + 1) * 129], PTs[r3][0:32, h * 32:(h + 1) * 32],
                                              Vg[0:32, j, h, :], start=False, stop=True, skip=True),
                                     reads=[PTsB[r3], VB[j]], writes=[psB[bO]])
                    oi = j % 8
                    P.op("pool", MS(ost[oi][:], 0.0), writes=[ostB[oi]])
                    for h in range(2):
                        P.op("dve", RCP(rec[oi][0:32, h:h + 1], ps[bO][0:32, h * 129 + 128:h * 129 + 129]),
                             reads=[psB[bO]], writes=[ostB[oi]])
                        P.op("dve", STT(ost[oi][0:32, h * 128:(h + 1) * 128], ps[bO][0:32, h * 129:h * 129 + 128],
                                        rec[oi][0:32, h:h + 1], fgs[s][0:32, k4, h * 128:(h + 1) * 128], ALU.mult, ALU.mult),
                             reads=[psB[bO], fgsB[s], ostB[oi]], writes=[ostB[oi]])
                    oa_st.append(P.op("sp", DMA(osrcA[j * 128:(j + 1) * 128, :], ost[oi][:]), reads=[ostB[oi]], dma=ostB[oi]))

            def block(i):
                s = i % 2
                if i + 1 < NBLK:
                    load_hT_block(i + 1, hTb[1 - s], hTbB[1 - s])
                hb, hbB = hTb[s], hTbB[s]
                for h in range(2):
                    bk = pj.next()
                    for k in range(KC):
                        P.op("pe", MM(ps[bk][:, :], WFfm[:, k, h * 128:(h + 1) * 128], hb[:, k, :], start=(k == 0), stop=(k == KC - 1)),
                             reads=[WB_, hbB], writes=[psB[bk]])
                    P.op("dve", TC(qT[s][:, h, :], ps[bk][:, :]), reads=[psB[bk]], writes=[qTB[s]])
                for tt in range(4):
                    j = 4 * i + tt
                    ks = j % 2
                    tsl = slice(tt * 128, (tt + 1) * 128)
                    bk = pj.next()
                    for k in range(KC):
                        P.op("pe", MM(ps[bk][:, :], hb[:, k, tsl], WFtm[:, k, 0:512], start=(k == 0), stop=(k == KC - 1)),
                             reads=[WB_, hbB], writes=[psB[bk]])
                    P.op("dve", TC(kvf[ks][:], ps[bk][:, :]), reads=[psB[bk]], writes=[kvfB[ks]])
                    P.op("sp", DMA(ko_d[l, j * 128:(j + 1) * 128, :], kvf[ks][:, 0:256]), reads=[kvfB[ks]], dma=kvfB[ks])
                    P.op("sp", DMA(vo_d[l, j * 128:(j + 1) * 128, :], kvf[ks][:, 256:512]), reads=[kvfB[ks]], dma=kvfB[ks])
                    P.op("pool", TC(Vg[:, j, :, 0:128], split(kvf[ks][:, 256:512], 2)), reads=[kvfB[ks]], writes=[VB[j]])
                    bk = pj.next()
                    for k in range(KC):
                        P.op("pe", MM(ps[bk][:, 0:258], hb[:, k, tsl], WFtm[:, k, 512:770], start=(k == 0), stop=(k == KC - 1)),
                             reads=[WB_, hbB], writes=[psB[bk]])
                    bk2 = pj.next()
                    for h in range(2):
                        P.op("pe", TR(ps[bk2][:, h * 128:(h + 1) * 128], kvf[ks][:, h * 128:(h + 1) * 128], ID),
                             reads=[kvfB[ks], Bc], writes=[psB[bk2]])
                    P.op("dve", TC(kT[:, :, j * 128:(j + 1) * 128], split(ps[bk2][:, 0:256], 2)), reads=[psB[bk2]], writes=[kTB[j]])
                    silu2_from_psum(fgs[s][:, tt, :], ps[bk][:, 0:256], et[ks][:], etB[ks], psB[bk], fgsB[s])
                    P.op("dve", TT(lfpre[s][:, tt, :], ps[bk][:, 256:258], bfb[:], ALU.add), reads=[psB[bk], WB_], writes=[lfpB[s]])
                lfp2 = flat(lfpre[s][:, :, :])
                P.op("act", ACT(lfp2, lfp2, AF.Exp, ZERO, -1.0), reads=[lfpB[s], Bc], writes=[lfpB[s]])
                P.op("act", ACT(lfp2, lfp2, AF.Ln, ONE, 1.0), reads=[lfpB[s], Bc], writes=[lfpB[s]])
                P.op("pool", TS(flat(LF[:, 4 * i:4 * i + 4, :]), lfp2, -1.0, None, ALU.mult), reads=[lfpB[s]], writes=[LFB])
                if i == 16:
                    sample_attention(s)
                    return
                cumsum_tiles(pj, LF[:, 4 * i:4 * i + 4, :], 4, carry, C[:, 4 * i:4 * i + 4, :], cs_tmp, csB, LFB, CB,
                             carry_out_ap=carry[:, :])
                P.op("dve", TC(cref[:, i, :], cs_tmp[1][:, 0:8].rearrange("p (a b) -> p a b", b=2)[:, 1, :]), reads=[csB], writes=[CB])
                nkt = 4 * i + 4
                for h in range(2):
                    P.op("dve", TS(btab[s][:, h, 0:nkt], C[:, 0:nkt, h], -1.0, cref[:, i, h:h + 1], ALU.mult, ALU.add),
                         reads=[CB], writes=[btB[s]])
                for h in range(2):
                    bA, bB = OB[h]

                    def score(kt):
                        jj = kt - 4 * i
                        c0 = 0 if jj < 0 else 128 * jj
                        sk = stb.next()
                        P.op("pe", MM(ps[sk][:, c0:512], kT[:, h, kt * 128:(kt + 1) * 128], qT[s][:, h, c0:512]),
                             reads=[kTB[kt], qTB[s]], writes=[psB[sk]])
                        return sk

                    skq = [score(0)]
                    if nkt > 1:
                        skq.append(score(1))
                    for kt in range(nkt):
                        if kt + 2 < nkt:
                            skq.append(score(kt + 2))
                        sk = skq.pop(0)
                        jj = kt - 4 * i
                        c0 = 0 if jj < 0 else 128 * jj
                        pi = ptrr.next()
                        P.op("act", ACT(PT[pi][:, c0:512], ps[sk][:, c0:512], AF.Exp, btab[s][:, h, kt:kt + 1], SCALE_F),
                             reads=[psB[sk], btB[s]], writes=[PTB[pi]])
                        if jj >= 0:
                            P.op("pool", TT(PT[pi][:, c0:c0 + 128], PT[pi][:, c0:c0 + 128], tri_b[:], ALU.mult),
                                 reads=[PTB[pi], Bc], writes=[PTB[pi]])
                        for sub in range(max(jj, 0), 4):
                            bo = bA if sub < 2 else bB
                            oc = (sub % 2) * 129
                            P.op("pe", MM(ps[bo][:, oc:oc + 129], PT[pi][:, sub * 128:(sub + 1) * 128], Vg[:, kt, h, :],
                                          start=(kt == 0 and sub % 2 == 0), stop=(kt == 4 * i + sub), skip=True),
                                 reads=[PTB[pi], VB[kt]], writes=[psB[bo]])
                    for sub in range(4):
                        j = 4 * i + sub
                        bo = bA if sub < 2 else bB
                        oc = (sub % 2) * 129
                        oi = j % 8
                        P.op("dve", RCP(rec[oi][:, h:h + 1], ps[bo][:, oc + 128:oc + 129]), reads=[psB[bo]], writes=[ostB[oi]])
                        P.op("dve", STT(ost[oi][:, h * 128:(h + 1) * 128], ps[bo][:, oc:oc + 128], rec[oi][:, h:h + 1],
                                        fgs[s][:, sub, h * 128:(h + 1) * 128], ALU.mult, ALU.mult),
                             reads=[psB[bo], fgsB[s], ostB[oi]], writes=[ostB[oi]])
                for sub in range(4):
                    j = 4 * i + sub
                    oi = j % 8
                    oa_st.append(P.op("sp", DMA(osrcA[j * 128:(j + 1) * 128, :], ost[oi][:]), reads=[ostB[oi]], dma=ostB[oi]))

            oa_st = []
            jobs = cast_jobs(l)
            load_hT_block(0, hTb[0], hTbB[0])
            for i in range(NBLK):
                block(i)
                emit_hw_casts((l, "G"), 2)
                emit_casts(l, jobs, 5)
            for q in range(5):
                ag_chunk(osrcA, oagA, oagAB, q, NTOK, 2048, oa_st[16 * q:16 * q + 16])
            emit_casts(l, jobs, len(jobs))
            P.op("sp", DMA(lfo_d[l], LF[:, :, :]), reads=[LFB], dma=LFB)
            P.barrier()

        def pass_g(l):
            A.reset(base_mark)
            WGfm = A.alloc("WGfm", [128, KC, 528], BF16)
            WGtm = A.alloc("WGtm", [128, KC, 512], BF16)
            WB_ = P.buf("WG")
            load_w_bf(WGfm, wgfm_b[l], WB_, (l, "G"))
            load_w_bf(WGtm, wgtm_b[l], WB_, (l, "G"))
            wa2 = A.alloc("wa2", [16, 256], BF16)
            wa2B = P.buf("wa2")
            P.op("pool", DMA(wa2[:], wa2_d[l]), writes=[wa2B], dma=wa2B)
            nba = A.alloc("nba", [128, 2], F32)
            gain = A.alloc("gain", [128, 256], F32)
            P.op("sp", DMA(nba[:], ba_d[l]), writes=[WB_], dma=WB_)
            P.op("sp", DMA(gain[:], gain_d[l]), writes=[WB_], dma=WB_)
            P.op("dve", TS(nba[:], nba[:], -1.0, None, ALU.mult), reads=[WB_], writes=[WB_])
            hTb = [A.alloc("hTb", [128, KC, 512], BF16) for _ in range(2)]
            hTbB = P.bufs(2, "hTb")
            glrT = A.alloc("glrT", [16, 512], BF16)
            glB = P.buf("glrT")
            et2 = A.alloc("et2", [128, 512], F32)
            et2B = P.buf("et2")
            sp_ = A.alloc("sp", [128, 2, 512], F32)
            spB = P.buf("sp")
            bpos = A.alloc("bpos", [128, 2, 512], F32)
            bpB = P.buf("bpos")
            ebT = A.alloc("ebT", [128, 2, 512], BF16)
            enbT = A.alloc("enbT", [128, 2, 512], BF16)
            ebB = P.buf("eb")
            ebl = [A.alloc("ebl", [128, 2, 4], F32) for _ in range(2)]
            eblB = P.bufs(2, "ebl")
            qtl = [A.alloc("qtl", [128, 2, 512], BF16) for _ in range(2)]
            qtB = P.bufs(2, "qtl")
            ktf = [A.alloc("ktf", [128, 2, 512], F32) for _ in range(2)]
            ktfB = P.bufs(2, "ktf")
            ktl = [A.alloc("ktl", [128, 2, 512], BF16) for _ in range(2)]
            ktlB = P.bufs(2, "ktl")
            gvb = [A.alloc("gvb", [128, 256], BF16) for _ in range(8)]
            gvB = P.bufs(8, "gvb")
            ggs = [A.alloc("ggs", [128, 256], BF16) for _ in range(8)]
            ggB = P.bufs(8, "ggs")
            etg = [A.alloc("etg", [128, 256], F32) for _ in range(2)]
            etgB = P.bufs(2, "etg")
            kttm = [A.alloc("kttm", [128, 256], BF16) for _ in range(2)]
            kttmB = P.bufs(2, "kttm")
            AT = [A.alloc("AT", [128, 128], BF16) for _ in range(2)]
            ATB = P.bufs(2, "AT")
            osb = [A.alloc("osb", [128, 256], F32) for _ in range(2)]
            osbB = P.bufs(2, "osb")
            junk = A.alloc("junkg", [128, 256], BF16)
            ssq = [A.alloc("ssqg", [128, 4], F32) for _ in range(2)]
            t1 = [A.alloc("t1g", [128, 256], F32) for _ in range(2)]
            obst = [A.alloc("obst", [128, 256], BF16) for _ in range(4)]
            obB = P.bufs(4, "obst")
            S = A.alloc("S", [128, 2, 256], F32)
            SB_ = P.buf("S")
            Sb = A.alloc("Sb", [128, 2, 256], BF16)
            SbB = P.buf("Sb")
            tmpS = A.alloc("tmpS", [128, 2, 256], F32)
            tSB = P.buf("tmpS")
            P.op("dve", MS(flat(S[:, :, :]), 0.0), writes=[SB_])
            P.op("dve", MS(flat(Sb[:, :, :]), 0.0), writes=[SbB])
            ostores = []

            def proj(i):
                s = i % 2
                hb, hbB = hTb[s], hTbB[s]
                bk = psrr.next()
                for k in range(KC):
                    P.op("pe", MM(ps[bk][0:16, :], WGfm[:, k, 512:528], hb[:, k, :], start=(k == 0), stop=(k == KC - 1)),
                         reads=[WB_, hbB], writes=[psB[bk]])
                P.op("act", CP(glrT[:, :], ps[bk][0:16, :]), reads=[psB[bk]], writes=[glB])
                for c in range(2):
                    bk = psrr.next()
                    P.op("pe", MM(ps[bk][:, :], wa2[:, c * 128:(c + 1) * 128], glrT[:, :]), reads=[wa2B, glB], writes=[psB[bk]])
                    P.op("act", ACT(et2[:], ps[bk][:, :], AF.Exp, nba[:, c:c + 1], -1.0), reads=[psB[bk], WB_], writes=[et2B])
                    P.op("act", ACT(sp_[:, c, :], et2[:], AF.Ln, ONE, 1.0), reads=[et2B, Bc], writes=[spB])
                    P.op("dve", SCAN(bpos[:, c, :], resetm[:], sp_[:, c, :], 0.0), reads=[spB, Bc], writes=[bpB])
                bp2 = flat(bpos[:, :, :])
                P.op("act", ACT(flat(ebT[:, :, :]), bp2, AF.Exp, NL16, -1.0 / 16.0), reads=[bpB, Bc], writes=[ebB])
                P.op("act", ACT(flat(enbT[:, :, :]), bp2, AF.Exp, ZERO, 1.0 / 16.0), reads=[bpB, Bc], writes=[ebB])
                lastc = 31 if i == 16 else 127
                for c in range(2):
                    P.op("act", ACT(ebl[s][:, c, :], split(bpos[:, c, :], 4)[:, :, lastc], AF.Exp, ZERO, -1.0 / 16.0),
                         reads=[bpB, Bc], writes=[eblB[s]])
                for c in range(2):
                    bk = psrr.next()
                    for k in range(KC):
                        P.op("pe", MM(ps[bk][:, :], WGfm[:, k, c * 128:(c + 1) * 128], hb[:, k, :], start=(k == 0), stop=(k == KC - 1)),
                             reads=[WB_, hbB], writes=[psB[bk]])
                    P.op("dve", TT(qtl[s][:, c, :], ps[bk][:, :], ebT[:, c, :], ALU.mult), reads=[psB[bk], ebB], writes=[qtB[s]])
                for c in range(2):
                    bk = psrr.next()
                    for k in range(KC):
                        P.op("pe", MM(ps[bk][:, :], WGfm[:, k, 256 + c * 128:256 + (c + 1) * 128], hb[:, k, :],
                                      start=(k == 0), stop=(k == KC - 1)), reads=[WB_, hbB], writes=[psB[bk]])
                    P.op("dve", TT(ktf[s][:, c, :], ps[bk][:, :], enbT[:, c, :], ALU.mult), reads=[psB[bk], ebB], writes=[ktfB[s]])
                P.op("act", CP(flat(ktl[s][:, :, :]), flat(ktf[s][:, :, :])), reads=[ktfB[s]], writes=[ktlB[s]])
                for tt in range(4):
                    j = 4 * i + tt
                    tsl = slice(tt * 128, (tt + 1) * 128)
                    g8, g2 = j % 8, j % 2
                    bk = psrr.next()
                    for k in range(KC):
                        P.op("pe", MM(ps[bk][:, :], hb[:, k, tsl], WGtm[:, k, :], start=(k == 0), stop=(k == KC - 1)),
                             reads=[WB_, hbB], writes=[psB[bk]])
                    P.op("act", CP(gvb[g8][:], ps[bk][:, 0:256]), reads=[psB[bk]], writes=[gvB[g8]])
                    silu2_from_psum(ggs[g8][:], ps[bk][:, 256:512], etg[g2][:], etgB[g2], psB[bk], ggB[g8])

            def recur(i):
                s = i % 2
                for tt in range(4):
                    j = 4 * i + tt
                    tsl = slice(tt * 128, (tt + 1) * 128)
                    g8, g4, g2 = j % 8, j % 4, j % 2
                    if i == 16:
                        P.op("sp", DMA(S[:, :, :], sg_d[l, tt]), writes=[SB_], dma=SB_)
                        P.op("act", CP(flat(Sb[:, :, :]), flat(S[:, :, :])), reads=[SB_], writes=[SbB])
                    bk = psrr.next()
                    for c in range(2):
                        P.op("pe", TR(ps[bk][:, c * 128:(c + 1) * 128], ktf[s][:, c, tsl], ID), reads=[ktfB[s], Bc], writes=[psB[bk]])
                    P.op("act", CP(kttm[g2][:], ps[bk][:, 0:256]), reads=[psB[bk]], writes=[kttmB[g2]])
                    bk = psrr.next()
                    for c in range(2):
                        P.op("pe", MM(ps[bk][:, 0:128], ktl[s][:, c, tsl], qtl[s][:, c, tsl], start=(c == 0), stop=(c == 1)),
                             reads=[ktlB[s], qtB[s]], writes=[psB[bk]])
                    P.op("dve", TT(AT[g2][:], ps[bk][:, 0:128], tri_f[:], ALU.mult), reads=[psB[bk], Bc], writes=[ATB[g2]])
                    bk = psrr.next()
                    P.op("pe", MM(ps[bk][:, 0:256], AT[g2][:], gvb[g8][:], start=True, stop=False),
                         reads=[ATB[g2], gvB[g8]], writes=[psB[bk]])
                    for c in range(2):
                        P.op("pe", MM(ps[bk][:, 0:256], qtl[s][:, c, tsl], Sb[:, c, :], start=False, stop=(c == 1)),
                             reads=[qtB[s], SbB], writes=[psB[bk]])
                    P.op("act", CP(osb[g2][:], ps[bk][:, 0:256]), reads=[psB[bk]], writes=[osbB[g2]])
                    bk = psrr.next()
                    for c in range(2):
                        P.op("pe", MM(ps[bk][:, c * 256:(c + 1) * 256], kttm[g2][:, c * 128:(c + 1) * 128], gvb[g8][:],
                                      start=(c == 0), stop=True, skip=True), reads=[kttmB[g2], gvB[g8]], writes=[psB[bk]])
                    P.op("dve", TT(flat(tmpS[:, :, :]), ps[bk][:, :], flat(S[:, :, :]), ALU.add), reads=[psB[bk], SB_], writes=[tSB])
                    for c in range(2):
                        P.op("dve", TS(S[:, c, :], tmpS[:, c, :], ebl[s][:, c, tt:tt + 1], None, ALU.mult), reads=[tSB, eblB[s]], writes=[SB_])
                    P.op("act", CP(flat(Sb[:, :, :]), flat(S[:, :, :])), reads=[SB_], writes=[SbB])
                    if i == 16:
                        P.op("sp", DMA(gs_d[l, tt], S[:, :, :]), reads=[SB_], dma=SB_)
                    elif j == 63:
                        P.op("sp", DMA(gp_d[l], S[:, :, :]), reads=[SB_], dma=SB_)
                    P.op("dve", STT(junk[:], osb[g2][:], 1.0, osb[g2][:], ALU.mult, ALU.mult, accum=ssq[g2][:, 0:1]),
                         reads=[osbB[g2]], writes=[osbB[g2]])
                    rstd_of(ssq[g2], 256, osbB[g2], lnbias=LNH)
                    P.op("dve", STT(t1[g2][:], osb[g2][:], ssq[g2][:, 2:3], gain[:], ALU.mult, ALU.mult),
                         reads=[osbB[g2], WB_], writes=[osbB[g2]])
                    P.op("pool", TT(obst[g4][:], t1[g2][:], ggs[g8][:], ALU.mult), reads=[osbB[g2], ggB[g8]], writes=[obB[g4]])
                    ostores.append(P.op("sp", DMA(osrcB[j * 128:(j + 1) * 128, :], obst[g4][:]), reads=[obB[g4]], dma=obB[g4]))

            load_hT_block(0, hTb[0], hTbB[0])
            load_hT_block(1, hTb[1], hTbB[1])
            proj(0)
            for i in range(NBLK):
                if i + 1 < NBLK:
                    proj(i + 1)
                    if i + 2 < NBLK:
                        load_hT_block(i + 2, hTb[i % 2], hTbB[i % 2])
                recur(i)
                if l + 1 < DEPTH:
                    emit_hw_casts((l + 1, "F"), 1)
                    emit_hw_casts((l + 1, "G"), 1)
                if i >= 4 and i % 4 == 0:
                    q = i // 4 - 1
                    ag_chunk(osrcB, oagB_d, oagBB, q, NTOK, 2048, ostores[0:16])
                    del ostores[0:16]
            ag_chunk(osrcB, oagB_d, oagBB, 4, NTOK, 2048, list(ostores))
            P.barrier()

        def token_phase(l):
            last = (l == DEPTH - 1)
            A.reset(base_mark)
            hTg = A.alloc("hTg", [128, KC, 512], BF16)
            hTgB = P.buf("hTg")
            oT = A.alloc("oT", [128, KC, 512], BF16)
            oTB = P.buf("oT")
            mT = A.alloc("mT", [128, KC, 512], BF16)
            mTB = P.buf("mT")
            z = A.alloc("z", [128, 4, D], F32)
            zB = P.bufs(4, "z")
            ogf = A.alloc("ogf", [128, D], F32)
            ogB = P.buf("ogf")
            wr = [dict(ma=A.alloc("wma", [128, KC, 128], BF16), mb=A.alloc("wmb", [128, KC, 128], BF16),
                       oa=A.alloc("woa", [128, 8, 128], BF16), ob=A.alloc("wob", [128, 8, 128], BF16)) for _ in range(2)]
            wrB = P.bufs(2, "wr")
            wo = [A.alloc("wo", [128, KC, 256], BF16) for _ in range(2)]
            woB = P.bufs(2, "wo")
            xt = A.alloc("xt", [128, D], F32)
            xB = P.buf("xt")
            yt = A.alloc("yt", [128, D], F32)
            yB = P.buf("yt")
            t1 = A.alloc("t1", [128, D], F32)
            tB1 = P.buf("t1")
            ssqz = A.alloc("ssqz", [128, 4], F32)
            post_t = A.alloc("post_t", [128, D], F32)
            pre_t = A.alloc("pre_t", [128, D], F32)
            ppB = P.buf("pp")
            P.op("sp", DMA(post_t[:], post_d[l]), writes=[ppB], dma=ppB)
            if not last:
                P.op("sp", DMA(pre_t[:], pre_d[l + 1]), writes=[ppB], dma=ppB)
            htmp = (A.alloc("junk", [128, D], BF16), A.alloc("ssq", [128, 4], F32), A.alloc("hf", [128, D], F32),
                    A.alloc("hTst", [128, KC, 128], BF16), P.buf("hTst"), P.buf("htmp"))
            ge = [A.alloc("ge", [128, 512], F32) for _ in range(4)]
            geB = P.bufs(4, "ge")
            gm = [A.alloc("gm", [128, 512], F32) for _ in range(4)]
            gmB = P.bufs(4, "gm")
            src_x = x_d if l == 0 else yscr

            wdeps = wcast_ops[l]

            def load_wr(c, sl):
                w, B = wr[sl], wrB[sl]
                P.op("sp", DMA(flat(w["ma"][:, :, :]), wm_b[l, c]), writes=[B], dma=B, extra_deps=wdeps)
                P.op("sp", DMA(flat(w["mb"][:, :, :]), wm_b[l, 16 + c]), writes=[B], dma=B, extra_deps=wdeps)
                P.op("sp", DMA(flat(w["oa"][:, :, :]), woa_b[l, c]), writes=[B], dma=B, extra_deps=wdeps)
                P.op("sp", DMA(flat(w["ob"][:, :, :]), wob_b[l, c]), writes=[B], dma=B, extra_deps=wdeps)

            def load_wo(blk, sl):
                P.op("sp", DMA(flat(wo[sl][:, :, :]), wout_b[l, blk]), writes=[woB[sl]], dma=woB[sl], extra_deps=wdeps)

            def stage_a_tile(grp, gi):
                t = grp[gi]
                for r in range(4):
                    for (src_, B_, c0_) in ((oagA, oagAB[t // 4], 0), (oagB_d, oagBB[t // 4], 256)):
                        P.op("pool", (lambda o_, i_, s_: (lambda e: e.indirect_dma_start(
                            out=o_, out_offset=None, in_=s_[:, :],
                            in_offset=bass.IndirectOffsetOnAxis(ap=i_, axis=0))))(
                                ogf[:, r * 512 + c0_:r * 512 + c0_ + 256], idx_t[:, t * 4 + r:t * 4 + r + 1], src_),
                            reads=[B_, Bc], writes=[ogB], dma=ogB)
                for q in range(4):
                    bk = psrr.next()
                    for jx in range(4):
                        c = q * 4 + jx
                        if c < 8:
                            col = (c // 2) * 512 + (c % 2) * 128
                        else:
                            cc = c - 8
                            col = (cc // 2) * 512 + 256 + (cc % 2) * 128
                        P.op("pe", TR(ps[bk][:, jx * 128:(jx + 1) * 128], ogf[:, col:col + 128], ID),
                             reads=[ogB, Bc], writes=[psB[bk]])
                    P.op("act", CP(oT[:, q * 4:(q + 1) * 4, gi * 128:(gi + 1) * 128], split(ps[bk][:, :], 4)),
                         reads=[psB[bk]], writes=[oTB])
                P.op("sp", DMA(hTg[:, :, gi * 128:(gi + 1) * 128], split(hsrc[t * 128:(t + 1) * 128, :], KC)),
                     reads=[hsrcB[t]], writes=[hTgB], dma=hTgB)

            def stage_b(grp, inter):
                ntok = len(grp) * 128
                load_wr(0, 0)
                for c in range(16):
                    sl = c % 2
                    if c + 1 < 16:
                        load_wr(c + 1, 1 - sl)
                    w, wB = wr[sl], wrB[sl]
                    bks = [psrr.next() for _ in range(4)]
                    for k in range(KC):
                        P.op("pe", MM(ps[bks[0]][:, 0:ntok], w["ma"][:, k, :], hTg[:, k, 0:ntok], start=(k == 0), stop=(k == KC - 1)),
                             reads=[wB, hTgB], writes=[psB[bks[0]]])
                    for k in range(KC):
                        P.op("pe", MM(ps[bks[1]][:, 0:ntok], w["mb"][:, k, :], hTg[:, k, 0:ntok], start=(k == 0), stop=(k == KC - 1)),
                             reads=[wB, hTgB], writes=[psB[bks[1]]])
                    for k in range(8):
                        P.op("pe", MM(ps[bks[2]][:, 0:ntok], w["oa"][:, k, :], oT[:, k, 0:ntok], start=(k == 0), stop=(k == 7)),
                             reads=[wB, oTB], writes=[psB[bks[2]]])
                    for k in range(8):
                        P.op("pe", MM(ps[bks[3]][:, 0:ntok], w["ob"][:, k, :], oT[:, 8 + k, 0:ntok], start=(k == 0), stop=(k == 7)),
                             reads=[wB, oTB], writes=[psB[bks[3]]])
                    for u in range(2):
                        gi_ = u + 2 * (c % 2)
                        gu = ge[gi_][:, 0:ntok]
                        P.op("act", ACT(gu, ps[bks[u]][:, 0:ntok], AF.Tanh, ZERO, 0.5), reads=[psB[bks[u]], Bc], writes=[geB[gi_]])
                        P.op("dve", STT(gm[gi_][:, 0:ntok], gu, 1.0, ps[bks[2 + u]][:, 0:ntok], ALU.add, ALU.mult),
                             reads=[psB[bks[2 + u]], geB[gi_]], writes=[gmB[gi_]])
                    g0, g1 = 2 * (c % 2), 1 + 2 * (c % 2)
                    P.op("dve", TT(mT[:, c, 0:ntok], gm[g0][:, 0:ntok], gm[g1][:, 0:ntok], ALU.add), reads=[gmB[g0], gmB[g1]], writes=[mTB])
                    if c in inter:
                        inter[c]()

            def stage_c(grp, inter):
                ng = len(grp)
                load_wo(0, 0)
                for blk in range(8):
                    sl = blk % 2
                    if blk + 1 < 8:
                        load_wo(blk + 1, 1 - sl)
                    for gi in range(ng):
                        bk = psrr.next()
                        for k in range(KC):
                            P.op("pe", MM(ps[bk][:, 0:256], mT[:, k, gi * 128:(gi + 1) * 128], wo[sl][:, k, :],
                                          start=(k == 0), stop=(k == KC - 1)), reads=[mTB, woB[sl]], writes=[psB[bk]])
                        P.op("act", MUL(z[:, gi, blk * 256:(blk + 1) * 256], ps[bk][:, 0:256], 0.5), reads=[psB[bk]], writes=[zB[gi]])
                    if blk in inter:
                        inter[blk]()

            def stage_d_tile(grp, gi):
                t = grp[gi]
                P.op("pool", DMA(xt[:], src_x[t * 128:(t + 1) * 128, :]), writes=[xB], dma=xB)
                P.op("dve", STT(htmp[0][:], z[:, gi, :], 1.0, z[:, gi, :], ALU.mult, ALU.mult, accum=ssqz[:, 0:1]),
                     reads=[zB[gi]], writes=[tB1, htmp[5]])
                rstd_of(ssqz, D, tB1)
                P.op("dve", STT(t1[:], z[:, gi, :], ssqz[:, 2:3], post_t[:], ALU.mult, ALU.mult), reads=[zB[gi], tB1, ppB], writes=[tB1])
                P.op("dve", TT(yt[:], t1[:], xt[:], ALU.add), reads=[tB1, xB], writes=[yB])
                if last:
                    P.op("pool", DMA(y_d[t * 128:(t + 1) * 128, :], yt[:]), reads=[yB], dma=yB)
                else:
                    P.op("pool", DMA(yscr[t * 128:(t + 1) * 128, :], yt[:]), reads=[yB], dma=yB)
                    hst[t] = emit_h(yt, yB, pre_t, ppB, t, htmp)
                    if t % 2 == 1 or t == TL - 1:
                        q = t // 2
                        ag_chunk(hsrc, hag, hagB, q, TL * 128, 256, [hst[t_] for t_ in range(2 * q, min(2 * q + 2, TL))])

            def spread(fns, slots):
                m = {}
                for n_, f in enumerate(fns):
                    sl_ = slots[min(n_, len(slots) - 1)]
                    m.setdefault(sl_, []).append(f)
                return {k_: (lambda fs=v_: [f() for f in fs]) for k_, v_ in m.items()}

            hst = {}
            G = len(TGROUPS)
            for gi in range(len(TGROUPS[0])):
                stage_a_tile(TGROUPS[0], gi)
            for g in range(G):
                grp = TGROUPS[g]
                d_fns = [] if g == 0 else [(lambda pg=TGROUPS[g - 1], x_=x: stage_d_tile(pg, x_)) for x in range(len(TGROUPS[g - 1]))]
                stage_b(grp, spread(d_fns, [1, 3, 5, 7]))
                a_fns = [] if g + 1 == G else [(lambda ng_=TGROUPS[g + 1], x_=x: stage_a_tile(ng_, x_)) for x in range(len(TGROUPS[g + 1]))]
                stage_c(grp, spread(a_fns, [1, 3, 5, 7]))
            for x in range(len(TGROUPS[G - 1])):
                stage_d_tile(TGROUPS[G - 1], x)
            P.barrier()

        def run_all():
            prologue()
            if STOP_AFTER == "pro":
                return
            for l in range(DEPTH):
                pass_f(l)
                if STOP_AFTER == "F%d" % l:
                    return
                pass_g(l)
                if STOP_AFTER == "G%d" % l:
                    return
                if STOP_AFTER == "AG%d" % l:
                    return
                token_phase(l)
                if STOP_AFTER == "T%d" % l:
                    return

        run_all()
        P.barrier()
        stats = P.emit()
        stats["sbuf_peak"] = A.peak
    return nc, stats


_OFF = dict(fq=0, fk=1024, fv=2048, ff=3072, fg=3080, gq=4104, gk=5128, gv=6152, glr=7176, gg=7192, ma=8216, mb=10264)


def _pkc(w):
    K = w.shape[0] // 128
    return np.ascontiguousarray(w.reshape(K, 128, w.shape[1]).transpose(1, 0, 2))


def _chunks(w, cw):
    K = w.shape[0] // 128
    NC = w.shape[1] // cw
    return np.ascontiguousarray(w.reshape(K, 128, NC, cw).transpose(2, 1, 0, 3))


def _prep(inp):
    f = lambda a: np.asarray(a, dtype=np.float32)
    x_prompt, x_sample = f(inp["x_prompt"]), f(inp["x_sample"])
    cache_k, cache_v, cache_logf, state_gla = f(inp["cache_k"]), f(inp["cache_v"]), f(inp["cache_logf"]), f(inp["state_gla"])
    w_in, w_a2, b_a, b_f = f(inp["w_in"]), f(inp["w_a2"]), f(inp["b_a"]), f(inp["b_f"])
    gla_gain, w_oa, w_ob, w_out = f(inp["gla_gain"]), f(inp["w_oa"]), f(inp["w_ob"]), f(inp["w_out"])
    pre_norm, post_norm = f(inp["pre_norm"]), f(inp["post_norm"])

    shared = {}
    shared["wm"] = np.stack([_chunks(w_in[l][:, _OFF["ma"]:_OFF["ma"] + 4096], 128) for l in range(DEPTH)])
    shared["woa"] = np.stack([_chunks(w_oa[l], 128) for l in range(DEPTH)])
    shared["wob"] = np.stack([_chunks(w_ob[l], 128) for l in range(DEPTH)])
    shared["wout"] = np.stack([_chunks(w_out[l], 256) for l in range(DEPTH)])
    shared["gain"] = np.ascontiguousarray(np.broadcast_to(gla_gain[:, None, :], (DEPTH, 128, 256)))
    shared["pre"] = np.ascontiguousarray(np.broadcast_to(pre_norm[:, None, :], (DEPTH, 128, D)))
    shared["post"] = np.ascontiguousarray(np.broadcast_to(post_norm[:, None, :], (DEPTH, 128, D)))

    maps = []
    for c in range(8):
        b, g = c // 4, c % 4
        m = dict(shared)
        xl = np.zeros((TL * 128, D), np.float32)
        for k in range(16):
            xl[k * 128:(k + 1) * 128] = x_prompt[b, (4 * k + g) * 128:(4 * k + g + 1) * 128]
        xl[2048:2080] = x_sample[4 * b + g]
        m["x_loc"] = xl
        fs = slice(256 * g, 256 * g + 256)

        def cols(l, name, sl=fs):
            o = _OFF[name]
            return w_in[l][:, o + sl.start:o + sl.stop]

        m["wffm"] = np.stack([_pkc(cols(l, "fq")) for l in range(DEPTH)])
        m["wftm"] = np.stack([_pkc(np.concatenate([cols(l, "fk"), cols(l, "fv"), cols(l, "fg"),
                                                   cols(l, "ff", slice(2 * g, 2 * g + 2))], 1)) for l in range(DEPTH)])
        m["wgfm"] = np.stack([_pkc(np.concatenate([cols(l, "gq"), cols(l, "gk"), cols(l, "glr", slice(0, 16))], 1))
                              for l in range(DEPTH)])
        m["wgtm"] = np.stack([_pkc(np.concatenate([cols(l, "gv"), cols(l, "gg")], 1)) for l in range(DEPTH)])
        m["wa2"] = np.ascontiguousarray(w_a2[:, :, fs])
        m["ba"] = np.ascontiguousarray(b_a[:, fs].reshape(DEPTH, 2, 128).transpose(0, 2, 1))
        m["bfb"] = np.ascontiguousarray(np.broadcast_to(b_f[:, None, 2 * g:2 * g + 2], (DEPTH, 128, 2)))
        sbs = slice(4 * b, 4 * b + 4)
        m["ck"] = np.ascontiguousarray(cache_k[:, sbs, :, 2 * g:2 * g + 2, :].reshape(DEPTH, 4, 4096, 256))
        m["cv"] = np.ascontiguousarray(cache_v[:, sbs, :, 2 * g:2 * g + 2, :].reshape(DEPTH, 4, 4096, 256))
        m["clf"] = np.ascontiguousarray(cache_logf[:, sbs, :, 2 * g:2 * g + 2].reshape(DEPTH, 4, 32, 128, 2).transpose(0, 1, 3, 2, 4))
        m["sg"] = np.ascontiguousarray(state_gla[:, sbs, g].reshape(DEPTH, 4, 2, 128, 256).transpose(0, 1, 3, 2, 4))
        idx = np.zeros((128, TL * 4), np.int32)
        p = np.arange(128)
        for t in range(TL):
            tok = ((4 * t + g) * 128 + p) if t < 16 else (8192 + 128 * g + p)
            q, within = tok // 2048, tok % 2048
            nq = np.where(q < 4, 2048, 512)
            for r in range(4):
                idx[:, t * 4 + r] = q * 8192 + r * nq + within
        m["idx"] = idx
        maps.append(m)
    return maps


_CACHE = {}


def kernel(**inputs):
    if "nc" not in _CACHE:
        _CACHE["nc"], _CACHE["stats"] = build_program()
    nc = _CACHE["nc"]
    maps = _prep(inputs)
    res = run_bass_kernel_spmd(nc, maps, core_ids=list(range(8)))
    R = res.results
    B, SEQ, DB, DS = 2, 8192, 8, 32
    y_prompt = np.zeros((B, SEQ, D), np.float32)
    y_sample = np.zeros((DB, DS, D), np.float32)
    k_prompt = np.zeros((DEPTH, B, SEQ, 8, 128), np.float32)
    v_prompt = np.zeros((DEPTH, B, SEQ, 8, 128), np.float32)
    logf_prompt = np.zeros((DEPTH, B, SEQ, 8), np.float32)
    gla_prompt = np.zeros((DEPTH, B, 4, 256, 256), np.float32)
    k_sample = np.zeros((DEPTH, DB, DS, 8, 128), np.float32)
    v_sample = np.zeros((DEPTH, DB, DS, 8, 128), np.float32)
    logf_sample = np.zeros((DEPTH, DB, DS, 8), np.float32)
    gla_sample = np.zeros((DEPTH, DB, 4, 256, 256), np.float32)
    for c in range(8):
        b, g = c // 4, c % 4
        r = R[c]
        y = np.asarray(r["y_loc"])
        for k in range(16):
            y_prompt[b, (4 * k + g) * 128:(4 * k + g + 1) * 128] = y[k * 128:(k + 1) * 128]
        y_sample[4 * b + g] = y[2048:2080]
        ko, vo = np.asarray(r["k_out"]), np.asarray(r["v_out"])
        lf = np.asarray(r["lf_out"]).transpose(0, 2, 1, 3).reshape(DEPTH, NTOK, 2)
        gp, gs = np.asarray(r["gla_p"]), np.asarray(r["gla_s"])
        for l in range(DEPTH):
            k_prompt[l, b, :, 2 * g:2 * g + 2, :] = ko[l, :SEQ].reshape(SEQ, 2, 128)
            v_prompt[l, b, :, 2 * g:2 * g + 2, :] = vo[l, :SEQ].reshape(SEQ, 2, 128)
            logf_prompt[l, b, :, 2 * g:2 * g + 2] = lf[l, :SEQ]
            gla_prompt[l, b, g] = gp[l].transpose(1, 0, 2).reshape(256, 256)
            for k4 in range(4):
                o = SEQ + 128 * k4
                k_sample[l, 4 * b + k4, :, 2 * g:2 * g + 2, :] = ko[l, o:o + DS].reshape(DS, 2, 128)
                v_sample[l, 4 * b + k4, :, 2 * g:2 * g + 2, :] = vo[l, o:o + DS].reshape(DS, 2, 128)
                logf_sample[l, 4 * b + k4, :, 2 * g:2 * g + 2] = lf[l, o:o + DS]
                gla_sample[l, 4 * b + k4, g] = gs[l, k4].transpose(1, 0, 2).reshape(256, 256)
    return (y_prompt, y_sample, k_prompt, v_prompt, logf_prompt, gla_prompt,
            k_sample, v_sample, logf_sample, gla_sample)
```

```python
import math
import numpy as np
from contextlib import ExitStack
import concourse.bass as bass
import concourse.mybir as mybir
from concourse.bass_utils import run_bass_kernel_spmd

F32, BF16, I32 = mybir.dt.float32, mybir.dt.bfloat16, mybir.dt.int32
AF = mybir.ActivationFunctionType
ALU = mybir.AluOpType

D = 2048
KC = 16
NT = 68
NBLK = 17
TL = 17
NTOK = NT * 128
DEPTH = 2
SCALE_F = 128 ** -0.5
GROUPS = [[0, 1, 2, 3], [4, 5, 6, 7]]
STOP_AFTER = None
TGROUPS = [[0, 1, 2, 3], [4, 5, 6, 7], [8, 9, 10, 11], [12, 13, 14], [15, 16]]
SB_BASE = 16512
SB_END = 229344


class Buf:
    __slots__ = ("name", "w", "r", "dsem", "dcnt")

    def __init__(self, name):
        self.name = name
        self.w = None
        self.r = []
        self.dsem = None
        self.dcnt = 0


class Op:
    __slots__ = ("eng", "fn", "deps", "dma", "token", "mark", "pos")

    def __init__(self, eng, fn, deps, dma):
        self.eng = eng
        self.fn = fn
        self.deps = deps
        self.dma = dma
        self.token = None
        self.mark = False
        self.pos = 0


class Prog:
    ENG = ("pe", "act", "dve", "pool", "sp")

    def __init__(self, nc, stack):
        self.nc = nc
        self.stack = stack
        self.ops = []
        self.h = {"pe": nc.tensor, "act": nc.scalar, "dve": nc.vector, "pool": nc.gpsimd, "sp": nc.sync}
        self.esem = {e: stack.enter_context(nc.semaphore("es_" + e)) for e in self.ENG}
        self.ccsem = stack.enter_context(nc.semaphore("ccsem"))
        self.cccnt = 0
        self.last = {e: None for e in self.ENG}
        self.pending = []
        self.nbuf = 0
        self.nsem = 6

    def buf(self, name=None):
        self.nbuf += 1
        return Buf("%s_%d" % (name or "b", self.nbuf))

    def bufs(self, n, name="b"):
        return [self.buf(name) for _ in range(n)]

    def _dsem(self, b):
        if b.dsem is None:
            b.dsem = self.stack.enter_context(self.nc.semaphore("ds_" + b.name))
            self.nsem += 1
        return b.dsem

    def op(self, eng, fn, reads=(), writes=(), dma=None, cc=False, extra_deps=(), nobarrier=False):
        idx = len(self.ops)
        deps = set(extra_deps)
        for b in reads:
            if b.w is not None:
                deps.add(b.w)
            b.r.append(idx)
        for b in writes:
            if b.w is not None:
                pw = self.ops[b.w]
                if dma is not None and pw.dma is dma and not b.r:
                    deps.update(pw.deps)
                else:
                    deps.add(b.w)
            deps.update(b.r)
            b.r = []
            b.w = idx
        deps.discard(idx)
        latest = {}
        keep = []
        for di in deps:
            d = self.ops[di]
            if d.dma is None and d.fn is not None:
                if d.eng not in latest or latest[d.eng] < di:
                    latest[d.eng] = di
            else:
                keep.append(di)
        deps = keep + list(latest.values())
        o = Op(eng, fn, sorted(deps), dma)
        if dma is not None:
            sem = self._dsem(dma)
            dma.dcnt += 16
            o.token = (sem, dma.dcnt)
            if not nobarrier:
                self.pending.append(idx)
        elif cc:
            self.cccnt += 1
            o.token = (self.ccsem, self.cccnt)
            o.dma = "cc"
        self.ops.append(o)
        if fn is not None and not cc:
            self.last[eng] = idx
        return idx

    def barrier(self):
        deps = [v for v in self.last.values() if v is not None] + list(self.pending)
        self.pending = []
        for e in self.ENG:
            self.op(e, None, extra_deps=deps)

    def _needs_wait(self, o, d):
        if d.dma is not None:
            return True
        if d.eng != o.eng:
            return True
        if o.dma is not None:
            return True
        if o.eng == "pe":
            return False
        return True

    def emit(self):
        ops = self.ops
        pos = {e: 0 for e in self.ENG}
        for o in ops:
            if o.fn is not None:
                pos[o.eng] += 1
            o.pos = pos[o.eng]
        for o in ops:
            for di in o.deps:
                d = ops[di]
                if d.dma is None and d.fn is not None and self._needs_wait(o, d):
                    d.mark = True
        cnt = {e: 0 for e in self.ENG}
        for o in ops:
            if o.dma is None and o.mark:
                cnt[o.eng] += 1
                o.token = (self.esem[o.eng], cnt[o.eng])
        seen = {e: {} for e in self.ENG}
        nwait = 0
        for o in ops:
            E = self.h[o.eng]
            waits = {}
            for di in o.deps:
                d = ops[di]
                if d.fn is None or not self._needs_wait(o, d):
                    continue
                sem, val = d.token
                k = id(sem)
                if k not in waits or waits[k][1] < val:
                    waits[k] = (sem, val)
            for k, (sem, val) in waits.items():
                if seen[o.eng].get(k, 0) < val:
                    E.wait_ge(sem, val)
                    seen[o.eng][k] = val
                    nwait += 1
            if o.fn is None:
                continue
            ins = o.fn(E)
            if o.dma is not None:
                ins.then_inc(o.token[0], 1 if o.dma == "cc" else 16)
            elif o.mark:
                ins.then_inc(o.token[0], 1)
        return dict(n_ops=len(ops), n_wait=nwait, marked=cnt, nsem=self.nsem)


class Arena:
    def __init__(self, nc):
        self.nc = nc
        self.off = SB_BASE
        self.n = 0
        self.peak = 0

    def alloc(self, name, shape, dtype):
        nbytes = int(np.prod(shape[1:])) * (4 if dtype in (F32, I32) else 2)
        nbytes = (nbytes + 31) // 32 * 32
        assert self.off + nbytes <= SB_END, (name, self.off, nbytes)
        self.n += 1
        t = self.nc.alloc_sbuf_tensor_at("%s_%d" % (name, self.n), list(shape), dtype, offset=self.off)
        self.off += nbytes
        self.peak = max(self.peak, self.off)
        return t

    def mark(self):
        return self.off

    def reset(self, m):
        self.off = m


class RR:
    def __init__(self, items):
        self.items = list(items)
        self.i = 0

    def next(self):
        x = self.items[self.i % len(self.items)]
        self.i += 1
        return x


def MM(out, lhsT, rhs, start=True, stop=True, skip=False):
    return lambda e: e.matmul(out, lhsT=lhsT, rhs=rhs, start=start, stop=stop, skip_group_check=skip)


def TR(out, in_, ident):
    return lambda e: e.transpose(out=out, in_=in_, identity=ident)


def ACT(out, in_, func, bias, scale):
    return lambda e: e.activation(out=out, in_=in_, func=func, bias=bias, scale=scale)


def MUL(out, in_, c):
    return lambda e: e.mul(out, in_, c)


def CP(out, in_):
    return lambda e: e.copy(out=out, in_=in_)


def TC(out, in_):
    return lambda e: e.tensor_copy(out=out, in_=in_)


def TT(out, in0, in1, op):
    return lambda e: e.tensor_tensor(out=out, in0=in0, in1=in1, op=op)


def TS(out, in0, s1, s2, op0, op1=None):
    if op1 is None:
        return lambda e: e.tensor_scalar(out=out, in0=in0, scalar1=s1, scalar2=None, op0=op0)
    return lambda e: e.tensor_scalar(out=out, in0=in0, scalar1=s1, scalar2=s2, op0=op0, op1=op1)


def STT(out, in0, scalar, in1, op0, op1, accum=None):
    if accum is None:
        return lambda e: e.scalar_tensor_tensor(out=out, in0=in0, scalar=scalar, in1=in1, op0=op0, op1=op1)
    return lambda e: e.scalar_tensor_tensor(out=out, in0=in0, scalar=scalar, in1=in1, op0=op0, op1=op1, accum_out=accum)


def SCAN(out, d0, d1, init):
    return lambda e: e.tensor_tensor_scan(out=out, data0=d0, data1=d1, initial=init, op0=ALU.mult, op1=ALU.add)


def RCP(out, in_):
    return lambda e: e.reciprocal(out=out, in_=in_)


def MS(ap, v):
    return lambda e: e.memset(ap, v)


def DMA(out, in_):
    return lambda e: e.dma_start(out=out, in_=in_)


def flat(ap3):
    n = len(ap3.shape)
    if n == 3:
        return ap3.rearrange("p a b -> p (a b)")
    if n == 4:
        return ap3.rearrange("p a b c -> p (a b c)")
    return ap3


def split(ap2, a):
    return ap2.rearrange("p (a b) -> p a b", a=a)


def build_program():
    nc = bass.Bass("TRN2", target_bir_lowering=False)

    def din(name, shape, dt=F32):
        return nc.dram_tensor(name, list(shape), dt, kind="ExternalInput").ap()

    def dout(name, shape, dt=F32):
        return nc.dram_tensor(name, list(shape), dt, kind="ExternalOutput").ap()

    x_d = din("x_loc", [TL * 128, D])
    wffm_d = din("wffm", [DEPTH, 128, KC, 256])
    wftm_d = din("wftm", [DEPTH, 128, KC, 770])
    wgfm_d = din("wgfm", [DEPTH, 128, KC, 528])
    wgtm_d = din("wgtm", [DEPTH, 128, KC, 512])
    wm_d = din("wm", [DEPTH, 32, 128, KC, 128])
    woa_d = din("woa", [DEPTH, 16, 128, 8, 128])
    wob_d = din("wob", [DEPTH, 16, 128, 8, 128])
    wout_d = din("wout", [DEPTH, 8, 128, KC, 256])
    wa2_d = din("wa2", [DEPTH, 16, 256])
    ba_d = din("ba", [DEPTH, 128, 2])
    bfb_d = din("bfb", [DEPTH, 128, 2])
    gain_d = din("gain", [DEPTH, 128, 256])
    pre_d = din("pre", [DEPTH, 128, D])
    post_d = din("post", [DEPTH, 128, D])
    ck_d = din("ck", [DEPTH, 4, 4096, 256])
    cv_d = din("cv", [DEPTH, 4, 4096, 256])
    clf_d = din("clf", [DEPTH, 4, 128, 32, 2])
    sg_d = din("sg", [DEPTH, 4, 128, 2, 256])
    idx_d = din("idx", [128, TL * 4], I32)

    y_d = dout("y_loc", [TL * 128, D])
    ko_d = dout("k_out", [DEPTH, NTOK, 256])
    vo_d = dout("v_out", [DEPTH, NTOK, 256])
    lfo_d = dout("lf_out", [DEPTH, 128, NT, 2])
    gp_d = dout("gla_p", [DEPTH, 128, 2, 256])
    gs_d = dout("gla_s", [DEPTH, 4, 128, 2, 256])

    hsrc = nc.dram_tensor("hsrc", [TL * 128, D], BF16).ap()
    hag = nc.dram_tensor("hag", [4 * TL * 128, D], BF16).ap()
    osrcA = nc.dram_tensor("osrcA", [NTOK, 256], BF16).ap()
    osrcB = nc.dram_tensor("osrcB", [NTOK, 256], BF16).ap()
    oagA = nc.dram_tensor("oagA", [4 * NTOK, 256], BF16).ap()
    oagB_d = nc.dram_tensor("oagB", [4 * NTOK, 256], BF16).ap()
    yscr = nc.dram_tensor("yscr", [TL * 128, D], F32).ap()
    wm_b = nc.dram_tensor("wm_b", [DEPTH, 32, 128, KC * 128], BF16).ap()
    wffm_b = nc.dram_tensor("wffm_b", [DEPTH, 128, KC * 256], BF16).ap()
    wftm_b = nc.dram_tensor("wftm_b", [DEPTH, 128, KC * 770], BF16).ap()
    wgfm_b = nc.dram_tensor("wgfm_b", [DEPTH, 128, KC * 528], BF16).ap()
    wgtm_b = nc.dram_tensor("wgtm_b", [DEPTH, 128, KC * 512], BF16).ap()
    woa_b = nc.dram_tensor("woa_b", [DEPTH, 16, 128, 8 * 128], BF16).ap()
    wob_b = nc.dram_tensor("wob_b", [DEPTH, 16, 128, 8 * 128], BF16).ap()
    wout_b = nc.dram_tensor("wout_b", [DEPTH, 8, 128, KC * 256], BF16).ap()

    with ExitStack() as st:
        P = Prog(nc, st)
        A = Arena(nc)
        ps = [nc.alloc_psum_tensor("psb%d" % i, [128, 512], F32) for i in range(8)]
        psB = [P.buf("ps%d" % i) for i in range(8)]

        ones_f = A.alloc("ones_f", [128, 128], F32)
        ident_f = A.alloc("ident_f", [128, 128], F32)
        tri_f = A.alloc("tri_f", [128, 128], F32)
        tri_b = A.alloc("tri_b", [128, 128], BF16)
        resetm = A.alloc("resetm", [128, 512], F32)
        cst = A.alloc("cst", [128, 8], F32)
        idx_t = A.alloc("idx_t", [128, TL * 4], I32)
        Bc = P.buf("consts")
        ID = ident_f[:]

        P.op("pool", MS(ones_f[:], 1.0), writes=[Bc])
        P.op("pool", MS(ident_f[:], 0.0), writes=[Bc])
        P.op("pool", lambda e: e.affine_select(out=ident_f[:], in_=ones_f[:], pattern=[[-1, 128]],
                                               compare_op=ALU.is_equal, fill=0.0, base=0, channel_multiplier=1),
             writes=[Bc])
        P.op("pool", MS(tri_f[:], 0.0), writes=[Bc])
        P.op("pool", lambda e: e.affine_select(out=tri_f[:], in_=ones_f[:], pattern=[[1, 128]],
                                               compare_op=ALU.is_ge, fill=0.0, base=0, channel_multiplier=-1),
             writes=[Bc])
        P.op("pool", TC(tri_b[:], tri_f[:]), writes=[Bc])
        P.op("pool", MS(resetm[:], 1.0), writes=[Bc])
        for q in range(4):
            P.op("pool", MS(resetm[:, q * 128:q * 128 + 1], 0.0), writes=[Bc])
        P.op("pool", MS(cst[:, 0:1], 1e-6), writes=[Bc])
        P.op("pool", MS(cst[:, 1:2], 1.0), writes=[Bc])
        P.op("pool", MS(cst[:, 2:3], -math.log(16.0)), writes=[Bc])
        P.op("pool", MS(cst[:, 3:4], 0.0), writes=[Bc])
        P.op("pool", MS(cst[:, 4:5], math.log(0.5)), writes=[Bc])
        P.op("sp", DMA(idx_t[:], idx_d), writes=[Bc], dma=Bc)
        EPS, ONE, NL16, ZERO, LNH = cst[:, 0:1], cst[:, 1:2], cst[:, 2:3], cst[:, 3:4], cst[:, 4:5]

        hsrcB = P.bufs(TL, "hsrc")
        hagB = P.bufs(9, "hag")
        oagAB = P.bufs(5, "oagA")
        oagBB = P.bufs(5, "oagB")
        base_mark = A.mark()
        psrr = RR(range(8))

        def rstd_of(ssq, n, B, lnbias=None):
            P.op("act", ACT(ssq[:, 1:2], ssq[:, 0:1], AF.Ln, EPS, 1.0 / n), reads=[B, Bc], writes=[B])
            P.op("act", ACT(ssq[:, 2:3], ssq[:, 1:2], AF.Exp, ZERO if lnbias is None else lnbias, -0.5), reads=[B, Bc], writes=[B])

        def emit_h(y_t, yB, pre_t, preB, t, tmp, q_="pool"):
            junk, ssq, hf, hTst, hTstB, tB = tmp
            P.op("dve", STT(junk[:], y_t[:], 1.0, y_t[:], ALU.mult, ALU.mult, accum=ssq[:, 0:1]), reads=[yB], writes=[tB])
            rstd_of(ssq, D, tB)
            P.op("dve", STT(hf[:], y_t[:], ssq[:, 2:3], pre_t[:], ALU.mult, ALU.mult), reads=[yB, tB, preB], writes=[tB])
            for q in range(4):
                bk = psrr.next()
                for j in range(4):
                    c = q * 4 + j
                    P.op("pe", TR(ps[bk][:, j * 128:(j + 1) * 128], hf[:, c * 128:(c + 1) * 128], ID),
                         reads=[tB, Bc], writes=[psB[bk]])
                P.op("act", CP(hTst[:, q * 4:(q + 1) * 4, :], split(ps[bk][:, :], 4)), reads=[psB[bk]], writes=[hTstB])
            return P.op(q_, DMA(hsrc[t * 128:(t + 1) * 128, :], flat(hTst[:, :, :])), reads=[hTstB], writes=[hsrcB[t]], dma=hTstB)

        def ag_chunk(src, dst, dstB, q, rows_total, rows_chunk, store_ops):
            r0 = q * rows_chunk
            n = min(rows_chunk, rows_total - r0)
            P.op("pool", (lambda i_, o_: (lambda e: e.collective_compute(
                "AllGather", ALU.bypass, replica_groups=GROUPS, ins=[i_], outs=[o_])))(
                    src[r0:r0 + n, :], dst[q * 4 * rows_chunk:q * 4 * rows_chunk + 4 * n, :]),
                writes=[dstB[q]], cc=True, extra_deps=store_ops)

        def cumsum_tiles(bkrr, lf_ap, n, carry_ap, out_ap, tmp, tB, lfB, outB, carry_out_ap=None):
            bk = bkrr.next()
            sb_, incl = tmp
            P.op("pe", MM(ps[bk][:, 0:2 * n], tri_f[:], flat(lf_ap)), reads=[lfB, Bc], writes=[psB[bk]])
            P.op("pe", MM(ps[bk][:, 2 * n:4 * n], ones_f[:], flat(lf_ap), start=False, stop=True, skip=True),
                 reads=[lfB, Bc], writes=[psB[bk]])
            P.op("act", CP(sb_[:, 0:4 * n], ps[bk][:, 0:4 * n]), reads=[psB[bk]], writes=[tB])
            tot = sb_[:, 2 * n:4 * n].rearrange("p (a b) -> p a b", b=2)
            loc = sb_[:, 0:2 * n].rearrange("p (a b) -> p a b", b=2)
            inc3 = incl[:, 0:2 * n].rearrange("p (a b) -> p a b", b=2)
            for hh in range(2):
                P.op("dve", SCAN(inc3[:, :, hh], ones_f[:, 0:n], tot[:, :, hh], carry_ap[:, hh:hh + 1]),
                     reads=[tB, Bc, outB], writes=[tB])
            P.op("dve", TT(loc, loc, tot, ALU.subtract), reads=[tB], writes=[tB])
            P.op("dve", TT(out_ap, loc, inc3, ALU.add), reads=[tB], writes=[outB])
            if carry_out_ap is not None:
                P.op("dve", TC(carry_out_ap, inc3[:, n - 1, :]), reads=[tB], writes=[outB])

        def load_hT_block(i, hTb_t, hTbB):
            q, tl = i // 2, i % 2
            nq = 256 if q < 8 else 128
            for tt in range(4):
                row = q * 1024 + tt * nq + tl * 128
                P.op("sp", DMA(hTb_t[:, :, tt * 128:(tt + 1) * 128], split(hag[row:row + 128, :], KC)),
                     reads=[hagB[q]], writes=[hTbB], dma=hTbB)

        hwB = {}
        hw_ops = {}
        hw_jobs = {}
        for l_ in range(DEPTH):
            for ps_, mats in (("F", ((wffm_b, wffm_d, 256, 8), (wftm_b, wftm_d, 770, 2))),
                              ("G", ((wgfm_b, wgfm_d, 528, 3), (wgtm_b, wgtm_d, 512, 4)))):
                hwB[(l_, ps_)] = P.buf("hwcast%d%s" % (l_, ps_))
                hw_ops[(l_, ps_)] = []
                jl = []
                for (dst_, src_, cols, kk) in mats:
                    for k0 in range(0, KC, kk):
                        k1 = min(KC, k0 + kk)
                        jl.append((dst_[l_, :, k0 * cols:k1 * cols], flat(src_[l_, :, k0:k1, :])))
                hw_jobs[(l_, ps_)] = jl

        def emit_hw_casts(key, n=None):
            jl = hw_jobs[key]
            for _ in range(len(jl) if n is None else min(n, len(jl))):
                o_, i_ = jl.pop(0)
                hw_ops[key].append(P.op("pool", DMA(o_, i_), dma=hwB[key], nobarrier=True))

        emit_hw_casts((0, "F"))

        def load_w_bf(dst_t, src_ap, B, key):
            emit_hw_casts(key)
            P.op("sp", DMA(flat(dst_t[:, :, :]), src_ap), writes=[B], dma=B, extra_deps=hw_ops[key])

        def silu2_from_psum(out_ap, ps_ap, et_ap, etB_, psBuf, outB):
            P.op("act", ACT(et_ap, ps_ap, AF.Tanh, ZERO, 0.5), reads=[psBuf, Bc], writes=[etB_])
            P.op("dve", STT(out_ap, et_ap, 1.0, ps_ap, ALU.add, ALU.mult), reads=[psBuf, etB_], writes=[outB])

        wcastB = [P.buf("wcast%d" % l_) for l_ in range(DEPTH)]
        wcast_ops = [[] for _ in range(DEPTH)]

        def cast_jobs(l):
            jobs = []
            for c in range(32):
                jobs.append((wm_b[l, c], flat(wm_d[l, c])))
            for c in range(16):
                jobs.append((woa_b[l, c], flat(woa_d[l, c])))
                jobs.append((wob_b[l, c], flat(wob_d[l, c])))
            for blk in range(8):
                for hf_ in range(2):
                    jobs.append((wout_b[l, blk, :, hf_ * 2048:(hf_ + 1) * 2048], flat(wout_d[l, blk, :, hf_ * 8:(hf_ + 1) * 8, :])))
            return jobs

        def emit_casts(l, jobs, n):
            for _ in range(min(n, len(jobs))):
                o_, i_ = jobs.pop(0)
                wcast_ops[l].append(P.op("pool", DMA(o_, i_), dma=wcastB[l], nobarrier=True))

        def prologue():
            A.reset(base_mark)
            pre_t = A.alloc("pre_t", [128, D], F32)
            preB = P.buf("pre")
            P.op("sp", DMA(pre_t[:], pre_d[0]), writes=[preB], dma=preB)
            xts = [A.alloc("xt", [128, D], F32) for _ in range(2)]
            xBs = P.bufs(2, "xt")
            tmps = []
            for _ in range(2):
                tmps.append((A.alloc("junk", [128, D], BF16), A.alloc("ssq", [128, 4], F32), A.alloc("hf", [128, D], F32),
                             A.alloc("hTst", [128, KC, 128], BF16), P.buf("hTst"), P.buf("htmp")))
            sts = []
            P.op("sp", DMA(xts[0][:], x_d[0:128, :]), writes=[xBs[0]], dma=xBs[0])
            for t in range(TL):
                s = t % 2
                if t + 1 < TL:
                    P.op("sp", DMA(xts[1 - s][:], x_d[(t + 1) * 128:(t + 2) * 128, :]), writes=[xBs[1 - s]], dma=xBs[1 - s])
                sts.append(emit_h(xts[s], xBs[s], pre_t, preB, t, tmps[s], q_="sp"))
                if t % 2 == 1 or t == TL - 1:
                    ag_chunk(hsrc, hag, hagB, t // 2, TL * 128, 256, sts)
                    sts = []
            P.barrier()

        def pass_f(l):
            A.reset(base_mark)
            WFfm = A.alloc("WFfm", [128, KC, 256], BF16)
            WFtm = A.alloc("WFtm", [128, KC, 770], BF16)
            WB_ = P.buf("WF")
            load_w_bf(WFfm, wffm_b[l], WB_, (l, "F"))
            load_w_bf(WFtm, wftm_b[l], WB_, (l, "F"))
            bfb = A.alloc("bfb", [128, 2], F32)
            P.op("sp", DMA(bfb[:], bfb_d[l]), writes=[WB_], dma=WB_)
            kT = A.alloc("kT", [128, 2, NTOK], BF16)
            Vg = A.alloc("Vg", [128, NT, 2, 129], BF16)
            kTB = P.bufs(NT, "kT")
            VB = P.bufs(NT, "V")
            P.op("dve", MS(flat(Vg[:, :, :, :]), 2.0), writes=VB)
            hTb = [A.alloc("hTb", [128, KC, 512], BF16) for _ in range(2)]
            hTbB = P.bufs(2, "hTb")
            qT = [A.alloc("qT", [128, 2, 512], BF16) for _ in range(2)]
            qTB = P.bufs(2, "qT")
            kvf = [A.alloc("kvf", [128, 512], F32) for _ in range(2)]
            kvfB = P.bufs(2, "kvf")
            et = [A.alloc("et", [128, 256], F32) for _ in range(2)]
            etB = P.bufs(2, "et")
            fgs = [A.alloc("fgs", [128, 4, 256], BF16) for _ in range(2)]
            fgsB = P.bufs(2, "fgs")
            lfpre = [A.alloc("lfpre", [128, 4, 2], F32) for _ in range(2)]
            lfpB = P.bufs(2, "lfp")
            LF = A.alloc("LF", [128, NT, 2], F32)
            LFB = P.buf("LF")
            C = A.alloc("C", [128, NT, 2], F32)
            CB = P.buf("C")
            carry = A.alloc("carry", [128, 2], F32)
            cref = A.alloc("cref", [128, NBLK, 2], F32)
            cs_tmp = (A.alloc("cs_sb", [128, 128], F32), A.alloc("cs_incl", [128, 64], F32))
            csB = P.buf("cstmp")
            btab = [A.alloc("btab", [128, 2, NT], F32) for _ in range(2)]
            btB = P.bufs(2, "btab")
            PT = [A.alloc("PT", [128, 512], BF16) for _ in range(4)]
            PTB = P.bufs(4, "PT")
            ptrr = RR(range(4))
            ost = [A.alloc("ost", [128, 256], BF16) for _ in range(8)]
            ostB = P.bufs(8, "ost")
            rec = [A.alloc("rec", [128, 4], F32) for _ in range(8)]
            ckf = [A.alloc("ckf", [128, 256], F32) for _ in range(3)]
            ckfB = P.bufs(3, "ckf")
            ckT = [A.alloc("ckT", [128, 2, 128], BF16) for _ in range(3)]
            ckTB = P.bufs(3, "ckT")
            cV = [A.alloc("cV", [128, 2, 129], BF16) for _ in range(3)]
            cVB = P.bufs(3, "cV")
            for s3 in range(3):
                P.op("dve", MS(flat(cV[s3][:, :, :]), 2.0), writes=[cVB[s3]])
            clf = A.alloc("clf", [128, 32, 2], F32)
            clfB = P.buf("clf")
            cC = A.alloc("cC", [128, 32, 2], F32)
            cCB = P.buf("cC")
            ccar = A.alloc("ccar", [128, 2], F32)
            Cn = A.alloc("Cn", [128, 1, 2], F32)
            btS = A.alloc("btS", [128, 2, 33], F32)
            btSB = P.buf("btS")
            PTs = [A.alloc("PTs", [128, 64], BF16) for _ in range(3)]
            PTsB = P.bufs(3, "PTs")

            P.op("dve", MS(carry[:], 0.0), writes=[CB])
            pj = RR([0, 1, 2, 3])
            pjs = RR([0])
            stb = RR([1, 2, 3])
            OB = [(4, 5), (6, 7)]

            def sample_attention(s):
                for k4 in range(4):
                    j = 64 + k4
                    P.op("sp", DMA(clf[:], clf_d[l, k4]), writes=[clfB], dma=clfB)
                    P.op("dve", MS(ccar[:], 0.0), writes=[cCB])
                    cumsum_tiles(pjs, clf[:, :, :], 32, ccar, cC[:, :, :], cs_tmp, csB, clfB, cCB, carry_out_ap=ccar[:, :])
                    cumsum_tiles(pjs, LF[:, j:j + 1, :], 1, ccar, Cn[:, :, :], cs_tmp, csB, LFB, cCB)
                    for h in range(2):
                        P.op("dve", TS(btS[:, h, 0:32], cC[:, :, h], -1.0, ccar[:, h:h + 1], ALU.mult, ALU.add),
                             reads=[cCB], writes=[btSB])
                        P.op("dve", TS(btS[:, h, 32:33], Cn[:, :, h], -1.0, ccar[:, h:h + 1], ALU.mult, ALU.add),
                             reads=[cCB], writes=[btSB])
                    bO = OB[k4 % 2][0]
                    qsl = slice(k4 * 128, k4 * 128 + 32)
                    for jt in range(33):
                        r3 = (k4 * 33 + jt) % 3
                        sk = stb.next()
                        if jt < 32:
                            P.op("sp", DMA(ckf[r3][:], ck_d[l, k4, jt * 128:(jt + 1) * 128, :]), writes=[ckfB[r3]], dma=ckfB[r3])
                            P.op("pool", DMA(cV[r3][:, :, 0:128], split(cv_d[l, k4, jt * 128:(jt + 1) * 128, :], 2)),
                                 writes=[cVB[r3]], dma=cVB[r3])
                            bk2 = pjs.next()
                            for h in range(2):
                                P.op("pe", TR(ps[bk2][:, h * 128:(h + 1) * 128], ckf[r3][:, h * 128:(h + 1) * 128], ID),
                                     reads=[ckfB[r3], Bc], writes=[psB[bk2]])
                            P.op("dve", TC(ckT[r3][:, :, :], split(ps[bk2][:, 0:256], 2)), reads=[psB[bk2]], writes=[ckTB[r3]])
                            for h in range(2):
                                P.op("pe", MM(ps[sk][:, h * 32:(h + 1) * 32], ckT[r3][:, h, :], qT[s][:, h, qsl],
                                              start=(h == 0), stop=True, skip=True),
                                     reads=[ckTB[r3], qTB[s]], writes=[psB[sk]])
                            for h in range(2):
                                P.op("act", ACT(PTs[r3][:, h * 32:(h + 1) * 32], ps[sk][:, h * 32:(h + 1) * 32], AF.Exp,
                                                btS[:, h, jt:jt + 1], SCALE_F), reads=[psB[sk], btSB], writes=[PTsB[r3]])
                            for h in range(2):
                                P.op("pe", MM(ps[bO][0:32, h * 129:(h + 1) * 129], PTs[r3][:, h * 32:(h + 1) * 32], cV[r3][:, h, :],
                                              start=(jt == 0 and h == 0), stop=False, skip=True),
                                     reads=[PTsB[r3], cVB[r3]], writes=[psB[bO]])
                        else:
                            for h in range(2):
                                P.op("pe", MM(ps[sk][0:32, h * 32:(h + 1) * 32], kT[:, h, j * 128:j * 128 + 32], qT[s][:, h, qsl],
                                              start=(h == 0), stop=True, skip=True),
                                     reads=[kTB[j], qTB[s]], writes=[psB[sk]])
                            for h in range(2):
                                P.op("act", ACT(PTs[r3][0:32, h * 32:(h + 1) * 32], ps[sk][0:32, h * 32:(h + 1) * 32], AF.Exp,
                                                btS[0:32, h, 32:33], SCALE_F), reads=[psB[sk], btSB], writes=[PTsB[r3]])
                                P.op("pool", TT(PTs[r3][0:32, h * 32:(h + 1) * 32], PTs[r3][0:32, h * 32:(h + 1) * 32],
                                                tri_b[0:32, 0:32], ALU.mult), reads=[PTsB[r3], Bc], writes=[PTsB[r3]])
                            for h in range(2):
                                P.op("pe", MM(ps[bO][0:32, h * 129:(h + 1) * 129], PTs[r3][0:32, h * 32:(h + 1) * 32],
                                              Vg[0:32, j, h, :], start=False, stop=True, skip=True),
                                     reads=[PTsB[r3], VB[j]], writes=[psB[bO]])
                    oi = j % 8
                    P.op("pool", MS(ost[oi][:], 0.0), writes=[ostB[oi]])
                    for h in range(2):
                        P.op("dve", RCP(rec[oi][0:32, h:h + 1], ps[bO][0:32, h * 129 + 128:h * 129 + 129]),
                             reads=[psB[bO]], writes=[ostB[oi]])
                        P.op("dve", STT(ost[oi][0:32, h * 128:(h + 1) * 128], ps[bO][0:32, h * 129:h * 129 + 128],
                                        rec[oi][0:32, h:h + 1], fgs[s][0:32, k4, h * 128:(h + 1) * 128], ALU.mult, ALU.mult),
                             reads=[psB[bO], fgsB[s], ostB[oi]], writes=[ostB[oi]])
                    oa_st.append(P.op("sp", DMA(osrcA[j * 128:(j + 1) * 128, :], ost[oi][:]), reads=[ostB[oi]], dma=ostB[oi]))

            def block(i):
                s = i % 2
                if i + 1 < NBLK:
                    load_hT_block(i + 1, hTb[1 - s], hTbB[1 - s])
                hb, hbB = hTb[s], hTbB[s]
                for h in range(2):
                    bk = pj.next()
                    for k in range(KC):
                        P.op("pe", MM(ps[bk][:, :], WFfm[:, k, h * 128:(h + 1) * 128], hb[:, k, :], start=(k == 0), stop=(k == KC - 1)),
                             reads=[WB_, hbB], writes=[psB[bk]])
                    P.op("dve", TC(qT[s][:, h, :], ps[bk][:, :]), reads=[psB[bk]], writes=[qTB[s]])
                for tt in range(4):
                    j = 4 * i + tt
                    ks = j % 2
                    tsl = slice(tt * 128, (tt + 1) * 128)
                    bk = pj.next()
                    for k in range(KC):
                        P.op("pe", MM(ps[bk][:, :], hb[:, k, tsl], WFtm[:, k, 0:512], start=(k == 0), stop=(k == KC - 1)),
                             reads=[WB_, hbB], writes=[psB[bk]])
                    P.op("dve", TC(kvf[ks][:], ps[bk][:, :]), reads=[psB[bk]], writes=[kvfB[ks]])
                    P.op("sp", DMA(ko_d[l, j * 128:(j + 1) * 128, :], kvf[ks][:, 0:256]), reads=[kvfB[ks]], dma=kvfB[ks])
                    P.op("sp", DMA(vo_d[l, j * 128:(j + 1) * 128, :], kvf[ks][:, 256:512]), reads=[kvfB[ks]], dma=kvfB[ks])
                    P.op("act", CP(Vg[:, j, :, 0:128], split(kvf[ks][:, 256:512], 2)), reads=[kvfB[ks]], writes=[VB[j]])
                    bk = pj.next()
                    for k in range(KC):
                        P.op("pe", MM(ps[bk][:, 0:258], hb[:, k, tsl], WFtm[:, k, 512:770], start=(k == 0), stop=(k == KC - 1)),
                             reads=[WB_, hbB], writes=[psB[bk]])
                    bk2 = pj.next()
                    for h in range(2):
                        P.op("pe", TR(ps[bk2][:, h * 128:(h + 1) * 128], kvf[ks][:, h * 128:(h + 1) * 128], ID),
                             reads=[kvfB[ks], Bc], writes=[psB[bk2]])
                    P.op("dve", TC(kT[:, :, j * 128:(j + 1) * 128], split(ps[bk2][:, 0:256], 2)), reads=[psB[bk2]], writes=[kTB[j]])
                    silu2_from_psum(fgs[s][:, tt, :], ps[bk][:, 0:256], et[ks][:], etB[ks], psB[bk], fgsB[s])
                    P.op("dve", TT(lfpre[s][:, tt, :], ps[bk][:, 256:258], bfb[:], ALU.add), reads=[psB[bk], WB_], writes=[lfpB[s]])
                lfp2 = flat(lfpre[s][:, :, :])
                P.op("act", ACT(lfp2, lfp2, AF.Exp, ZERO, -1.0), reads=[lfpB[s], Bc], writes=[lfpB[s]])
                P.op("act", ACT(lfp2, lfp2, AF.Ln, ONE, 1.0), reads=[lfpB[s], Bc], writes=[lfpB[s]])
                P.op("pool", TS(flat(LF[:, 4 * i:4 * i + 4, :]), lfp2, -1.0, None, ALU.mult), reads=[lfpB[s]], writes=[LFB])
                if i == 16:
                    sample_attention(s)
                    return
                cumsum_tiles(pj, LF[:, 4 * i:4 * i + 4, :], 4, carry, C[:, 4 * i:4 * i + 4, :], cs_tmp, csB, LFB, CB,
                             carry_out_ap=carry[:, :])
                P.op("dve", TC(cref[:, i, :], cs_tmp[1][:, 0:8].rearrange("p (a b) -> p a b", b=2)[:, 1, :]), reads=[csB], writes=[CB])
                nkt = 4 * i + 4
                for h in range(2):
                    P.op("dve", TS(btab[s][:, h, 0:nkt], C[:, 0:nkt, h], -1.0, cref[:, i, h:h + 1], ALU.mult, ALU.add),
                         reads=[CB], writes=[btB[s]])
                for h in range(2):
                    bA, bB = OB[h]

                    def score(kt):
                        jj = kt - 4 * i
                        c0 = 0 if jj < 0 else 128 * jj
                        sk = stb.next()
                        P.op("pe", MM(ps[sk][:, c0:512], kT[:, h, kt * 128:(kt + 1) * 128], qT[s][:, h, c0:512]),
                             reads=[kTB[kt], qTB[s]], writes=[psB[sk]])
                        return sk

                    skq = [score(0)]
                    if nkt > 1:
                        skq.append(score(1))
                    for kt in range(nkt):
                        if kt + 2 < nkt:
                            skq.append(score(kt + 2))
                        sk = skq.pop(0)
                        jj = kt - 4 * i
                        c0 = 0 if jj < 0 else 128 * jj
                        pi = ptrr.next()
                        P.op("act", ACT(PT[pi][:, c0:512], ps[sk][:, c0:512], AF.Exp, btab[s][:, h, kt:kt + 1], SCALE_F),
                             reads=[psB[sk], btB[s]], writes=[PTB[pi]])
                        if jj >= 0:
                            P.op("dve", TT(PT[pi][:, c0:c0 + 128], PT[pi][:, c0:c0 + 128], tri_b[:], ALU.mult),
                                 reads=[PTB[pi], Bc], writes=[PTB[pi]])
                        for sub in range(max(jj, 0), 4):
                            bo = bA if sub < 2 else bB
                            oc = (sub % 2) * 129
                            P.op("pe", MM(ps[bo][:, oc:oc + 129], PT[pi][:, sub * 128:(sub + 1) * 128], Vg[:, kt, h, :],
                                          start=(kt == 0 and sub % 2 == 0), stop=(kt == 4 * i + sub), skip=True),
                                 reads=[PTB[pi], VB[kt]], writes=[psB[bo]])
                    for sub in range(4):
                        j = 4 * i + sub
                        bo = bA if sub < 2 else bB
                        oc = (sub % 2) * 129
                        oi = j % 8
                        P.op("dve", RCP(rec[oi][:, h:h + 1], ps[bo][:, oc + 128:oc + 129]), reads=[psB[bo]], writes=[ostB[oi]])
                        P.op("dve", STT(ost[oi][:, h * 128:(h + 1) * 128], ps[bo][:, oc:oc + 128], rec[oi][:, h:h + 1],
                                        fgs[s][:, sub, h * 128:(h + 1) * 128], ALU.mult, ALU.mult),
                             reads=[psB[bo], fgsB[s], ostB[oi]], writes=[ostB[oi]])
                for sub in range(4):
                    j = 4 * i + sub
                    oi = j % 8
                    oa_st.append(P.op("sp", DMA(osrcA[j * 128:(j + 1) * 128, :], ost[oi][:]), reads=[ostB[oi]], dma=ostB[oi]))

            oa_st = []
            jobs = cast_jobs(l)
            load_hT_block(0, hTb[0], hTbB[0])
            for i in range(NBLK):
                block(i)
                emit_hw_casts((l, "G"), 2)
                emit_casts(l, jobs, 5)
            for q in range(5):
                ag_chunk(osrcA, oagA, oagAB, q, NTOK, 2048, oa_st[16 * q:16 * q + 16])
            emit_casts(l, jobs, len(jobs))
            P.op("sp", DMA(lfo_d[l], LF[:, :, :]), reads=[LFB], dma=LFB)
            P.barrier()

        def pass_g(l):
            A.reset(base_mark)
            WGfm = A.alloc("WGfm", [128, KC, 528], BF16)
            WGtm = A.alloc("WGtm", [128, KC, 512], BF16)
            WB_ = P.buf("WG")
            load_w_bf(WGfm, wgfm_b[l], WB_, (l, "G"))
            load_w_bf(WGtm, wgtm_b[l], WB_, (l, "G"))
            wa2 = A.alloc("wa2", [16, 256], BF16)
            wa2f = A.alloc("wa2f", [16, 256], F32)
            wa2B = P.buf("wa2")
            P.op("sp", DMA(wa2f[:], wa2_d[l]), writes=[wa2B], dma=wa2B)
            P.op("act", CP(wa2[:], wa2f[:]), reads=[wa2B], writes=[wa2B])
            nba = A.alloc("nba", [128, 2], F32)
            gain = A.alloc("gain", [128, 256], F32)
            P.op("sp", DMA(nba[:], ba_d[l]), writes=[WB_], dma=WB_)
            P.op("sp", DMA(gain[:], gain_d[l]), writes=[WB_], dma=WB_)
            P.op("dve", TS(nba[:], nba[:], -1.0, None, ALU.mult), reads=[WB_], writes=[WB_])
            hTb = [A.alloc("hTb", [128, KC, 512], BF16) for _ in range(2)]
            hTbB = P.bufs(2, "hTb")
            glrT = A.alloc("glrT", [16, 512], BF16)
            glB = P.buf("glrT")
            et2 = A.alloc("et2", [128, 512], F32)
            et2B = P.buf("et2")
            sp_ = A.alloc("sp", [128, 2, 512], F32)
            spB = P.buf("sp")
            bpos = A.alloc("bpos", [128, 2, 512], F32)
            bpB = P.buf("bpos")
            ebT = A.alloc("ebT", [128, 2, 512], BF16)
            enbT = A.alloc("enbT", [128, 2, 512], BF16)
            ebB = P.buf("eb")
            ebl = [A.alloc("ebl", [128, 2, 4], F32) for _ in range(2)]
            eblB = P.bufs(2, "ebl")
            qtl = [A.alloc("qtl", [128, 2, 512], BF16) for _ in range(2)]
            qtB = P.bufs(2, "qtl")
            ktf = [A.alloc("ktf", [128, 2, 512], F32) for _ in range(2)]
            ktfB = P.bufs(2, "ktf")
            ktl = [A.alloc("ktl", [128, 2, 512], BF16) for _ in range(2)]
            ktlB = P.bufs(2, "ktl")
            gvb = [A.alloc("gvb", [128, 256], BF16) for _ in range(8)]
            gvB = P.bufs(8, "gvb")
            ggs = [A.alloc("ggs", [128, 256], BF16) for _ in range(8)]
            ggB = P.bufs(8, "ggs")
            etg = [A.alloc("etg", [128, 256], F32) for _ in range(2)]
            etgB = P.bufs(2, "etg")
            kttm = [A.alloc("kttm", [128, 256], BF16) for _ in range(2)]
            kttmB = P.bufs(2, "kttm")
            AT = [A.alloc("AT", [128, 128], BF16) for _ in range(2)]
            ATB = P.bufs(2, "AT")
            osb = [A.alloc("osb", [128, 256], F32) for _ in range(2)]
            osbB = P.bufs(2, "osb")
            junk = A.alloc("junkg", [128, 256], BF16)
            ssq = [A.alloc("ssqg", [128, 4], F32) for _ in range(2)]
            t1 = [A.alloc("t1g", [128, 256], F32) for _ in range(2)]
            obst = [A.alloc("obst", [128, 256], BF16) for _ in range(4)]
            obB = P.bufs(4, "obst")
            S = A.alloc("S", [128, 2, 256], F32)
            SB_ = P.buf("S")
            Sb = A.alloc("Sb", [128, 2, 256], BF16)
            SbB = P.buf("Sb")
            tmpS = A.alloc("tmpS", [128, 2, 256], F32)
            tSB = P.buf("tmpS")
            P.op("dve", MS(flat(S[:, :, :]), 0.0), writes=[SB_])
            P.op("dve", MS(flat(Sb[:, :, :]), 0.0), writes=[SbB])
            ostores = []

            def proj(i):
                s = i % 2
                hb, hbB = hTb[s], hTbB[s]
                bk = psrr.next()
                for k in range(KC):
                    P.op("pe", MM(ps[bk][0:16, :], WGfm[:, k, 512:528], hb[:, k, :], start=(k == 0), stop=(k == KC - 1)),
                         reads=[WB_, hbB], writes=[psB[bk]])
                P.op("act", CP(glrT[:, :], ps[bk][0:16, :]), reads=[psB[bk]], writes=[glB])
                for c in range(2):
                    bk = psrr.next()
                    P.op("pe", MM(ps[bk][:, :], wa2[:, c * 128:(c + 1) * 128], glrT[:, :]), reads=[wa2B, glB], writes=[psB[bk]])
                    P.op("act", ACT(et2[:], ps[bk][:, :], AF.Exp, nba[:, c:c + 1], -1.0), reads=[psB[bk], WB_], writes=[et2B])
                    P.op("act", ACT(sp_[:, c, :], et2[:], AF.Ln, ONE, 1.0), reads=[et2B, Bc], writes=[spB])
                    P.op("dve", SCAN(bpos[:, c, :], resetm[:], sp_[:, c, :], 0.0), reads=[spB, Bc], writes=[bpB])
                bp2 = flat(bpos[:, :, :])
                P.op("act", ACT(flat(ebT[:, :, :]), bp2, AF.Exp, NL16, -1.0 / 16.0), reads=[bpB, Bc], writes=[ebB])
                P.op("act", ACT(flat(enbT[:, :, :]), bp2, AF.Exp, ZERO, 1.0 / 16.0), reads=[bpB, Bc], writes=[ebB])
                lastc = 31 if i == 16 else 127
                for c in range(2):
                    P.op("act", ACT(ebl[s][:, c, :], split(bpos[:, c, :], 4)[:, :, lastc], AF.Exp, ZERO, -1.0 / 16.0),
                         reads=[bpB, Bc], writes=[eblB[s]])
                for c in range(2):
                    bk = psrr.next()
                    for k in range(KC):
                        P.op("pe", MM(ps[bk][:, :], WGfm[:, k, c * 128:(c + 1) * 128], hb[:, k, :], start=(k == 0), stop=(k == KC - 1)),
                             reads=[WB_, hbB], writes=[psB[bk]])
                    P.op("dve", TT(qtl[s][:, c, :], ps[bk][:, :], ebT[:, c, :], ALU.mult), reads=[psB[bk], ebB], writes=[qtB[s]])
                for c in range(2):
                    bk = psrr.next()
                    for k in range(KC):
                        P.op("pe", MM(ps[bk][:, :], WGfm[:, k, 256 + c * 128:256 + (c + 1) * 128], hb[:, k, :],
                                      start=(k == 0), stop=(k == KC - 1)), reads=[WB_, hbB], writes=[psB[bk]])
                    P.op("dve", TT(ktf[s][:, c, :], ps[bk][:, :], enbT[:, c, :], ALU.mult), reads=[psB[bk], ebB], writes=[ktfB[s]])
                P.op("act", CP(flat(ktl[s][:, :, :]), flat(ktf[s][:, :, :])), reads=[ktfB[s]], writes=[ktlB[s]])
                for tt in range(4):
                    j = 4 * i + tt
                    tsl = slice(tt * 128, (tt + 1) * 128)
                    g8, g2 = j % 8, j % 2
                    bk = psrr.next()
                    for k in range(KC):
                        P.op("pe", MM(ps[bk][:, :], hb[:, k, tsl], WGtm[:, k, :], start=(k == 0), stop=(k == KC - 1)),
                             reads=[WB_, hbB], writes=[psB[bk]])
                    P.op("act", CP(gvb[g8][:], ps[bk][:, 0:256]), reads=[psB[bk]], writes=[gvB[g8]])
                    silu2_from_psum(ggs[g8][:], ps[bk][:, 256:512], etg[g2][:], etgB[g2], psB[bk], ggB[g8])

            def recur(i):
                s = i % 2
                for tt in range(4):
                    j = 4 * i + tt
                    tsl = slice(tt * 128, (tt + 1) * 128)
                    g8, g4, g2 = j % 8, j % 4, j % 2
                    if i == 16:
                        P.op("sp", DMA(S[:, :, :], sg_d[l, tt]), writes=[SB_], dma=SB_)
                        P.op("act", CP(flat(Sb[:, :, :]), flat(S[:, :, :])), reads=[SB_], writes=[SbB])
                    bk = psrr.next()
                    for c in range(2):
                        P.op("pe", TR(ps[bk][:, c * 128:(c + 1) * 128], ktf[s][:, c, tsl], ID), reads=[ktfB[s], Bc], writes=[psB[bk]])
                    P.op("act", CP(kttm[g2][:], ps[bk][:, 0:256]), reads=[psB[bk]], writes=[kttmB[g2]])
                    bk = psrr.next()
                    for c in range(2):
                        P.op("pe", MM(ps[bk][:, 0:128], ktl[s][:, c, tsl], qtl[s][:, c, tsl], start=(c == 0), stop=(c == 1)),
                             reads=[ktlB[s], qtB[s]], writes=[psB[bk]])
                    P.op("dve", TT(AT[g2][:], ps[bk][:, 0:128], tri_f[:], ALU.mult), reads=[psB[bk], Bc], writes=[ATB[g2]])
                    bk = psrr.next()
                    P.op("pe", MM(ps[bk][:, 0:256], AT[g2][:], gvb[g8][:], start=True, stop=False),
                         reads=[ATB[g2], gvB[g8]], writes=[psB[bk]])
                    for c in range(2):
                        P.op("pe", MM(ps[bk][:, 0:256], qtl[s][:, c, tsl], Sb[:, c, :], start=False, stop=(c == 1)),
                             reads=[qtB[s], SbB], writes=[psB[bk]])
                    P.op("act", CP(osb[g2][:], ps[bk][:, 0:256]), reads=[psB[bk]], writes=[osbB[g2]])
                    bk = psrr.next()
                    for c in range(2):
                        P.op("pe", MM(ps[bk][:, c * 256:(c + 1) * 256], kttm[g2][:, c * 128:(c + 1) * 128], gvb[g8][:],
                                      start=(c == 0), stop=True, skip=True), reads=[kttmB[g2], gvB[g8]], writes=[psB[bk]])
                    P.op("dve", TT(flat(tmpS[:, :, :]), ps[bk][:, :], flat(S[:, :, :]), ALU.add), reads=[psB[bk], SB_], writes=[tSB])
                    for c in range(2):
                        P.op("dve", TS(S[:, c, :], tmpS[:, c, :], ebl[s][:, c, tt:tt + 1], None, ALU.mult), reads=[tSB, eblB[s]], writes=[SB_])
                    P.op("act", CP(flat(Sb[:, :, :]), flat(S[:, :, :])), reads=[SB_], writes=[SbB])
                    if i == 16:
                        P.op("sp", DMA(gs_d[l, tt], S[:, :, :]), reads=[SB_], dma=SB_)
                    elif j == 63:
                        P.op("sp", DMA(gp_d[l], S[:, :, :]), reads=[SB_], dma=SB_)
                    P.op("dve", STT(junk[:], osb[g2][:], 1.0, osb[g2][:], ALU.mult, ALU.mult, accum=ssq[g2][:, 0:1]),
                         reads=[osbB[g2]], writes=[osbB[g2]])
                    rstd_of(ssq[g2], 256, osbB[g2], lnbias=LNH)
                    P.op("dve", STT(t1[g2][:], osb[g2][:], ssq[g2][:, 2:3], gain[:], ALU.mult, ALU.mult),
                         reads=[osbB[g2], WB_], writes=[osbB[g2]])
                    P.op("pool", TT(obst[g4][:], t1[g2][:], ggs[g8][:], ALU.mult), reads=[osbB[g2], ggB[g8]], writes=[obB[g4]])
                    ostores.append(P.op("sp", DMA(osrcB[j * 128:(j + 1) * 128, :], obst[g4][:]), reads=[obB[g4]], dma=obB[g4]))

            load_hT_block(0, hTb[0], hTbB[0])
            load_hT_block(1, hTb[1], hTbB[1])
            proj(0)
            for i in range(NBLK):
                if i + 1 < NBLK:
                    proj(i + 1)
                    if i + 2 < NBLK:
                        load_hT_block(i + 2, hTb[i % 2], hTbB[i % 2])
                recur(i)
                if l + 1 < DEPTH:
                    emit_hw_casts((l + 1, "F"), 1)
                    emit_hw_casts((l + 1, "G"), 1)
                if i % 4 == 3:
                    q = i // 4
                    ag_chunk(osrcB, oagB_d, oagBB, q, NTOK, 2048, ostores[0:16])
                    del ostores[0:16]
            ag_chunk(osrcB, oagB_d, oagBB, 4, NTOK, 2048, list(ostores))
            P.barrier()

        def token_phase(l):
            last = (l == DEPTH - 1)
            A.reset(base_mark)
            hTg = A.alloc("hTg", [128, KC, 512], BF16)
            hTgB = P.buf("hTg")
            oT = A.alloc("oT", [128, KC, 512], BF16)
            oTB = P.buf("oT")
            mT = A.alloc("mT", [128, KC, 512], BF16)
            mTB = P.buf("mT")
            z = A.alloc("z", [128, 4, D], F32)
            zB = P.bufs(4, "z")
            ogf = A.alloc("ogf", [128, D], F32)
            ogB = P.buf("ogf")
            wr = [dict(ma=A.alloc("wma", [128, KC, 128], BF16), mb=A.alloc("wmb", [128, KC, 128], BF16),
                       oa=A.alloc("woa", [128, 8, 128], BF16), ob=A.alloc("wob", [128, 8, 128], BF16)) for _ in range(2)]
            wrB = P.bufs(2, "wr")
            wo = [A.alloc("wo", [128, KC, 256], BF16) for _ in range(2)]
            woB = P.bufs(2, "wo")
            xt = A.alloc("xt", [128, D], F32)
            xB = P.buf("xt")
            yt = A.alloc("yt", [128, D], F32)
            yB = P.buf("yt")
            t1 = A.alloc("t1", [128, D], F32)
            tB1 = P.buf("t1")
            ssqz = A.alloc("ssqz", [128, 4], F32)
            post_t = A.alloc("post_t", [128, D], F32)
            pre_t = A.alloc("pre_t", [128, D], F32)
            ppB = P.buf("pp")
            P.op("sp", DMA(post_t[:], post_d[l]), writes=[ppB], dma=ppB)
            if not last:
                P.op("sp", DMA(pre_t[:], pre_d[l + 1]), writes=[ppB], dma=ppB)
            htmp = (A.alloc("junk", [128, D], BF16), A.alloc("ssq", [128, 4], F32), A.alloc("hf", [128, D], F32),
                    A.alloc("hTst", [128, KC, 128], BF16), P.buf("hTst"), P.buf("htmp"))
            ge = [A.alloc("ge", [128, 512], F32) for _ in range(4)]
            geB = P.bufs(4, "ge")
            gm = [A.alloc("gm", [128, 512], F32) for _ in range(4)]
            gmB = P.bufs(4, "gm")
            src_x = x_d if l == 0 else yscr

            wdeps = wcast_ops[l]

            def load_wr(c, sl):
                w, B = wr[sl], wrB[sl]
                P.op("sp", DMA(flat(w["ma"][:, :, :]), wm_b[l, c]), writes=[B], dma=B, extra_deps=wdeps)
                P.op("sp", DMA(flat(w["mb"][:, :, :]), wm_b[l, 16 + c]), writes=[B], dma=B, extra_deps=wdeps)
                P.op("sp", DMA(flat(w["oa"][:, :, :]), woa_b[l, c]), writes=[B], dma=B, extra_deps=wdeps)
                P.op("sp", DMA(flat(w["ob"][:, :, :]), wob_b[l, c]), writes=[B], dma=B, extra_deps=wdeps)

            def load_wo(blk, sl):
                P.op("sp", DMA(flat(wo[sl][:, :, :]), wout_b[l, blk]), writes=[woB[sl]], dma=woB[sl], extra_deps=wdeps)

            def stage_a_tile(grp, gi):
                t = grp[gi]
                for r in range(4):
                    for (src_, B_, c0_) in ((oagA, oagAB[t // 4], 0), (oagB_d, oagBB[t // 4], 256)):
                        P.op("pool", (lambda o_, i_, s_: (lambda e: e.indirect_dma_start(
                            out=o_, out_offset=None, in_=s_[:, :],
                            in_offset=bass.IndirectOffsetOnAxis(ap=i_, axis=0))))(
                                ogf[:, r * 512 + c0_:r * 512 + c0_ + 256], idx_t[:, t * 4 + r:t * 4 + r + 1], src_),
                            reads=[B_, Bc], writes=[ogB], dma=ogB)
                for q in range(4):
                    bk = psrr.next()
                    for jx in range(4):
                        c = q * 4 + jx
                        if c < 8:
                            col = (c // 2) * 512 + (c % 2) * 128
                        else:
                            cc = c - 8
                            col = (cc // 2) * 512 + 256 + (cc % 2) * 128
                        P.op("pe", TR(ps[bk][:, jx * 128:(jx + 1) * 128], ogf[:, col:col + 128], ID),
                             reads=[ogB, Bc], writes=[psB[bk]])
                    P.op("act", CP(oT[:, q * 4:(q + 1) * 4, gi * 128:(gi + 1) * 128], split(ps[bk][:, :], 4)),
                         reads=[psB[bk]], writes=[oTB])
                P.op("sp", DMA(hTg[:, :, gi * 128:(gi + 1) * 128], split(hsrc[t * 128:(t + 1) * 128, :], KC)),
                     reads=[hsrcB[t]], writes=[hTgB], dma=hTgB)

            def stage_b(grp, inter):
                ntok = len(grp) * 128
                load_wr(0, 0)
                for c in range(16):
                    sl = c % 2
                    if c + 1 < 16:
                        load_wr(c + 1, 1 - sl)
                    w, wB = wr[sl], wrB[sl]
                    bks = [psrr.next() for _ in range(4)]
                    for k in range(KC):
                        P.op("pe", MM(ps[bks[0]][:, 0:ntok], w["ma"][:, k, :], hTg[:, k, 0:ntok], start=(k == 0), stop=(k == KC - 1)),
                             reads=[wB, hTgB], writes=[psB[bks[0]]])
                    for k in range(KC):
                        P.op("pe", MM(ps[bks[1]][:, 0:ntok], w["mb"][:, k, :], hTg[:, k, 0:ntok], start=(k == 0), stop=(k == KC - 1)),
                             reads=[wB, hTgB], writes=[psB[bks[1]]])
                    for k in range(8):
                        P.op("pe", MM(ps[bks[2]][:, 0:ntok], w["oa"][:, k, :], oT[:, k, 0:ntok], start=(k == 0), stop=(k == 7)),
                             reads=[wB, oTB], writes=[psB[bks[2]]])
                    for k in range(8):
                        P.op("pe", MM(ps[bks[3]][:, 0:ntok], w["ob"][:, k, :], oT[:, 8 + k, 0:ntok], start=(k == 0), stop=(k == 7)),
                             reads=[wB, oTB], writes=[psB[bks[3]]])
                    for u in range(2):
                        gi_ = u + 2 * (c % 2)
                        gu = ge[gi_][:, 0:ntok]
                        P.op("act", ACT(gu, ps[bks[u]][:, 0:ntok], AF.Tanh, ZERO, 0.5), reads=[psB[bks[u]], Bc], writes=[geB[gi_]])
                        P.op("dve", STT(gm[gi_][:, 0:ntok], gu, 1.0, ps[bks[2 + u]][:, 0:ntok], ALU.add, ALU.mult),
                             reads=[psB[bks[2 + u]], geB[gi_]], writes=[gmB[gi_]])
                    g0, g1 = 2 * (c % 2), 1 + 2 * (c % 2)
                    P.op("dve", TT(mT[:, c, 0:ntok], gm[g0][:, 0:ntok], gm[g1][:, 0:ntok], ALU.add), reads=[gmB[g0], gmB[g1]], writes=[mTB])
                    if c in inter:
                        inter[c]()

            def stage_c(grp, inter):
                ng = len(grp)
                load_wo(0, 0)
                for blk in range(8):
                    sl = blk % 2
                    if blk + 1 < 8:
                        load_wo(blk + 1, 1 - sl)
                    for gi in range(ng):
                        bk = psrr.next()
                        for k in range(KC):
                            P.op("pe", MM(ps[bk][:, 0:256], mT[:, k, gi * 128:(gi + 1) * 128], wo[sl][:, k, :],
                                          start=(k == 0), stop=(k == KC - 1)), reads=[mTB, woB[sl]], writes=[psB[bk]])
                        P.op("act", MUL(z[:, gi, blk * 256:(blk + 1) * 256], ps[bk][:, 0:256], 0.5), reads=[psB[bk]], writes=[zB[gi]])
                    if blk in inter:
                        inter[blk]()

            def stage_d_tile(grp, gi):
                t = grp[gi]
                P.op("pool", DMA(xt[:], src_x[t * 128:(t + 1) * 128, :]), writes=[xB], dma=xB)
                P.op("dve", STT(htmp[0][:], z[:, gi, :], 1.0, z[:, gi, :], ALU.mult, ALU.mult, accum=ssqz[:, 0:1]),
                     reads=[zB[gi]], writes=[tB1, htmp[5]])
                rstd_of(ssqz, D, tB1)
                P.op("dve", STT(t1[:], z[:, gi, :], ssqz[:, 2:3], post_t[:], ALU.mult, ALU.mult), reads=[zB[gi], tB1, ppB], writes=[tB1])
                P.op("dve", TT(yt[:], t1[:], xt[:], ALU.add), reads=[tB1, xB], writes=[yB])
                if last:
                    P.op("pool", DMA(y_d[t * 128:(t + 1) * 128, :], yt[:]), reads=[yB], dma=yB)
                else:
                    P.op("pool", DMA(yscr[t * 128:(t + 1) * 128, :], yt[:]), reads=[yB], dma=yB)
                    hst[t] = emit_h(yt, yB, pre_t, ppB, t, htmp)
                    if t % 2 == 1 or t == TL - 1:
                        q = t // 2
                        ag_chunk(hsrc, hag, hagB, q, TL * 128, 256, [hst[t_] for t_ in range(2 * q, min(2 * q + 2, TL))])

            def spread(fns, slots):
                m = {}
                for n_, f in enumerate(fns):
                    sl_ = slots[min(n_, len(slots) - 1)]
                    m.setdefault(sl_, []).append(f)
                return {k_: (lambda fs=v_: [f() for f in fs]) for k_, v_ in m.items()}

            hst = {}
            G = len(TGROUPS)
            for gi in range(len(TGROUPS[0])):
                stage_a_tile(TGROUPS[0], gi)
            for g in range(G):
                grp = TGROUPS[g]
                d_fns = [] if g == 0 else [(lambda pg=TGROUPS[g - 1], x_=x: stage_d_tile(pg, x_)) for x in range(len(TGROUPS[g - 1]))]
                stage_b(grp, spread(d_fns, [1, 3, 5, 7]))
                a_fns = [] if g + 1 == G else [(lambda ng_=TGROUPS[g + 1], x_=x: stage_a_tile(ng_, x_)) for x in range(len(TGROUPS[g + 1]))]
                stage_c(grp, spread(a_fns, [1, 3, 5, 7]))
            for x in range(len(TGROUPS[G - 1])):
                stage_d_tile(TGROUPS[G - 1], x)
            P.barrier()

        def run_all():
            prologue()
            if STOP_AFTER == "pro":
                return
            for l in range(DEPTH):
                pass_f(l)
                if STOP_AFTER == "F%d" % l:
                    return
                pass_g(l)
                if STOP_AFTER == "G%d" % l:
                    return
                if STOP_AFTER == "AG%d" % l:
                    return
                token_phase(l)
                if STOP_AFTER == "T%d" % l:
                    return

        run_all()
        P.barrier()
        stats = P.emit()
        stats["sbuf_peak"] = A.peak
    return nc, stats


_OFF = dict(fq=0, fk=1024, fv=2048, ff=3072, fg=3080, gq=4104, gk=5128, gv=6152, glr=7176, gg=7192, ma=8216, mb=10264)


def _pkc(w):
    K = w.shape[0] // 128
    return np.ascontiguousarray(w.reshape(K, 128, w.shape[1]).transpose(1, 0, 2))


def _chunks(w, cw):
    K = w.shape[0] // 128
    NC = w.shape[1] // cw
    return np.ascontiguousarray(w.reshape(K, 128, NC, cw).transpose(2, 1, 0, 3))


def _prep(inp):
    f = lambda a: np.asarray(a, dtype=np.float32)
    x_prompt, x_sample = f(inp["x_prompt"]), f(inp["x_sample"])
    cache_k, cache_v, cache_logf, state_gla = f(inp["cache_k"]), f(inp["cache_v"]), f(inp["cache_logf"]), f(inp["state_gla"])
    w_in, w_a2, b_a, b_f = f(inp["w_in"]), f(inp["w_a2"]), f(inp["b_a"]), f(inp["b_f"])
    gla_gain, w_oa, w_ob, w_out = f(inp["gla_gain"]), f(inp["w_oa"]), f(inp["w_ob"]), f(inp["w_out"])
    pre_norm, post_norm = f(inp["pre_norm"]), f(inp["post_norm"])

    shared = {}
    shared["wm"] = np.stack([_chunks(w_in[l][:, _OFF["ma"]:_OFF["ma"] + 4096], 128) for l in range(DEPTH)])
    shared["woa"] = np.stack([_chunks(w_oa[l], 128) for l in range(DEPTH)])
    shared["wob"] = np.stack([_chunks(w_ob[l], 128) for l in range(DEPTH)])
    shared["wout"] = np.stack([_chunks(w_out[l], 256) for l in range(DEPTH)])
    shared["gain"] = np.ascontiguousarray(np.broadcast_to(gla_gain[:, None, :], (DEPTH, 128, 256)))
    shared["pre"] = np.ascontiguousarray(np.broadcast_to(pre_norm[:, None, :], (DEPTH, 128, D)))
    shared["post"] = np.ascontiguousarray(np.broadcast_to(post_norm[:, None, :], (DEPTH, 128, D)))

    maps = []
    for c in range(8):
        b, g = c // 4, c % 4
        m = dict(shared)
        xl = np.zeros((TL * 128, D), np.float32)
        for k in range(16):
            xl[k * 128:(k + 1) * 128] = x_prompt[b, (4 * k + g) * 128:(4 * k + g + 1) * 128]
        xl[2048:2080] = x_sample[4 * b + g]
        m["x_loc"] = xl
        fs = slice(256 * g, 256 * g + 256)

        def cols(l, name, sl=fs):
            o = _OFF[name]
            return w_in[l][:, o + sl.start:o + sl.stop]

        m["wffm"] = np.stack([_pkc(cols(l, "fq")) for l in range(DEPTH)])
        m["wftm"] = np.stack([_pkc(np.concatenate([cols(l, "fk"), cols(l, "fv"), cols(l, "fg"),
                                                   cols(l, "ff", slice(2 * g, 2 * g + 2))], 1)) for l in range(DEPTH)])
        m["wgfm"] = np.stack([_pkc(np.concatenate([cols(l, "gq"), cols(l, "gk"), cols(l, "glr", slice(0, 16))], 1))
                              for l in range(DEPTH)])
        m["wgtm"] = np.stack([_pkc(np.concatenate([cols(l, "gv"), cols(l, "gg")], 1)) for l in range(DEPTH)])
        m["wa2"] = np.ascontiguousarray(w_a2[:, :, fs])
        m["ba"] = np.ascontiguousarray(b_a[:, fs].reshape(DEPTH, 2, 128).transpose(0, 2, 1))
        m["bfb"] = np.ascontiguousarray(np.broadcast_to(b_f[:, None, 2 * g:2 * g + 2], (DEPTH, 128, 2)))
        sbs = slice(4 * b, 4 * b + 4)
        m["ck"] = np.ascontiguousarray(cache_k[:, sbs, :, 2 * g:2 * g + 2, :].reshape(DEPTH, 4, 4096, 256))
        m["cv"] = np.ascontiguousarray(cache_v[:, sbs, :, 2 * g:2 * g + 2, :].reshape(DEPTH, 4, 4096, 256))
        m["clf"] = np.ascontiguousarray(cache_logf[:, sbs, :, 2 * g:2 * g + 2].reshape(DEPTH, 4, 32, 128, 2).transpose(0, 1, 3, 2, 4))
        m["sg"] = np.ascontiguousarray(state_gla[:, sbs, g].reshape(DEPTH, 4, 2, 128, 256).transpose(0, 1, 3, 2, 4))
        idx = np.zeros((128, TL * 4), np.int32)
        p = np.arange(128)
        for t in range(TL):
            tok = ((4 * t + g) * 128 + p) if t < 16 else (8192 + 128 * g + p)
            q, within = tok // 2048, tok % 2048
            nq = np.where(q < 4, 2048, 512)
            for r in range(4):
                idx[:, t * 4 + r] = q * 8192 + r * nq + within
        m["idx"] = idx
        maps.append(m)
    return maps


_CACHE = {}


def kernel(**inputs):
    if "nc" not in _CACHE:
        _CACHE["nc"], _CACHE["stats"] = build_program()
    nc = _CACHE["nc"]
    maps = _prep(inputs)
    res = run_bass_kernel_spmd(nc, maps, core_ids=list(range(8)))
    R = res.results
    B, SEQ, DB, DS = 2, 8192, 8, 32
    y_prompt = np.zeros((B, SEQ, D), np.float32)
    y_sample = np.zeros((DB, DS, D), np.float32)
    k_prompt = np.zeros((DEPTH, B, SEQ, 8, 128), np.float32)
    v_prompt = np.zeros((DEPTH, B, SEQ, 8, 128), np.float32)
    logf_prompt = np.zeros((DEPTH, B, SEQ, 8), np.float32)
    gla_prompt = np.zeros((DEPTH, B, 4, 256, 256), np.float32)
    k_sample = np.zeros((DEPTH, DB, DS, 8, 128), np.float32)
    v_sample = np.zeros((DEPTH, DB, DS, 8, 128), np.float32)
    logf_sample = np.zeros((DEPTH, DB, DS, 8), np.float32)
    gla_sample = np.zeros((DEPTH, DB, 4, 256, 256), np.float32)
    for c in range(8):
        b, g = c // 4, c % 4
        r = R[c]
        y = np.asarray(r["y_loc"])
        for k in range(16):
            y_prompt[b, (4 * k + g) * 128:(4 * k + g + 1) * 128] = y[k * 128:(k + 1) * 128]
        y_sample[4 * b + g] = y[2048:2080]
        ko, vo = np.asarray(r["k_out"]), np.asarray(r["v_out"])
        lf = np.asarray(r["lf_out"]).transpose(0, 2, 1, 3).reshape(DEPTH, NTOK, 2)
        gp, gs = np.asarray(r["gla_p"]), np.asarray(r["gla_s"])
        for l in range(DEPTH):
            k_prompt[l, b, :, 2 * g:2 * g + 2, :] = ko[l, :SEQ].reshape(SEQ, 2, 128)
            v_prompt[l, b, :, 2 * g:2 * g + 2, :] = vo[l, :SEQ].reshape(SEQ, 2, 128)
            logf_prompt[l, b, :, 2 * g:2 * g + 2] = lf[l, :SEQ]
            gla_prompt[l, b, g] = gp[l].transpose(1, 0, 2).reshape(256, 256)
            for k4 in range(4):
                o = SEQ + 128 * k4
                k_sample[l, 4 * b + k4, :, 2 * g:2 * g + 2, :] = ko[l, o:o + DS].reshape(DS, 2, 128)
                v_sample[l, 4 * b + k4, :, 2 * g:2 * g + 2, :] = vo[l, o:o + DS].reshape(DS, 2, 128)
                logf_sample[l, 4 * b + k4, :, 2 * g:2 * g + 2] = lf[l, o:o + DS]
                gla_sample[l, 4 * b + k4, g] = gs[l, k4].transpose(1, 0, 2).reshape(256, 256)
    return (y_prompt, y_sample, k_prompt, v_prompt, logf_prompt, gla_prompt,
            k_sample, v_sample, logf_sample, gla_sample)
```
